# Optimizing a Trainium2 kernel written in Bass

```python
import jax
import jax.numpy as jnp
from jax import lax
import numpy as np

D_MODEL = 1024
BATCH = 16
SEQ = 4096
DEPTH = 4

GRID_W = 64
CTX_LEN = 256
RMS_EPS = 1e-6
N_MOD = 6

RWKV_HEADS = 8
RWKV_HEAD_DIM = 64
RWKV_DIM = RWKV_HEADS * RWKV_HEAD_DIM
RWKV_DECAY_LORA = 64
RWKV_ICLR_LORA = 64
RWKV_GATE_LORA = 160
RWKV_LNX_EPS = 64e-5

GDN_HEADS = 4
GDN_HEAD_DIM = 128
GDN_DIM = GDN_HEADS * GDN_HEAD_DIM
GDN_CONV_W = 5
GDN_CHUNK = 64

ATTN_Q_HEADS = 8
ATTN_KV_HEADS = 2
ATTN_GROUP = ATTN_Q_HEADS // ATTN_KV_HEADS
ATTN_HEAD_DIM = 64
ATTN_Q_DIM = ATTN_Q_HEADS * ATTN_HEAD_DIM
ATTN_KV_DIM = ATTN_KV_HEADS * ATTN_HEAD_DIM
ATTN_BLOCK = 128
ROPE_THETA = 10000.0

N_BRANCH = 3
FFN_HIDDEN = -(-8 * D_MODEL // (3 * 256)) * 256

IN_SIZES = (3 * RWKV_DIM, 3 * GDN_DIM, GDN_DIM, ATTN_Q_DIM, 2 * ATTN_KV_DIM)
IN_DIM = 3 * RWKV_DIM + 4 * GDN_DIM + ATTN_Q_DIM + 2 * ATTN_KV_DIM
IN_SPLIT_POINTS = (3 * RWKV_DIM, 3 * RWKV_DIM + 3 * GDN_DIM, 3 * RWKV_DIM + 4 * GDN_DIM, 3 * RWKV_DIM + 4 * GDN_DIM + ATTN_Q_DIM)

kernel_name = 'hybrid_rwkv7_gdn_gqa_prefix_dit'


def rmsnorm(x, g):
    xf = x.astype(jnp.float32)
    y = xf * lax.rsqrt(jnp.mean(xf * xf, axis=-1, keepdims=True) + RMS_EPS)
    return (y * g.astype(jnp.float32)).astype(x.dtype)


def l2normalize(t):
    t = t.astype(jnp.float32)
    return t * lax.rsqrt(jnp.sum(t * t, axis=-1, keepdims=True) + RMS_EPS)


def modulate(h, shift, scale):
    return h * (1.0 + scale) + shift


def swiglu(h, w1, w3, w2):
    return (jax.nn.silu(h @ w1) * (h @ w3)) @ w2


def token_shift_mix(u, mu):
    p = jnp.pad(u, ((0, 0), (1, 1), (0, 0)))
    return u + (0.5 * (p[:, :-2] + p[:, 2:]) - u) * mu


def short_conv(u, w):
    n_ch, width = w.shape
    out = lax.conv_general_dilated(
        u, jnp.transpose(w)[:, None, :].astype(u.dtype), window_strides=(1,),
        padding=[(width // 2, width // 2)], dimension_numbers=('NWC', 'WIO', 'NWC'),
        feature_group_count=n_ch)
    return jax.nn.silu(out)


def rwkv7_scan(r, w, k, v, a, b, s0, reverse):
    def step(s, inp):
        r_t, w_t, k_t, v_t, a_t, b_t = inp
        sa = jnp.einsum('bhvk,bhk->bhv', s, a_t)
        s = s * w_t[:, :, None, :] + sa[..., None] * b_t[:, :, None, :] + v_t[..., None] * k_t[:, :, None, :]
        return s, jnp.einsum('bhvk,bhk->bhv', s, r_t)
    xs = tuple(jnp.swapaxes(t, 0, 1) for t in (r, w, k, v, a, b))
    s_final, ys = lax.scan(step, s0, xs, reverse=reverse)
    return jnp.swapaxes(ys, 0, 1), s_final


def rwkv_branch(h, rkv, lp, init):
    B, T, _ = h.shape
    H, N = RWKV_HEADS, RWKV_HEAD_DIM
    heads = lambda t: t.reshape(B, T, H, N).astype(jnp.float32)
    mu_x, mu_rkv = lp['rwkv_mu_x'], lp['rwkv_mu_rkv']
    x_w = token_shift_mix(h, mu_x[0])
    x_a = token_shift_mix(h, mu_x[1])
    x_g = token_shift_mix(h, mu_x[2])
    r0, k0, v0 = jnp.split(rkv, 3, axis=-1)
    r = heads(token_shift_mix(r0, mu_rkv[0]))
    k = heads(token_shift_mix(k0, mu_rkv[1]))
    v = heads(token_shift_mix(v0, mu_rkv[2]))
    g = jax.nn.sigmoid(x_g @ lp['rwkv_g1']) @ lp['rwkv_g2']
    kk = l2normalize(k * lp['rwkv_k_k'].reshape(H, N))
    k_a = lp['rwkv_k_a'].reshape(H, N)
    outs, finals, bonuses = [], [], []
    for d in range(2):
        w_log = -jax.nn.softplus(-(lp['rwkv_w0'][d] + jnp.tanh(x_w @ lp['rwkv_w1'][d]) @ lp['rwkv_w2'][d])) - 0.5
        decay = jnp.exp(-jnp.exp(heads(w_log)))
        a = heads(jax.nn.sigmoid(lp['rwkv_a0'][d] + (x_a @ lp['rwkv_a1'][d]) @ lp['rwkv_a2'][d]))
        k_d = k * (1.0 + (a - 1.0) * k_a)
        o, s = rwkv7_scan(r, decay, k_d, v, -kk, kk * a, init[d], reverse=(d == 1))
        outs.append(o)
        finals.append(s)
        bonuses.append(jnp.sum(r * k_d * lp['rwkv_r_k'], axis=-1, keepdims=True) * v)
    o = outs[0] + outs[1]
    mean = jnp.mean(o, axis=-1, keepdims=True)
    var = jnp.mean(jnp.square(o - mean), axis=-1, keepdims=True)
    o = (o - mean) * lax.rsqrt(var + RWKV_LNX_EPS) * lp['rwkv_lnx_w'].reshape(H, N) + lp['rwkv_lnx_b'].reshape(H, N)
    y = (o + bonuses[0] + bonuses[1]).reshape(B, T, RWKV_DIM).astype(h.dtype) * g
    return y, (finals[0], finals[1])


def chunk_gated_delta(q, k, v, g, beta, s0):
    B, T, H, K = q.shape
    V = v.shape[-1]
    C = GDN_CHUNK
    n = T // C
    q = q * (K ** -0.5)
    chunks = lambda t: t.reshape(B, n, C, H, t.shape[-1]).transpose(1, 0, 3, 2, 4)
    qc, kc, vc = chunks(q), chunks(k), chunks(v)
    gc = jnp.cumsum(g.reshape(B, n, C, H).transpose(1, 0, 3, 2), axis=-1)
    bc = beta.reshape(B, n, C, H).transpose(1, 0, 3, 2)
    tril = jnp.tril(jnp.ones((C, C), dtype=bool))
    strict = jnp.tril(jnp.ones((C, C), dtype=bool), -1)
    diff = gc[..., :, None] - gc[..., None, :]
    decay = jnp.where(tril, jnp.exp(jnp.where(tril, diff, 0.0)), 0.0)
    kb = kc * bc[..., None]
    lower = jnp.where(strict, jnp.einsum('nbhik,nbhjk->nbhij', kb, kc) * decay, 0.0)
    a_mat = lower + jnp.eye(C, dtype=jnp.float32)
    rhs = jnp.concatenate([vc * bc[..., None], kb * jnp.exp(gc)[..., None]], axis=-1)
    sol = lax.linalg.triangular_solve(a_mat, rhs, left_side=True, lower=True, unit_diagonal=True)
    u, w = sol[..., :V], sol[..., V:]
    attn_intra = jnp.einsum('nbhik,nbhjk->nbhij', qc, kc) * decay

    def step(s, inp):
        q_i, k_i, u_i, w_i, g_i, a_i = inp
        v_new = u_i - jnp.einsum('bhck,bhkv->bhcv', w_i, s)
        o = jnp.einsum('bhck,bhkv->bhcv', q_i * jnp.exp(g_i)[..., None], s) + jnp.einsum('bhij,bhjv->bhiv', a_i, v_new)
        g_last = g_i[..., -1]
        s = s * jnp.exp(g_last)[..., None, None] + jnp.einsum(
            'bhck,bhcv->bhkv', k_i * jnp.exp(g_last[..., None] - g_i)[..., None], v_new)
        return s, o

    s_final, o = lax.scan(step, s0, (qc, kc, u, w, gc, attn_intra))
    return o.transpose(1, 0, 3, 2, 4).reshape(B, T, H, V), s_final


def gdn_branch(h, qkv, z, lp, init):
    B, T, _ = h.shape
    heads = lambda t: t.reshape(B, T, GDN_HEADS, GDN_HEAD_DIM)
    q, k, v = jnp.split(short_conv(qkv, lp['gdn_conv']), 3, axis=-1)
    q, k = l2normalize(heads(q)), l2normalize(heads(k))
    v = heads(v).astype(jnp.float32)
    outs, finals = [], []
    for d in range(2):
        beta = jax.nn.sigmoid((h @ lp['gdn_w_beta'][d]).astype(jnp.float32))
        g = -jnp.exp(lp['gdn_A_log'][d].astype(jnp.float32)) * jax.nn.softplus(
            (h @ lp['gdn_w_alpha'][d] + lp['gdn_dt_bias'][d]).astype(jnp.float32))
        if d == 0:
            o, s = chunk_gated_delta(q, k, v, g, beta, init[0])
        else:
            o, s = chunk_gated_delta(*(jnp.flip(t, 1) for t in (q, k, v, g, beta)), init[1])
            o = jnp.flip(o, 1)
        outs.append(o)
        finals.append(s)
    o = rmsnorm(outs[0] + outs[1], lp['gdn_norm']) * jax.nn.silu(heads(z).astype(jnp.float32))
    return o.reshape(B, T, GDN_DIM).astype(h.dtype), (finals[0], finals[1])


def axial_rope_tables(T):
    rows = T // GRID_W
    row = jnp.repeat(jnp.arange(rows), GRID_W).astype(jnp.float32)
    col = jnp.tile(jnp.arange(GRID_W), rows).astype(jnp.float32)
    half = ATTN_HEAD_DIM // 2
    inv = ROPE_THETA ** (-jnp.arange(0, half, 2, dtype=jnp.float32) / half)
    ang = jnp.concatenate([row[:, None] * inv, col[:, None] * inv], axis=-1)
    return jnp.cos(ang), jnp.sin(ang)


def apply_rope(x, cos, sin):
    x2 = x.reshape(*x.shape[:-1], -1, 2).astype(jnp.float32)
    x0, x1 = x2[..., 0], x2[..., 1]
    c, s = cos[None, :, None, :], sin[None, :, None, :]
    out = jnp.stack([x0 * c - x1 * s, x0 * s + x1 * c], axis=-1)
    return out.reshape(x.shape).astype(x.dtype)


def gqa(q, k, v):
    s = jnp.einsum('bkgqd,bksd->bkgqs', q, k).astype(jnp.float32) * (ATTN_HEAD_DIM ** -0.5)
    p = jax.nn.softmax(s, axis=-1).astype(v.dtype)
    return jnp.einsum('bkgqs,bksd->bkgqd', p, v)


def attn_kv(kv, lp):
    B, T, _ = kv.shape
    k, v = jnp.split(kv.reshape(B, T, 2 * ATTN_KV_HEADS, ATTN_HEAD_DIM), 2, axis=2)
    return rmsnorm(k, lp['attn_k_norm']), v


def attn_latent(q_lin, kv_lin, kc, vc, lp, cos, sin):
    B, T, _ = q_lin.shape
    q = apply_rope(rmsnorm(q_lin.reshape(B, T, ATTN_Q_HEADS, ATTN_HEAD_DIM), lp['attn_q_norm']), cos, sin)
    k, v = attn_kv(kv_lin, lp)
    k = apply_rope(k, cos, sin)
    k_all = jnp.concatenate([kc, k], axis=1).transpose(0, 2, 1, 3)
    v_all = jnp.concatenate([vc, v], axis=1).transpose(0, 2, 1, 3)
    nblk = T // ATTN_BLOCK
    qb = q.reshape(B, nblk, ATTN_BLOCK, ATTN_KV_HEADS, ATTN_GROUP, ATTN_HEAD_DIM).transpose(1, 0, 3, 4, 2, 5)
    o = lax.map(lambda blk: gqa(blk, k_all, v_all), qb)
    return o.transpose(1, 0, 4, 2, 3, 5).reshape(B, T, ATTN_Q_DIM)


def attn_context(q_lin, kc, vc, lp):
    B, Tc, _ = q_lin.shape
    q = rmsnorm(q_lin.reshape(B, Tc, ATTN_KV_HEADS, ATTN_GROUP, ATTN_HEAD_DIM), lp['attn_q_norm']).transpose(0, 2, 3, 1, 4)
    o = gqa(q, kc.transpose(0, 2, 1, 3), vc.transpose(0, 2, 1, 3))
    return o.transpose(0, 3, 1, 2, 4).reshape(B, Tc, ATTN_Q_DIM)


def merge_branches(h, ya, yb, yc, lp):
    gates = jax.nn.sigmoid(h @ lp['w_gate'] + lp['b_gate'])
    ga, gb, gc = jnp.split(gates, N_BRANCH, axis=-1)
    m = ga * (ya @ lp['w_up_a']) + gb * (yb @ lp['w_up_b']) + gc * (yc @ lp['w_up_c'])
    return m @ lp['w_out']


def token_mixer(hx, hc, lp, cos, sin, need_ctx):
    B = hx.shape[0]
    px = jnp.split(hx @ lp['w_in'], IN_SPLIT_POINTS, axis=-1)
    pc = jnp.split(hc @ lp['w_in'], IN_SPLIT_POINTS, axis=-1)
    zr = jnp.zeros((B, RWKV_HEADS, RWKV_HEAD_DIM, RWKV_HEAD_DIM), jnp.float32)
    zg = jnp.zeros((B, GDN_HEADS, GDN_HEAD_DIM, GDN_HEAD_DIM), jnp.float32)
    ya_c, st_a = rwkv_branch(hc, pc[0], lp, (zr, zr))
    ya_x, _ = rwkv_branch(hx, px[0], lp, st_a)
    yb_c, st_b = gdn_branch(hc, pc[1], pc[2], lp, (zg, zg))
    yb_x, _ = gdn_branch(hx, px[1], px[2], lp, st_b)
    kc, vc = attn_kv(pc[4], lp)
    yc_x = attn_latent(px[3], px[4], kc, vc, lp, cos, sin)
    out_x = merge_branches(hx, ya_x, yb_x, yc_x, lp)
    if not need_ctx:
        return out_x, None
    yc_c = attn_context(pc[3], kc, vc, lp)
    return out_x, merge_branches(hc, ya_c, yb_c, yc_c, lp)


def setup_inputs(seed: int = 0) -> dict:
    key = jax.random.key(seed)
    ks = iter(jax.random.split(key, 64))
    L, D = DEPTH, D_MODEL
    nrm = lambda shape, scale: jax.random.normal(next(ks), shape, jnp.float32) * scale
    uni = lambda shape, lo, hi: jax.random.uniform(next(ks), shape, jnp.float32, lo, hi)
    return {
        'x': nrm((BATCH, SEQ, D), 1.0),
        'c': nrm((BATCH, D), 1.0),
        'ctx': nrm((BATCH, CTX_LEN, D), 1.0),
        'c_ctx': nrm((D,), 1.0),
        'ada_w': nrm((L, D, N_MOD * D), D ** -0.5),
        'ada_b': nrm((L, N_MOD * D), 0.02),
        'norm1': 1.0 + nrm((L, D), 0.05),
        'norm2': 1.0 + nrm((L, D), 0.05),
        'w_in': nrm((L, D, IN_DIM), D ** -0.5),
        'rwkv_mu_x': uni((L, 3, D), 0.0, 1.0),
        'rwkv_mu_rkv': uni((L, 3, RWKV_DIM), 0.0, 1.0),
        'rwkv_w0': uni((L, 2, RWKV_DIM), -6.0, -1.0),
        'rwkv_w1': nrm((L, 2, D, RWKV_DECAY_LORA), D ** -0.5),
        'rwkv_w2': nrm((L, 2, RWKV_DECAY_LORA, RWKV_DIM), 0.1 * RWKV_DECAY_LORA ** -0.5),
        'rwkv_a0': nrm((L, 2, RWKV_DIM), 0.1),
        'rwkv_a1': nrm((L, 2, D, RWKV_ICLR_LORA), D ** -0.5),
        'rwkv_a2': nrm((L, 2, RWKV_ICLR_LORA, RWKV_DIM), 0.1 * RWKV_ICLR_LORA ** -0.5),
        'rwkv_g1': nrm((L, D, RWKV_GATE_LORA), D ** -0.5),
        'rwkv_g2': nrm((L, RWKV_GATE_LORA, RWKV_DIM), RWKV_GATE_LORA ** -0.5),
        'rwkv_k_k': 0.85 + nrm((L, RWKV_DIM), 0.05),
        'rwkv_k_a': 1.0 + nrm((L, RWKV_DIM), 0.05),
        'rwkv_r_k': nrm((L, RWKV_HEADS, RWKV_HEAD_DIM), 0.1),
        'rwkv_lnx_w': 1.0 + nrm((L, RWKV_DIM), 0.05),
        'rwkv_lnx_b': nrm((L, RWKV_DIM), 0.02),
        'gdn_conv': nrm((L, 3 * GDN_DIM, GDN_CONV_W), GDN_CONV_W ** -0.5),
        'gdn_w_alpha': nrm((L, 2, D, GDN_HEADS), D ** -0.5),
        'gdn_dt_bias': uni((L, 2, GDN_HEADS), -6.9, -2.3),
        'gdn_A_log': jnp.log(uni((L, 2, GDN_HEADS), 1.0, 16.0)),
        'gdn_w_beta': nrm((L, 2, D, GDN_HEADS), D ** -0.5),
        'gdn_norm': 1.0 + nrm((L, GDN_HEAD_DIM), 0.05),
        'attn_q_norm': 1.0 + nrm((L, ATTN_HEAD_DIM), 0.05),
        'attn_k_norm': 1.0 + nrm((L, ATTN_HEAD_DIM), 0.05),
        'w_up_a': nrm((L, RWKV_DIM, D), RWKV_DIM ** -0.5),
        'w_up_b': nrm((L, GDN_DIM, D), GDN_DIM ** -0.5),
        'w_up_c': nrm((L, ATTN_Q_DIM, D), ATTN_Q_DIM ** -0.5),
        'w_gate': nrm((L, D, N_BRANCH * D), D ** -0.5),
        'b_gate': nrm((L, N_BRANCH * D), 0.02),
        'w_out': nrm((L, D, D), D ** -0.5),
        'ffn_w1': nrm((L, D, FFN_HIDDEN), D ** -0.5),
        'ffn_w3': nrm((L, D, FFN_HIDDEN), D ** -0.5),
        'ffn_w2': nrm((L, FFN_HIDDEN, D), FFN_HIDDEN ** -0.5),
        'final_norm': 1.0 + nrm((D,), 0.05),
    }


def reference(x, c, ctx, c_ctx, ada_w, ada_b, norm1, norm2, w_in, rwkv_mu_x, rwkv_mu_rkv,
              rwkv_w0, rwkv_w1, rwkv_w2, rwkv_a0, rwkv_a1, rwkv_a2, rwkv_g1, rwkv_g2,
              rwkv_k_k, rwkv_k_a, rwkv_r_k, rwkv_lnx_w, rwkv_lnx_b, gdn_conv, gdn_w_alpha,
              gdn_dt_bias, gdn_A_log, gdn_w_beta, gdn_norm, attn_q_norm, attn_k_norm,
              w_up_a, w_up_b, w_up_c, w_gate, b_gate, w_out, ffn_w1, ffn_w3, ffn_w2, final_norm):
    T = x.shape[1]
    cos, sin = axial_rope_tables(T)
    silu_c = jax.nn.silu(c)
    silu_cc = jax.nn.silu(c_ctx)[None, :]
    cs = ctx
    for l in range(DEPTH):
        last = l == DEPTH - 1
        mod_x = (silu_c @ ada_w[l] + ada_b[l])[:, None, :]
        mod_c = (silu_cc @ ada_w[l] + ada_b[l])[:, None, :]
        sh1x, sc1x, g1x, sh2x, sc2x, g2x = jnp.split(mod_x, N_MOD, axis=-1)
        sh1c, sc1c, g1c, sh2c, sc2c, g2c = jnp.split(mod_c, N_MOD, axis=-1)
        lp = {
            'w_in': w_in[l], 'rwkv_mu_x': rwkv_mu_x[l], 'rwkv_mu_rkv': rwkv_mu_rkv[l],
            'rwkv_w0': rwkv_w0[l], 'rwkv_w1': rwkv_w1[l], 'rwkv_w2': rwkv_w2[l],
            'rwkv_a0': rwkv_a0[l], 'rwkv_a1': rwkv_a1[l], 'rwkv_a2': rwkv_a2[l],
            'rwkv_g1': rwkv_g1[l], 'rwkv_g2': rwkv_g2[l], 'rwkv_k_k': rwkv_k_k[l],
            'rwkv_k_a': rwkv_k_a[l], 'rwkv_r_k': rwkv_r_k[l], 'rwkv_lnx_w': rwkv_lnx_w[l],
            'rwkv_lnx_b': rwkv_lnx_b[l], 'gdn_conv': gdn_conv[l], 'gdn_w_alpha': gdn_w_alpha[l],
            'gdn_dt_bias': gdn_dt_bias[l], 'gdn_A_log': gdn_A_log[l], 'gdn_w_beta': gdn_w_beta[l],
            'gdn_norm': gdn_norm[l], 'attn_q_norm': attn_q_norm[l], 'attn_k_norm': attn_k_norm[l],
            'w_up_a': w_up_a[l], 'w_up_b': w_up_b[l], 'w_up_c': w_up_c[l],
            'w_gate': w_gate[l], 'b_gate': b_gate[l], 'w_out': w_out[l],
        }
        hx = modulate(rmsnorm(x, norm1[l]), sh1x, sc1x)
        hc = modulate(rmsnorm(cs, norm1[l]), sh1c, sc1c)
        yx, yc = token_mixer(hx, hc, lp, cos, sin, need_ctx=not last)
        x = x + g1x * yx
        x = x + g2x * swiglu(modulate(rmsnorm(x, norm2[l]), sh2x, sc2x), ffn_w1[l], ffn_w3[l], ffn_w2[l])
        if not last:
            cs = cs + g1c * yc
            cs = cs + g2c * swiglu(modulate(rmsnorm(cs, norm2[l]), sh2c, sc2c), ffn_w1[l], ffn_w3[l], ffn_w2[l])
    return rmsnorm(x, final_norm)
```

```python
import math
from contextlib import ExitStack
import numpy as np
import ml_dtypes
import concourse.bass as bass
import concourse.mybir as mybir
from concourse.bass_utils import run_bass_kernel_spmd

F32 = mybir.dt.float32
BF16 = mybir.dt.bfloat16
AF = mybir.ActivationFunctionType
ALU = mybir.AluOpType

ENGS = ['pe', 'act', 'dve', 'pool', 'sp']
RMS_EPS = 1e-6


class Res:
    dram = False

    def __init__(self, name):
        self.name = name
        self.lw = {}
        self.rd = {}
        self.sems = {}


class Tile(Res):
    def __init__(self, name, t):
        super().__init__(name)
        self.t = t

    def __getitem__(self, idx):
        return self.t[idx]


class DRes(Res):
    dram = True

    def __init__(self, name, ap):
        super().__init__(name)
        self.ap = ap

    def __getitem__(self, idx):
        return self.ap[idx]


class Sched:
    def __init__(self, nc, stack):
        self.nc = nc
        self.stack = stack
        self.eng = {'pe': nc.tensor, 'act': nc.scalar, 'dve': nc.vector, 'pool': nc.gpsimd, 'sp': nc.sync}
        self.cnt = {e: 0 for e in ENGS}
        self.esem = {e: stack.enter_context(nc.semaphore('es_' + e)) for e in ENGS if e != 'sp'}
        self.known = {e: {} for e in ENGS}
        self.dma_sems = []
        self.free_sems = []
        self.nwaits = 0
        self.nops = 0
        self.uid = 0
        self.pending = []
        self.trace = None

    def defer(self, out, in_, reads, writes, semres, key, kw):
        self.pending.append((out, in_, list(reads), list(writes), semres, key, kw))

    def flush(self):
        p, self.pending = self.pending, []
        for (out, in_, reads, writes, semres, key, kw) in p:
            self.dma('sp', out, in_, reads, writes, semres, key=key, **kw)

    def _conflict(self, reads, writes):
        for (_o, _i, pr, pw, _s, _key, _k) in self.pending:
            for x in writes:
                if any(x is y for y in pr) or any(x is y for y in pw):
                    return True
            for x in reads:
                if any(x is y for y in pw):
                    return True
        return False

    def sb(self, stack, name, shape, dtype):
        self.uid += 1
        t = stack.enter_context(self.nc.sbuf_tensor(f"{name}_{self.uid}", list(shape), dtype))
        return Tile(name, t)

    def ps(self, stack, name, shape, dtype=F32):
        self.uid += 1
        t = stack.enter_context(self.nc.psum_tensor(f"{name}_{self.uid}", list(shape), dtype))
        return Tile(name, t)

    def _waits(self, eng, reads, writes, dma=False):
        w = {}
        for r in reads:
            for sem, val in r.lw.items():
                if w.get(sem, 0) < val:
                    w[sem] = val
            if getattr(r, 'psum', False):
                for sem, val in r.rd.items():
                    if w.get(sem, 0) < val:
                        w[sem] = val
        for r in writes:
            for d in (r.lw, r.rd):
                for sem, val in d.items():
                    if w.get(sem, 0) < val:
                        w[sem] = val
        own = self.esem.get(eng)
        kn = self.known[eng]
        e = self.eng[eng]
        for sem, val in w.items():
            if sem is own and eng == 'pe' and not dma:
                continue
            if kn.get(sem, 0) >= val:
                continue
            kn[sem] = val
            e.wait_ge(sem, val)
            self.nwaits += 1
            if self.trace is not None:
                self.trace.append(f"  {eng} WAIT {sem.name}>={val}")

    def _post(self, ev_sem, ev_val, reads, writes, dma=False):
        for r in reads:
            if r.rd.get(ev_sem, 0) < ev_val:
                r.rd[ev_sem] = ev_val
        for r in writes:
            if r.dram or dma:
                if r.lw.get(ev_sem, 0) < ev_val:
                    r.lw[ev_sem] = ev_val
            else:
                r.lw = {ev_sem: ev_val}
                r.rd = {}

    def op(self, eng, fn, reads=(), writes=()):
        if self.pending and self._conflict(reads, writes):
            self.flush()
        self._waits(eng, reads, writes)
        ins = fn(self.eng[eng])
        self.cnt[eng] += 1
        sem = self.esem[eng]
        ins.then_inc(sem, 1)
        if self.trace is not None:
            self.trace.append(f"{eng} OP reads={[r.name for r in reads]} writes={[r.name for r in writes]} -> {sem.name}={self.cnt[eng]}")
        self._post(sem, self.cnt[eng], reads, writes)
        self.nops += 1
        return ins

    def dma(self, q, out, in_, reads, writes, semres, key=0, **kw):
        if self.pending and self._conflict(reads, writes):
            self.flush()
        self._waits(q, reads, writes, dma=True)
        slot = semres.sems.get(key)
        if slot is None:
            if self.free_sems:
                slot = self.free_sems.pop()
            else:
                self.uid += 1
                slot = [self.stack.enter_context(self.nc.semaphore(f"ds_{self.uid}")), 0]
            semres.sems[key] = slot
            self.dma_sems.append(slot)
        sem = slot[0]
        if slot[1] > 0 and self.known[q].get(sem, 0) < slot[1]:
            self.known[q][sem] = slot[1]
            self.eng[q].wait_ge(sem, slot[1])
            self.nwaits += 1
        ins = self.eng[q].dma_start(out=out, in_=in_, **kw)
        slot[1] += 16
        ins.then_inc(sem, 16)
        if self.trace is not None:
            self.trace.append(f"{q} DMA reads={[r.name for r in reads]} writes={[r.name for r in writes]} -> {sem.name}#{sem.num}={slot[1]}")
        self._post(sem, slot[1], reads, writes, dma=True)
        self.nops += 1
        return ins

    def barrier(self):
        self.flush()
        evs = {}
        for e, sem in self.esem.items():
            if self.cnt[e] > 0:
                evs[sem] = self.cnt[e]
        for slot in self.dma_sems:
            if slot[1] > 0:
                evs[slot[0]] = slot[1]
        for eng in ENGS:
            own = self.esem.get(eng)
            kn = self.known[eng]
            for sem, val in evs.items():
                if sem is own:
                    continue
                if kn.get(sem, 0) >= val:
                    continue
                kn[sem] = val
                self.eng[eng].wait_ge(sem, val)
                self.nwaits += 1
                if self.trace is not None:
                    self.trace.append(f"  {eng} BWAIT {sem.name}>={val}")


class Phase:
    def __init__(self, s):
        self.s = s
        self.stack = ExitStack()
        self.tiles = []
        if not hasattr(s, 'open_ph'):
            s.open_ph = []
        s.open_ph.append(self)

    def sb(self, name, shape, dtype):
        t = self.s.sb(self.stack, name, shape, dtype)
        self.tiles.append(t)
        return t

    def ps(self, name, shape, dtype=F32):
        t = self.s.ps(self.stack, name, shape, dtype)
        t.psum = True
        self.tiles.append(t)
        return t

    def ring(self, name, shape, dtype, n, psum=False):
        return Ring([(self.ps if psum else self.sb)(f"{name}{i}", shape, dtype) for i in range(n)])

    def close(self):
        s = self.s
        s.barrier()
        for t in self.tiles:
            for slot in t.sems.values():
                s.free_sems.append(slot)
                s.dma_sems.remove(slot)
            t.sems = {}
        self.stack.close()
        s.open_ph.remove(self)


class Ring:
    def __init__(self, tiles):
        self.tiles = tiles
        self.i = 0

    def next(self):
        t = self.tiles[self.i % len(self.tiles)]
        self.i += 1
        return t


class Cfg:
    def __init__(self, S=2, TC=256, TL=4096, L=4, debug=()):
        self.S, self.TC, self.TL, self.L = S, TC, TL, L
        self.TT = TC + TL
        self.debug = set(debug)

    def tiles(self, nt):
        out = []
        for t0 in range(0, self.TC, nt):
            out.append((0, t0, min(nt, self.TC - t0), t0))
        for t0 in range(0, self.TL, nt):
            out.append((1, t0, min(nt, self.TL - t0), self.TC + t0))
        return out

    def pcol(self, seg, t, halo):
        return (halo + t) if seg == 0 else (self.TC + 3 * halo + t)


WEIGHT_NAMES = ['ada_w', 'w_in', 'rwkv_w1', 'rwkv_w2', 'rwkv_a1', 'rwkv_a2', 'rwkv_g1', 'rwkv_g2',
                'gdn_w_alpha', 'gdn_w_beta', 'w_up_a', 'w_up_b', 'w_up_c', 'w_gate', 'w_out',
                'ffn_w1', 'ffn_w3', 'ffn_w2']


class Kern:
    def __init__(self, cfg, shapes):
        self.cfg = cfg
        self.nc = nc = bass.Bass("TRN2", target_bir_lowering=False)
        self.inp = {}
        for name, (shape, dt) in shapes.items():
            self.inp[name] = DRes(name, nc.dram_tensor(name, list(shape), dt, kind="ExternalInput").ap())
        self.scr = {}
        self.dbg_outs = []

    def dscr(self, name, shape, dtype):
        kind = "ExternalOutput" if name in self.cfg.debug else "Internal"
        r = DRes(name, self.nc.dram_tensor(name, list(shape), dtype, kind=kind).ap())
        if name in self.cfg.debug:
            self.dbg_outs.append(name)
        self.scr[name] = r
        return r

    def mm(self, ps, out, lt, lhsT, rt, rhs, start=True, stop=True):
        self.s.op('pe', lambda e: e.matmul(out, lhsT=lhsT, rhs=rhs, start=start, stop=stop),
                  [lt, rt] if lt is not rt else [lt], [ps])

    def act(self, ot, out, it, in_, func, bias=0.0, scale=1.0, extra=(), eng='act'):
        kw = {}
        if not (isinstance(bias, float) and bias == 0.0):
            kw['bias'] = bias
        if not (isinstance(scale, float) and scale == 1.0):
            kw['scale'] = scale
        self.s.op('act', lambda e: e.activation(out=out, in_=in_, func=func, **kw), [it] + list(extra), [ot])

    def tt(self, eng, ot, out, t0, in0, t1, in1, op):
        self.s.op(eng, lambda e: e.tensor_tensor(out=out, in0=in0, in1=in1, op=op), [t0, t1], [ot])

    def ts(self, eng, ot, out, t0, in0, s1, s2=None, op0=ALU.mult, op1=None, extra=()):
        if op1 is None:
            self.s.op(eng, lambda e: e.tensor_scalar(out=out, in0=in0, scalar1=s1, scalar2=None, op0=op0),
                      [t0] + list(extra), [ot])
        else:
            self.s.op(eng, lambda e: e.tensor_scalar(out=out, in0=in0, scalar1=s1, scalar2=s2, op0=op0, op1=op1),
                      [t0] + list(extra), [ot])

    def stt(self, ot, out, t0, in0, sc, t1, in1, op0, op1, extra=()):
        self.s.op('dve', lambda e: e.scalar_tensor_tensor(out=out, in0=in0, scalar=sc, in1=in1, op0=op0, op1=op1),
                  [t0, t1] + list(extra), [ot])

    def cp(self, eng, ot, out, it, in_):
        if eng == 'act':
            self.s.op('act', lambda e: e.activation(out=out, in_=in_, func=AF.Copy), [it], [ot])
        else:
            self.s.op(eng, lambda e: e.tensor_copy(out=out, in_=in_), [it], [ot])

    def ld(self, t, out, src, in_, q='sp', key=0):
        self.s.dma(q, out, in_, [src], [t], t, key=key)

    def st(self, dst, out, t, in_, q='sp', key=0, **kw):
        import os
        if dst.name in os.environ.get('NOST', '').split(','):
            return
        self.s.defer(out, in_, [t], [dst], t, key, kw)

    def fl(self):
        self.s.flush()

    def load_w(self, ph, wt, dst_fn, src, src_fn, ncols, rows_kc, piece=512, row_scale=None, col_scale=None,
               cast_eng=('dve', 'act')):
        i = 0
        for c0 in range(0, ncols, piece):
            c1 = min(ncols, c0 + piece)
            stg = self.wstage.next()
            sv = stg[:, 0:rows_kc, 0:c1 - c0]
            self.ld(stg, sv, src, src_fn(c0, c1), q='sp')
            if col_scale is not None:
                cst, cfn = col_scale
                for kc in range(rows_kc):
                    self.tt('dve', stg, stg[:, kc, 0:c1 - c0], stg, stg[:, kc, 0:c1 - c0], cst, cfn(c0, c1), ALU.mult)
            if row_scale is not None:
                rst, rfn = row_scale
                for kc in range(rows_kc):
                    self.ts('dve', wt, dst_fn(c0, c1)[:, kc, :], stg, stg[:, kc, 0:c1 - c0], rfn(kc), extra=[rst])
            else:
                eng = cast_eng[i % len(cast_eng)]
                self.cp(eng, wt, dst_fn(c0, c1), stg, sv)
            i += 1

    def build(self):
        cfg = self.cfg
        nc = self.nc
        S, L, TT, TC, TL = cfg.S, cfg.L, cfg.TT, cfg.TC, cfg.TL
        with ExitStack() as gstack:
            self.s = s = Sched(nc, gstack)
            if getattr(cfg, 'trace', False):
                s.trace = []
            self.xcur = self.dscr('xcur', [S, 1024, TT], F32)
            self.xmid = self.dscr('xmid', [S, 1024, TT], F32)
            self.hT = self.dscr('hT', [S, 1024, TT + 8], BF16)
            self.qT = self.dscr('qT', [S, 512, TT], BF16)
            self.kT = self.dscr('kT', [S, 128, TT], BF16)
            self.vtok = self.dscr('vtok', [S, TT, 130], BF16)
            self.yaT = self.dscr('yaT', [S, 512, TT], BF16)
            self.ybT = self.dscr('ybT', [S, 512, TT], BF16)
            self.ycT = self.dscr('ycT', [S, 512, TT], BF16)
            self.dbgT = self.dscr('dbg_scan', [64, 8, 2048], BF16)
            self.gd_u = self.dscr('gd_u', [S, 1536, TT + 8], BF16)
            self.gd_z = self.dscr('gd_z', [S, 512, TT], BF16)
            self.gd_ab = self.dscr('gd_ab', [S, 16, TT], F32)
            self.gd_gc = self.dscr('gd_gc', [S, 16, TT // 64], F32)
            self.gd_dm = self.dscr('gd_dm', [S, 2, TT // 64, 64, 512], BF16)
            self.gd_fm = self.dscr('gd_fm', [S, 8, 512, TT], BF16)
            self.gd_tm = self.dscr('gd_tm', [S, 4, TT, 512], BF16)
            self.gd_y = self.dscr('gd_y', [S, 2, 512, TT], F32)
            self.rw_fm = self.dscr('rw_fm', [S, 2, 4, 512, TT], BF16)
            self.rw_tm = self.dscr('rw_tm', [S, 5, TT, 512], BF16)
            self.rw_gc = self.dscr('rw_gc', [S, 2, 512, TT // 64], F32)
            self.rw_bg = self.dscr('rw_bg', [S, 2, 512, TT], BF16)
            self.rw_y = self.dscr('rw_y', [S, 2, 512, TT], F32)
            self.outT = DRes('outT', nc.dram_tensor('outT', [S, 1024, TL], F32, kind="ExternalOutput").ap())
            self.G = gph = Phase(s)
            self.ident = gph.sb('ident', [128, 128], BF16)
            self.ones128 = gph.sb('ones128', [128, 128], BF16)
            self.onesblk = gph.sb('onesblk', [128, 128], BF16)
            self.onesf = gph.sb('onesf', [128, 128], F32)
            self.MOD = gph.sb('MOD', [128, L, 48, S + 1], F32)
            self.epsc = gph.sb('epsc', [128, 1], F32)
            self.ld(self.ident, self.ident[:], self.inp['c_ident'], self.inp['c_ident'][:, :])
            self.ld(self.onesblk, self.onesblk[:], self.inp['c_onesblk'], self.inp['c_onesblk'][:, :])
            s.op('dve', lambda e: e.memset(self.ones128[:], 1.0), [], [self.ones128])
            s.op('dve', lambda e: e.memset(self.onesf[:], 1.0), [], [self.onesf])
            s.op('dve', lambda e: e.memset(self.epsc[:], RMS_EPS), [], [self.epsc])
            try:
                self.zero_pads()
                self.chk('pads')
                self.phase_mod()
                self.chk('mod')
                for l in range(L):
                    self.layer(l)
                self.phase_final()
            except StopIteration:
                for p in reversed(list(s.open_ph)):
                    if p is not gph:
                        p.close()
            gph.close()
            s.barrier()
        return nc

    def chk(self, name):
        if getattr(self.cfg, 'stop', None) == name:
            raise StopIteration

    def zero_pads(self):
        cfg = self.cfg
        ph = Phase(self.s)
        z = ph.sb('z', [128, 12, 2], BF16)
        self.s.op('dve', lambda e: e.memset(z[:], 0.0), [], [z])
        for s_ in range(cfg.S):
            for i, c in enumerate((0, cfg.TC + 2, cfg.TC + 4, cfg.TT + 6)):
                self.st(self.hT, self.hT[s_, :, c:c + 2].rearrange("(kc p) t -> p kc t", p=128), z, z[:, 0:8, 0:2], key=i)
                self.st(self.gd_u, self.gd_u[s_, :, c:c + 2].rearrange("(kc p) t -> p kc t", p=128), z, z[:, 0:12, 0:2], key=4 + i)
        ph.close()

    def phase_mod(self):
        cfg = self.cfg
        S, L = cfg.S, cfg.L
        s = self.s
        ph = Phase(s)
        cT = ph.sb('cT', [128, 8, S + 1], F32)
        sc = ph.sb('sc', [128, 8, S + 1], F32)
        bia = ph.sb('bia', [128, L, 48], F32)
        wst = ph.ring('adaw', [128, 8, 1024], F32, 2)
        pm = ph.ring('pm', [128, 8, 4], F32, 2, psum=True)
        self.ld(cT, cT[:], self.inp['cT'], self.inp['cT'][:, :, :])
        self.ld(bia, bia[:], self.inp['ada_bT'], self.inp['ada_bT'][:, :, :])
        self.act(sc, sc[:], cT, cT[:], AF.Silu)
        adaw = self.inp['ada_w']
        for l in range(L):
            for pc in range(6):
                w = wst.next()
                self.ld(w, w[:], adaw, adaw[l, :, pc * 1024:(pc + 1) * 1024].rearrange("(kc p) c -> p kc c", p=128))
                p = pm.next()
                for m in range(8):
                    for kc in range(8):
                        self.mm(p, p[:, m, 0:S + 1], w, w[:, kc, m * 128:(m + 1) * 128], sc, sc[:, kc, :],
                                start=(kc == 0), stop=(kc == 7))
                self.tt('dve', self.MOD, self.MOD[:, l, pc * 8:(pc + 1) * 8, :], p, p[:, :, 0:S + 1],
                        bia, bia[:, l, pc * 8:(pc + 1) * 8].unsqueeze(2).to_broadcast([128, 8, S + 1]), ALU.add)
        ph.close()

    def norm_mod(self, ph, xt, n, A, Acol, B, Bcol, hb, sqr, psr, tmr):
        sq = sqr.next()
        self.act(sq, sq[:, :, 0:n], xt, xt[:, :, 0:n], AF.Square)
        pss = psr.next()
        for kc in range(8):
            self.mm(pss, pss[:, 0:n], self.ones128, self.ones128[:], sq, sq[:, kc, 0:n], start=(kc == 0), stop=(kc == 7))
        if not hasattr(ph, 'rsr'):
            ph.rsr = ph.ring('rs', [128, 512], F32, 2)
        lnv = ph.rsr.next()
        self.act(lnv, lnv[:, 0:n], pss, pss[:, 0:n], AF.Ln, bias=self.epsc[:, 0:1], scale=1.0 / 1024, extra=[self.epsc])
        rstd = ph.rsr.next()
        self.act(rstd, rstd[:, 0:n], lnv, lnv[:, 0:n], AF.Exp, scale=-0.5)
        for kc in range(8):
            tmp = tmr.next()
            self.stt(tmp, tmp[:, 0:n], xt, xt[:, kc, 0:n], Acol(kc), rstd, rstd[:, 0:n], ALU.mult, ALU.mult, extra=[A])
            if B is None:
                self.cp('act', hb, hb[:, kc, 0:n], tmp, tmp[:, 0:n])
            else:
                self.act(hb, hb[:, kc, 0:n], tmp, tmp[:, 0:n], AF.Identity, bias=Bcol(kc), extra=[B])

    def layer_consts(self, l):
        cfg = self.cfg
        S = cfg.S
        ph = self.LC
        s = self.s
        g12 = ph.sb('g12', [128, 2, 8], F32)
        self.ld(g12, g12[:, 0, :], self.inp['norm1T'], self.inp['norm1T'][:, l, :])
        self.ld(g12, g12[:, 1, :], self.inp['norm2T'], self.inp['norm2T'][:, l, :], key=1)
        self.A1 = ph.sb('A1', [128, 8, S + 1], F32)
        self.A2 = ph.sb('A2', [128, 8, S + 1], F32)
        for A, j, gi in ((self.A1, 1, 0), (self.A2, 4, 1)):
            self.ts('dve', A, A[:], self.MOD, self.MOD[:, l, j * 8:(j + 1) * 8, :], 1.0, op0=ALU.add)
            self.tt('dve', A, A[:], A, A[:], g12, g12[:, gi, :].unsqueeze(2).to_broadcast([128, 8, S + 1]), ALU.mult)

    def modcol(self, l, j, which):
        return lambda kc: self.MOD[:, l, j * 8 + kc, which:which + 1]

    def layer(self, l):
        cfg = self.cfg
        self.LC = Phase(self.s)
        self.layer_consts(l)
        use = getattr(cfg, 'use', (1, 1, 1))
        try:
            self.phase_norm1(l)
            self.chk('norm1')
            for j, yt in enumerate((self.yaT, self.ybT, self.ycT)):
                if not use[j]:
                    self.zero_y(yt)
            if use[0]:
                self.phase_rwkv_proj(l)
                self.chk('rwkv_proj')
                self.phase_rwkv_scan(l)
                self.chk('rwkv_scan')
                self.phase_rwkv_post(l)
                self.chk('rwkv_post')
            if use[1]:
                self.phase_gdn_proj(l)
                self.chk('gdn_proj')
                self.phase_gdn_prep(l)
                self.chk('gdn_prep')
                self.phase_gdn_scan(l)
                self.chk('gdn_scan')
                self.phase_gdn_post(l)
                self.chk('gdn_post')
            if use[2]:
                self.phase_attn_proj(l)
                self.chk('attn_proj')
                self.phase_attn(l)
                self.chk('attn')
            self.phase_merge(l)
            self.chk('merge')
            self.phase_ffn(l)
            self.chk('ffn')
        except StopIteration:
            raise
        self.LC.close()

    def zero_y(self, yt):
        cfg = self.cfg
        ph = Phase(self.s)
        z = ph.sb('z', [128, 4, 512], BF16)
        self.s.op('dve', lambda e: e.memset(z[:], 0.0), [], [z])
        for s_ in range(cfg.S):
            for (seg, t0, n, g0) in cfg.tiles(512):
                self.st(yt, yt[s_, :, g0:g0 + n].rearrange("(kc p) t -> p kc t", p=128), z, z[:, :, 0:n])
        ph.close()

    def xsrc(self, l):
        return self.inp['xin'] if l == 0 else self.xcur

    def phase_norm1(self, l):
        cfg = self.cfg
        S = cfg.S
        ph = Phase(self.s)
        xr = ph.ring('x', [128, 8, 512], F32, 2)
        hr = ph.ring('hb', [128, 8, 512], BF16, 2)
        sqr = ph.ring('sq', [128, 8, 512], BF16, 1)
        tmr = ph.ring('tm', [128, 512], F32, 4)
        psr = ph.ring('pss', [128, 512], F32, 2, psum=True)
        xs = self.xsrc(l)
        for s_ in range(S):
            for (seg, t0, n, g0) in cfg.tiles(512):
                which = S if seg == 0 else s_
                xt = xr.next()
                self.ld(xt, xt[:, :, 0:n], xs, xs[s_, :, g0:g0 + n].rearrange("(kc p) t -> p kc t", p=128))
                self.fl()
                hb = hr.next()
                self.norm_mod(ph, xt, n, self.A1, lambda kc: self.A1[:, kc, which:which + 1],
                              self.MOD, self.modcol(l, 0, which), hb, sqr, psr, tmr)
                c0 = cfg.pcol(seg, t0, 2)
                self.st(self.hT, self.hT[s_, :, c0:c0 + n].rearrange("(kc p) t -> p kc t", p=128), hb, hb[:, :, 0:n])
        ph.close()

    def phase_attn_proj(self, l):
        cfg = self.cfg
        S = cfg.S
        s = self.s
        ph = Phase(s)
        self.wstage = ph.ring('wstg', [128, 8, 512], F32, 2)
        win = self.inp['w_in']
        winp = self.inp['w_in_perm']
        QO = 3 * 512 + 4 * 512
        wq = ph.sb('wq', [128, 8, 512], BF16)
        wqs = ph.sb('wqs', [128, 8, 512], BF16)
        wk = ph.sb('wk', [128, 8, 128], BF16)
        wks = ph.sb('wks', [128, 8, 128], BF16)
        wv = ph.sb('wv', [128, 8, 128], BF16)
        r = lambda ap: ap.rearrange("(kc p) c -> p kc c", p=128)
        self.load_w(ph, wq, lambda a, b: wq[:, :, a:b], win, lambda a, b: r(win[l, :, QO + a:QO + b]), 512, 8)
        self.load_w(ph, wqs, lambda a, b: wqs[:, :, a:b], winp, lambda a, b: r(winp[l, :, a:b]), 512, 8)
        self.load_w(ph, wk, lambda a, b: wk[:, :, a:b], win, lambda a, b: r(win[l, :, QO + 512 + a:QO + 512 + b]), 128, 8)
        self.load_w(ph, wks, lambda a, b: wks[:, :, a:b], winp, lambda a, b: r(winp[l, :, 512 + a:512 + b]), 128, 8)
        self.load_w(ph, wv, lambda a, b: wv[:, :, a:b], win, lambda a, b: r(win[l, :, QO + 640 + a:QO + 640 + b]), 128, 8)
        self.chk('ap_w')
        gn = ph.sb('gn', [128, 4], F32)
        self.ld(gn, gn[:], self.inp['attn_gT'], self.inp['attn_gT'][:, l, :])
        hr = ph.ring('hb', [128, 8, 512], BF16, 2)
        csr = ph.ring('cs', [128, 2, 512], F32, 2)
        qst = ph.ring('qst', [128, 5, 512], BF16, 2)
        vst = ph.ring('vst', [128, 4, 130], BF16, 2)
        for v in vst.tiles:
            s.op('dve', lambda e: e.memset(v[:], 1.0), [], [v])
        sqr = ph.ring('sq', [128, 512], BF16, 2)
        tmr = ph.ring('tm', [128, 512], F32, 8)
        pq = ph.ring('pq', [128, 512], F32, 4, psum=True)
        pn = ph.ring('pn', [128, 512], F32, 2, psum=True)
        pv = ph.ring('pv', [128, 128], F32, 2, psum=True)
        cst = self.inp['c_rope']
        for s_ in range(S):
            for (seg, t0, n, g0) in cfg.tiles(512):
                hb = hr.next()
                c0 = cfg.pcol(seg, t0, 2)
                self.ld(hb, hb[:, :, 0:n], self.hT, self.hT[s_, :, c0:c0 + n].rearrange("(kc p) t -> p kc t", p=128))
                cs = csr.next()
                self.ld(cs, cs[:, :, 0:n], cst, cst[:, :, g0:g0 + n])
                self.fl()
                qs = qst.next()
                for c in range(5):
                    w0, w1, col, gi = (wq, wqs, c * 128, 0) if c < 4 else (wk, wks, 0, 2)
                    p0 = pq.next()
                    for kc in range(8):
                        self.mm(p0, p0[:, 0:n], w0, w0[:, kc, col:col + 128], hb, hb[:, kc, 0:n], start=(kc == 0), stop=(kc == 7))
                    p1 = pq.next()
                    for kc in range(8):
                        self.mm(p1, p1[:, 0:n], w1, w1[:, kc, col:col + 128], hb, hb[:, kc, 0:n], start=(kc == 0), stop=(kc == 7))
                    import os
                    SK = os.environ.get('SKIP', '')
                    if 'post' in SK:
                        self.cp('dve', qs, qs[:, c, 0:n], p0, p0[:, 0:n])
                        self.cp('act', sqr.next(), sqr.tiles[0][:, 0:n], p1, p1[:, 0:n])
                        continue
                    sq = sqr.next()
                    if 'nosq' in SK:
                        t0_ = tmr.next()
                        self.cp('act', t0_, t0_[:, 0:n], p0, p0[:, 0:n])
                        self.act(sq, sq[:, 0:n], t0_, t0_[:, 0:n], AF.Square)
                    else:
                        self.act(sq, sq[:, 0:n], p0, p0[:, 0:n], AF.Square)
                    pss = pn.next()
                    self.mm(pss, pss[:, 0:n], self.onesblk, self.onesblk[:], sq, sq[:, 0:n])
                    lnv = tmr.next()
                    self.act(lnv, lnv[:, 0:n], pss, pss[:, 0:n], AF.Ln, bias=self.epsc[:, 0:1], scale=1.0 / 64, extra=[self.epsc])
                    rstd = tmr.next()
                    self.act(rstd, rstd[:, 0:n], lnv, lnv[:, 0:n], AF.Exp, scale=-0.5)
                    a = tmr.next()
                    b = tmr.next()
                    self.act(a, a[:, 0:n], p0, p0[:, 0:n], AF.Copy, scale=gn[:, gi:gi + 1], extra=[gn])
                    self.act(b, b[:, 0:n], p1, p1[:, 0:n], AF.Copy, scale=gn[:, gi + 1:gi + 2], extra=[gn])
                    self.tt('dve', a, a[:, 0:n], a, a[:, 0:n], cs, cs[:, 0, 0:n], ALU.mult)
                    self.tt('dve', b, b[:, 0:n], b, b[:, 0:n], cs, cs[:, 1, 0:n], ALU.mult)
                    self.tt('dve', a, a[:, 0:n], a, a[:, 0:n], b, b[:, 0:n], ALU.add)
                    self.tt('dve', qs, qs[:, c, 0:n], a, a[:, 0:n], rstd, rstd[:, 0:n], ALU.mult)
                self.chk('ap_q')
                if not getattr(cfg, 'skipq', False):
                    self.st(self.qT, self.qT[s_, :, g0:g0 + n].rearrange("(c p) t -> p c t", p=128), qs, qs[:, 0:4, 0:n])
                self.chk('ap_qs1')
                if getattr(cfg, 'ksem', False):
                    if not hasattr(ph, 'ksems'):
                        ph.ksems = {}
                    kr = ph.ksems.setdefault(id(qs), Tile('ksem', None))
                    if kr not in ph.tiles:
                        ph.tiles.append(kr)
                    self.s.defer(self.kT[s_, :, g0:g0 + n], qs[:, 4, 0:n], [qs], [self.kT], kr, 0, {})
                else:
                    self.st(self.kT, self.kT[s_, :, g0:g0 + n], qs, qs[:, 4, 0:n], key=1)
                self.chk('ap_qst')
                vs = vst.next()
                nb = n // 128
                for tb in range(nb if 'v' not in os.environ.get('SKIP', '') else 0):
                    p = pv.next()
                    for kc in range(8):
                        self.mm(p, p[:, :], hb, hb[:, kc, tb * 128:(tb + 1) * 128], wv, wv[:, kc, :], start=(kc == 0), stop=(kc == 7))
                    self.cp('act', vs, vs[:, tb, :].rearrange("p (g d) -> p g d", g=2)[:, :, 0:64], p, p[:, :].rearrange("p (g d) -> p g d", g=2))
                self.chk('ap_v1')
                self.st(self.vtok, self.vtok[s_, g0:g0 + n, :].rearrange("(tb p) c -> p tb c", p=128), vs, vs[:, 0:nb, :])
        ph.close()

    def phase_attn(self, l):
        cfg = self.cfg
        S, TT, TC = cfg.S, cfg.TT, cfg.TC
        s = self.s
        ph = Phase(s)
        NST = TT // 128
        gq = ph.sb('gq', [128, 2, 64], F32)
        self.ld(gq, gq[:], self.inp['attn_gB'], self.inp['attn_gB'][l:l + 1, :, :].to_broadcast([128, 2, 64]))
        mx = ph.sb('mx', [128, 2], F32)
        s.op('dve', lambda e: e.tensor_reduce(out=mx[:], in_=gq[:], axis=mybir.AxisListType.X, op=ALU.max,
                                              apply_absolute_value=True), [gq], [mx])
        negM = ph.sb('negM', [128, 1], F32)
        self.stt(negM, negM[:], mx, mx[:, 0:1], -8.0, mx, mx[:, 1:2], ALU.mult, ALU.mult)
        kt = ph.sb('kt', [64, 2, TT], BF16)
        va = ph.sb('va', [128, NST, 130], BF16)
        qr = ph.ring('q', [64, 8, 512], BF16, 2)
        ptr = ph.ring('pt', [128, 512], BF16, 4)
        ysr = ph.ring('ys', [64, 8, 512], BF16, 2)
        osr = ph.ring('os', [64, 512], F32, 2)
        rcr = ph.ring('rc', [65, 512], F32, 2)
        psr = ph.ring('ps', [128, 512], F32, 4, psum=True)
        por = ph.ring('po', [65, 512], F32, 2, psum=True)
        pbr = ph.ring('pb', [64, 512], F32, 1, psum=True)
        for s_ in range(S):
            self.ld(kt, kt[:], self.kT, self.kT[s_].rearrange("(g d) t -> d g t", g=2))
            self.ld(va, va[:], self.vtok, self.vtok[s_].rearrange("(tb p) c -> p tb c", p=128))
            for (seg, t0, n, g0) in cfg.tiles(512):
                if seg == 0 and l == cfg.L - 1:
                    continue
                q = qr.next()
                self.ld(q, q[:, :, 0:n], self.qT, self.qT[s_, :, g0:g0 + n].rearrange("(h d) t -> d h t", d=64))
                self.fl()
                nst = (TC // 128) if seg == 0 else NST
                ys = ysr.next()
                for h in range(8):
                    g = h // 4
                    po = por.next()
                    for st in range(nst):
                        p = psr.next()
                        self.mm(p, p[:, 0:n], kt, kt[:, g, st * 128:(st + 1) * 128], q, q[:, h, 0:n])
                        pt = ptr.next()
                        self.act(pt, pt[:, 0:n], p, p[:, 0:n], AF.Exp, bias=negM[:, 0:1], scale=0.125, extra=[negM])
                        self.mm(po, po[0:65, 0:n], va, va[:, st, g * 65:(g + 1) * 65], pt, pt[:, 0:n],
                                start=(st == 0), stop=(st == nst - 1))
                    rc = rcr.next()
                    s.op('dve', lambda e: e.reciprocal(out=rc[64:65, 0:n], in_=po[64:65, 0:n]), [po], [rc])
                    os_ = osr.next()
                    self.cp('act', os_, os_[:, 0:n], po, po[0:64, 0:n])
                    pb = pbr.next()
                    self.mm(pb, pb[:, 0:n], self.onesf, self.onesf[64:65, 0:64], rc, rc[64:65, 0:n])
                    self.tt('dve', ys, ys[:, h, 0:n], os_, os_[:, 0:n], pb, pb[:, 0:n], ALU.mult)
                self.st(self.ycT, self.ycT[s_, :, g0:g0 + n].rearrange("(h d) t -> d h t", d=64), ys, ys[:, :, 0:n])
        ph.close()

    def phase_rwkv_proj(self, l):
        cfg = self.cfg
        S, TT = cfg.S, cfg.TT
        s = self.s
        ph = Phase(s)
        CDEC = math.exp(-0.5)
        wc = ph.sb('wc', [128, 8, 1536], BF16)
        ws = ph.sb('ws', [128, 8, 1536], BF16)
        cmask = ph.sb('cmask', [128, 512], F32)
        pp = ph.sb('pp', [128, 9, 4], F32)
        l1c = ph.sb('l1c', [128, 8, 416], BF16)
        l1s = ph.sb('l1s', [128, 8, 416], BF16)
        w2w = ph.sb('w2w', [128, 1, 512], BF16)
        w2a = ph.sb('w2a', [128, 1, 512], BF16)
        g2a = ph.sb('g2a', [128, 1, 512], BF16)
        g2b = ph.sb('g2b', [32, 1, 512], BF16)
        omka = ph.sb('omka', [128, 4], F32)
        outer = ph
        ph = Phase(s)
        self.wstage = ph.ring('wstg', [128, 8, 256], F32, 2)
        r = lambda ap: ap.rearrange("(kc p) c -> p kc c", p=128)
        win = self.inp['w_in']
        self.ld(pp, pp[:], self.inp['rwkv_pT'], self.inp['rwkv_pT'][:, l, :, :])
        self.ts('dve', omka, omka[:], pp, pp[:, 5, :], -1.0, 1.0, op0=ALU.mult, op1=ALU.add)
        mux = ph.sb('mux', [128, 3, 8], F32)
        self.ld(mux, mux[:], self.inp['rwkv_mu_xT'], self.inp['rwkv_mu_xT'][:, l, :, :])
        mxc = ph.sb('mxc', [128, 3, 8], F32)
        mxs = ph.sb('mxs', [128, 3, 8], F32)
        self.ts('dve', mxc, mxc[:], mux, mux[:], -1.0, 1.0, op0=ALU.mult, op1=ALU.add)
        self.ts('dve', mxs, mxs[:], mux, mux[:], 0.5, op0=ALU.mult)
        mur = ph.sb('mur', [128, 1536], F32)
        self.ld(mur, mur[:], self.inp['rwkv_mu_rkv'], self.inp['rwkv_mu_rkv'][l:l + 1, :].to_broadcast([128, 1536]))
        murc = ph.sb('murc', [128, 1536], F32)
        self.ts('dve', murc, murc[:], mur, mur[:], -1.0, 1.0, op0=ALU.mult, op1=ALU.add)
        self.ts('dve', mur, mur[:], mur, mur[:], 0.5, op0=ALU.mult)
        self.load_w(ph, wc, lambda a, b: wc[:, :, a:b], win, lambda a, b: r(win[l, :, a:b]), 1536, 8, piece=256,
                    col_scale=(murc, lambda a, b: murc[:, a:b]))
        self.load_w(ph, ws, lambda a, b: ws[:, :, a:b], win, lambda a, b: r(win[l, :, a:b]), 1536, 8, piece=256,
                    col_scale=(mur, lambda a, b: mur[:, a:b]))
        srcs = [(self.inp['rwkv_w1'], 0, 0, 0), (self.inp['rwkv_w1'], 1, 64, 0), (self.inp['rwkv_a1'], 0, 128, 1),
                (self.inp['rwkv_a1'], 1, 192, 1)]
        for (src, d, off, mi) in srcs:
            for (dst, sc) in ((l1c, mxc), (l1s, mxs)):
                self.load_w(ph, dst, lambda a, b, dst=dst, off=off: dst[:, :, off + a:off + b], src,
                            lambda a, b, src=src, d=d: r(src[l, d, :, a:b]), 64, 8, piece=256,
                            row_scale=(sc, lambda kc, sc=sc, mi=mi: sc[:, mi, kc:kc + 1]))
        g1 = self.inp['rwkv_g1']
        for (dst, sc) in ((l1c, mxc), (l1s, mxs)):
            self.load_w(ph, dst, lambda a, b, dst=dst: dst[:, :, 256 + a:256 + b], g1, lambda a, b: r(g1[l, :, a:b]), 160, 8,
                        piece=256, row_scale=(sc, lambda kc, sc=sc: sc[:, 2, kc:kc + 1]))
        for dst, nm in ((w2w, 'rwkv_w2'), (w2a, 'rwkv_a2')):
            src = self.inp[nm]
            self.load_w(ph, dst, lambda a, b, dst=dst: dst[:, :, a:b], src,
                        lambda a, b, src=src: src[l, :, :, a:b].rearrange("d (o k) c -> (d k) o c", o=1), 512, 1, piece=256)
        g2 = self.inp['rwkv_g2']
        self.load_w(ph, g2a, lambda a, b: g2a[:, :, a:b], g2, lambda a, b: g2[l, 0:128, a:b].rearrange("(o k) c -> k o c", o=1), 512, 1, piece=256)
        stg = self.wstage.next()
        self.ld(stg, stg[0:32, 0, 0:256], g2, g2[l, 128:160, 0:256])
        self.cp('dve', g2b, g2b[:, 0, 0:256], stg, stg[0:32, 0, 0:256])
        stg = self.wstage.next()
        self.ld(stg, stg[0:32, 0, 0:256], g2, g2[l, 128:160, 256:512])
        self.cp('dve', g2b, g2b[:, 0, 256:512], stg, stg[0:32, 0, 0:256])
        self.ld(cmask, cmask[:], self.inp['c_chunkmask'], self.inp['c_chunkmask'][:, :])
        ph.close()
        ph = outer
        self.chk('rp_w')
        hr = ph.ring('hb', [128, 8, 516], BF16, 2)
        hsr = ph.ring('hs', [128, 8, 512], BF16, 1)
        l1o = ph.ring('l1o', [128, 4, 512], BF16, 1)
        F = lambda nm, k=1: ph.ring(nm, [128, 512], F32, k)
        B = lambda nm, k=1: ph.ring(nm, [128, 512], BF16, k)
        rsb_r, ksb_r, vsb_r, kk_r = F('rsb'), F('ksb'), F('vsb'), F('kk')
        vbf_r = B('vbf')
        sg_r, a_r, G_r, Ge_r, Te_r, Hi_r = F('sg', 2), F('a_', 2), F('G', 2), F('Ge', 2), F('Te', 2), F('Hi', 1)
        E_r = F('E', 5)
        kd_r, bd_r, tmp_r = F('kd', 2), F('bd', 2), F('tmp', 4)
        sq_r = B('sq', 2)
        fst = ph.ring('fst', [128, 8, 512], BF16, 1)
        kb_r = B('kb', 4)
        tms = ph.ring('tms', [128, 5, 4, 512], BF16, 1)
        ost = ph.ring('ost', [128, 2, 512], BF16, 2)
        gcs = ph.ring('gcs', [128, 2, 8], F32, 2)
        pp_r = ph.ring('pp', [128, 512], F32, 6, psum=True)
        pt_r = ph.ring('ptr', [128, 4, 128], BF16, 2, psum=True)
        for s_ in range(S):
            for (seg, t0, n, g0) in cfg.tiles(512):
                nb = n // 128
                nch = n // 64
                hb = hr.next()
                c0 = cfg.pcol(seg, t0, 2)
                self.ld(hb, hb[:, :, 0:n + 4], self.hT, self.hT[s_, :, c0 - 2:c0 + n + 2].rearrange("(kc p) t -> p kc t", p=128))
                self.fl()
                hs = hsr.next()
                self.tt('dve', hs, hs[:, :, 0:n], hb, hb[:, :, 1:1 + n], hb, hb[:, :, 3:3 + n], ALU.add)
                hc = lambda kc: hb[:, kc, 2:2 + n]

                def proj(p, pap, wA, wB, col, m):
                    for kc in range(8):
                        self.mm(p, pap, wA, wA[:, kc, col:col + m], hb, hc(kc), start=(kc == 0), stop=False)
                    for kc in range(8):
                        self.mm(p, pap, wB, wB[:, kc, col:col + m], hs, hs[:, kc, 0:n], start=False, stop=(kc == 7))
                lo = l1o.next()
                for i, (col, m, fn) in enumerate(((0, 128, AF.Tanh), (128, 128, AF.Copy), (256, 128, AF.Sigmoid), (384, 32, AF.Sigmoid))):
                    p = pp_r.next()
                    proj(p, p[0:m, 0:n], l1c, l1s, col, m)
                    self.act(lo, lo[0:m, i, 0:n], p, p[0:m, 0:n], fn)
                self.chk('rp_l1')
                tm = tms.next()
                for c in range(4):
                    cs_ = slice(c * 128, (c + 1) * 128)
                    pr, pk, pv = pp_r.next(), pp_r.next(), pp_r.next()
                    proj(pr, pr[:, 0:n], wc, ws, c * 128, 128)
                    proj(pk, pk[:, 0:n], wc, ws, 512 + c * 128, 128)
                    proj(pv, pv[:, 0:n], wc, ws, 1024 + c * 128, 128)
                    self.chk('rp_p')
                    rsb, ksb, vsb, vbf = rsb_r.next(), ksb_r.next(), vsb_r.next(), vbf_r.next()
                    self.cp('act', rsb, rsb[:, 0:n], pr, pr[:, 0:n])
                    self.cp('act', ksb, ksb[:, 0:n], pk, pk[:, 0:n])
                    self.cp('act', vsb, vsb[:, 0:n], pv, pv[:, 0:n])
                    self.cp('dve', vbf, vbf[:, 0:n], pv, pv[:, 0:n])
                    self.chk('rp_cp')
                    kk = kk_r.next()
                    self.ts('dve', kk, kk[:, 0:n], ksb, ksb[:, 0:n], pp[:, 4, c:c + 1], extra=[pp])
                    self.chk('rp_ts')
                    sq = sq_r.next()
                    self.act(sq, sq[:, 0:n], kk, kk[:, 0:n], AF.Square)
                    pss = pp_r.next()
                    self.mm(pss, pss[:, 0:n], self.onesblk, self.onesblk[:], sq, sq[:, 0:n])
                    t1 = tmp_r.next()
                    self.act(t1, t1[:, 0:n], pss, pss[:, 0:n], AF.Ln, bias=self.epsc[:, 0:1], extra=[self.epsc])
                    t2 = tmp_r.next()
                    self.act(t2, t2[:, 0:n], t1, t1[:, 0:n], AF.Exp, scale=-0.5)
                    self.tt('dve', kk, kk[:, 0:n], kk, kk[:, 0:n], t2, t2[:, 0:n], ALU.mult)
                    self.chk('rp_kk')
                    fs = fst.next()
                    ksum = None
                    gc = gcs.next()
                    for d in range(2):
                        b0 = 64 * d
                        pw, pa = pp_r.next(), pp_r.next()
                        self.mm(pw, pw[:, 0:n], w2w, w2w[b0:b0 + 64, 0, cs_], lo, lo[b0:b0 + 64, 0, 0:n])
                        self.mm(pa, pa[:, 0:n], w2a, w2a[b0:b0 + 64, 0, cs_], lo, lo[b0:b0 + 64, 1, 0:n])
                        sg, a_ = sg_r.next(), a_r.next()
                        self.act(sg, sg[:, 0:n], pw, pw[:, 0:n], AF.Sigmoid, bias=pp[:, 0 + d, c:c + 1], extra=[pp])
                        self.act(a_, a_[:, 0:n], pa, pa[:, 0:n], AF.Sigmoid, bias=pp[:, 2 + d, c:c + 1], extra=[pp])
                        G, Ge, Te = G_r.next(), Ge_r.next(), Te_r.next()
                        s.op('dve', lambda e: e.tensor_tensor_scan(out=G[:, 0:n], data0=cmask[:, 0:n], data1=sg[:, 0:n],
                                                                    initial=0.0, op0=ALU.mult, op1=ALU.add), [cmask, sg], [G])
                        self.tt('dve', Ge, Ge[:, 0:n], G, G[:, 0:n], sg, sg[:, 0:n], ALU.subtract)
                        Gv = G[:, 0:n].rearrange("p (c t) -> p c t", t=64)
                        self.tt('dve', Te, Te[:, 0:n].rearrange("p (c t) -> p c t", t=64), G,
                                Gv[:, :, 63:64].to_broadcast([128, nch, 64]), G, Gv, ALU.subtract)
                        self.act(gc, gc[:, d, 0:nch], G, Gv[:, :, 63], AF.Exp, scale=-CDEC)
                        if d == 0:
                            inc, exc, toend = G, Ge, Te
                        else:
                            Hi = Hi_r.next()
                            self.tt('dve', Hi, Hi[:, 0:n], Te, Te[:, 0:n], sg, sg[:, 0:n], ALU.add)
                            inc, exc, toend = Hi, Te, Ge
                        E1, E2, E3, E4 = E_r.next(), E_r.next(), E_r.next(), E_r.next()
                        self.act(E1, E1[:, 0:n], inc, inc[:, 0:n], AF.Exp, scale=-CDEC)
                        self.act(E2, E2[:, 0:n], inc, inc[:, 0:n], AF.Exp, scale=CDEC)
                        self.act(E3, E3[:, 0:n], exc, exc[:, 0:n], AF.Exp, scale=-CDEC)
                        self.act(E4, E4[:, 0:n], toend, toend[:, 0:n], AF.Exp, scale=-CDEC)
                        self.chk('rp_exp')
                        kd, bd = kd_r.next(), bd_r.next()
                        self.ts('dve', kd, kd[:, 0:n], a_, a_[:, 0:n], pp[:, 5, c:c + 1], omka[:, c:c + 1], op0=ALU.mult, op1=ALU.add,
                                extra=[pp, omka])
                        self.tt('dve', kd, kd[:, 0:n], kd, kd[:, 0:n], ksb, ksb[:, 0:n], ALU.mult)
                        self.tt('dve', bd, bd[:, 0:n], kk, kk[:, 0:n], a_, a_[:, 0:n], ALU.mult)
                        self.tt('dve', fs, fs[:, d * 4 + 0, 0:n], rsb, rsb[:, 0:n], E1, E1[:, 0:n], ALU.mult)
                        self.tt('dve', fs, fs[:, d * 4 + 1, 0:n], kd, kd[:, 0:n], E2, E2[:, 0:n], ALU.mult)
                        self.tt('dve', fs, fs[:, d * 4 + 2, 0:n], bd, bd[:, 0:n], E2, E2[:, 0:n], ALU.mult)
                        self.stt(fs, fs[:, d * 4 + 3, 0:n], kk, kk[:, 0:n], -1.0, E3, E3[:, 0:n], ALU.mult, ALU.mult)
                        ke, be = kb_r.next(), kb_r.next()
                        self.tt('dve', ke, ke[:, 0:n], kd, kd[:, 0:n], E4, E4[:, 0:n], ALU.mult)
                        self.tt('dve', be, be[:, 0:n], bd, bd[:, 0:n], E4, E4[:, 0:n], ALU.mult)
                        self.chk('rp_fs')
                        for ti, src in ((2 * d, ke), (2 * d + 1, be)):
                            pt = pt_r.next()
                            for tb in range(nb):
                                s.op('pe', lambda e: e.transpose(pt[:, tb, :], src[:, tb * 128:(tb + 1) * 128], self.ident[:]),
                                     [src, self.ident], [pt])
                            self.cp('act', tm, tm[:, ti, 0:nb, cs_], pt, pt[:, 0:nb, :])
                        if d == 0:
                            ksum = tmp_r.next()
                            self.cp('dve', ksum, ksum[:, 0:n], kd, kd[:, 0:n])
                        else:
                            self.tt('dve', ksum, ksum[:, 0:n], ksum, ksum[:, 0:n], kd, kd[:, 0:n], ALU.add)
                    pt = pt_r.next()
                    for tb in range(nb):
                        s.op('pe', lambda e: e.transpose(pt[:, tb, :], vbf[:, tb * 128:(tb + 1) * 128], self.ident[:]),
                             [vbf, self.ident], [pt])
                    self.cp('act', tm, tm[:, 4, 0:nb, cs_], pt, pt[:, 0:nb, :])
                    sq2 = sq_r.next()
                    self.stt(sq2, sq2[:, 0:n], ksum, ksum[:, 0:n], pp[:, 6, c:c + 1], rsb, rsb[:, 0:n], ALU.mult, ALU.mult, extra=[pp])
                    pb = pp_r.next()
                    self.mm(pb, pb[:, 0:n], self.onesblk, self.onesblk[:], sq2, sq2[:, 0:n])
                    os_ = ost.next()
                    self.tt('dve', os_, os_[:, 0, 0:n], vsb, vsb[:, 0:n], pb, pb[:, 0:n], ALU.mult)
                    pg = pp_r.next()
                    self.mm(pg, pg[:, 0:n], g2a, g2a[:, 0, cs_], lo, lo[:, 2, 0:n], start=True, stop=False)
                    self.mm(pg, pg[:, 0:n], g2b, g2b[:, 0, cs_], lo, lo[0:32, 3, 0:n], start=False, stop=True)
                    self.cp('act', os_, os_[:, 1, 0:n], pg, pg[:, 0:n])
                    self.chk('rp_c0')
                    for d in range(2):
                        self.st(self.rw_fm, self.rw_fm[s_, d, :, c * 128:(c + 1) * 128, g0:g0 + n].rearrange("k p t -> p k t"),
                                fs, fs[:, d * 4:(d + 1) * 4, 0:n], key=d)
                    self.st(self.rw_bg, self.rw_bg[s_, :, c * 128:(c + 1) * 128, g0:g0 + n].rearrange("k p t -> p k t"), os_, os_[:, :, 0:n])
                    self.st(self.rw_gc, self.rw_gc[s_, :, c * 128:(c + 1) * 128, g0 // 64:g0 // 64 + nch].rearrange("d p c -> p d c"),
                            gc, gc[:, :, 0:nch])
                for ti in range(5):
                    self.st(self.rw_tm, self.rw_tm[s_, ti, g0:g0 + n, :].rearrange("(tb p) f -> p tb f", p=128), tm, tm[:, ti, 0:nb, :], key=ti)
        ph.close()

    def dplr_scan(self, spec):
        cfg = self.cfg
        s = self.s
        H, Kd, Vd = spec['H'], spec['Kd'], spec['Vd']
        NCC, NCL = cfg.TC // 64, cfg.TL // 64
        NC = NCC + NCL
        ph = Phase(s)
        kinds = spec['kinds']
        nk = spec['nfm']
        ki = spec['ki']
        ntm = spec['ntm']
        ti_ = spec['ti']
        chains = spec['chains']
        nchain = len(chains)
        masks = ph.sb('masks', [64, 2, 128], F32)
        self.ld(masks, masks[:], self.inp['c_scanmask'], self.inp['c_scanmask'][:, :, :])
        identf = ph.sb('identf', [64, 64], F32)
        self.ld(identf, identf[:], self.inp['c_ident64'], self.inp['c_ident64'][:, :])
        fm_r = [ph.ring(f'fm{i}', [Kd, nk, H, 64], BF16, 2) for i in range(nchain)]
        tm_r = [ph.ring(f'tm{i}', [64, ntm, H * max(Kd, Vd)], BF16, 2) for i in range(nchain)]
        dm_r = [ph.ring(f'dm{i}', [64, H, 128], BF16, 2) for i in range(nchain)] if spec.get('dmat') else None
        gc_t = [ph.sb(f'gc{i}', [Kd, H, NC], F32) for i in range(nchain)]
        S_t = [ph.sb(f'S{i}', [Kd, H, Vd], F32) for i in range(nchain)]
        Sb_t = [ph.sb(f'Sb{i}', [Kd, H, Vd], BF16) for i in range(nchain)]
        M_r = [ph.ring(f'Msb{i}', [64, H, 2, 128], BF16, 1) for i in range(nchain)]
        sq_r = [ph.ring(f'sqm{i}', [64, H, 64], F32, 5) for i in range(nchain)]
        X_r = [ph.ring(f'X{i}', [64, H, 64], F32, 5) for i in range(nchain)]
        off_r = [ph.ring(f'off{i}', [64, H, 64], F32, 1) for i in range(nchain)]
        bm = ph.sb('bm', [64, 2, 64], F32)
        self.ld(bm, bm[:], self.inp['c_blkmask'], self.inp['c_blkmask'][:, :, :])
        Xf_r = [ph.ring(f'Xf{i}', [64, H, 64], BF16, 2) for i in range(nchain)]
        wt_r = ph.ring('wt', [64, H, Vd], BF16, nchain)
        u_r = ph.ring('u', [64, H, Vd], BF16, nchain)
        y_r = ph.ring('y', [Vd, H, 64], F32, 2)
        pM = ph.ring('pM', [64, 2, 2, 128], F32, 2, psum=True)
        pG = ph.ring('pG', [128, 512], F32, 6, psum=True)
        hv = lambda p: p[0:64, 0:H * 64].rearrange("p (h t) -> p h t", h=H)
        wv = lambda p: p[0:64, 0:H * Vd].rearrange("p (h t) -> p h t", h=H)
        yv = lambda p: p[0:Vd, 0:H * 64].rearrange("p (h t) -> p h t", h=H)
        sv = lambda p: p[0:Kd, 0:H * Vd].rearrange("p (h t) -> p h t", h=H)
        for i, (s_, d) in enumerate(chains):
            spec['load_gc'](gc_t[i], s_, d)
            s.op('dve', lambda e: e.memset(S_t[i][:], 0.0), [], [S_t[i]])
            s.op('dve', lambda e: e.memset(Sb_t[i][:], 0.0), [], [Sb_t[i]])
        order = {0: list(range(NC)), 1: list(range(NCC - 1, -1, -1)) + list(range(NC - 1, NCC - 1, -1))}
        for step in range(NC):
            cur = []
            for i, (s_, d) in enumerate(chains):
                ci = order[d][step]
                fm = fm_r[i].next()
                tm = tm_r[i].next()
                spec['load_fm'](fm, s_, d, ci)
                spec['load_tm'](tm, s_, d, ci)
                dm = None
                if dm_r is not None:
                    dm = dm_r[i].next()
                    spec['load_dm'](dm, s_, d, ci)
                cur.append((i, s_, d, ci, fm, tm, dm))
            self.fl()
            st = {}
            def stage_a(i, s_, d, ci, fm, tm, dm):
                Msb = M_r[i].next()
                AR = lambda h: fm[:, ki['A']:ki['A'] + 2, h, :]
                for h0 in range(0, H, 2):
                    p = pM.next()
                    for hh in range(2):
                        h = h0 + hh
                        self.mm(p, p[:, hh, 0, :].rearrange("p (a t) -> p a t", a=2), fm, fm[:, ki['B'], h, :], fm, AR(h))
                        self.mm(p, p[:, hh, 1, :].rearrange("p (a t) -> p a t", a=2), fm, fm[:, ki['K'], h, :], fm, AR(h))
                    if dm is None:
                        self.tt('dve', Msb, Msb[:, h0:h0 + 2, :, :], p, p[:, :, :, :], masks,
                                masks[:, d, :].unsqueeze(1).unsqueeze(1).to_broadcast([64, 2, 2, 128]), ALU.mult)
                    else:
                        self.tt('dve', Msb, Msb[:, h0:h0 + 2, :, :], p, p[:, :, :, :], dm,
                                dm[:, h0:h0 + 2, :].unsqueeze(2).to_broadcast([64, 2, 2, 128]), ALU.mult)
                yield
                bmb = bm[:, 0, :].unsqueeze(1).to_broadcast([64, H, 64])
                bmcb = bm[:, 1, :].unsqueeze(1).to_broadcast([64, H, 64])
                idb = identf[:].unsqueeze(1).to_broadcast([64, H, 64])
                P0 = sq_r[i].next()
                self.cp('act', P0, P0[:], Msb, Msb[:, :, 0, 0:64])
                pt = pG.next()
                for h in range(H):
                    self.mm(pt, hv(pt)[:, h, :], P0, P0[:, h, :], identf, identf[:, :])
                PT = sq_r[i].next()
                self.tt('dve', PT, PT[:], pt, hv(pt), bm, bmb, ALU.mult)
                PToff = off_r[i].next()
                self.tt('dve', PToff, PToff[:], pt, hv(pt), bm, bmcb, ALU.mult)
                P = sq_r[i].next()
                self.tt('dve', P, P[:], P0, P0[:], bm, bmb, ALU.mult)
                X = X_r[i].next()
                self.tt('dve', X, X[:], P, P[:], identf, idb, ALU.add)
                XT = X_r[i].next()
                self.tt('dve', XT, XT[:], PT, PT[:], identf, idb, ALU.add)
                yield
                for lev in range(1, 5):
                    last = lev == 4
                    if not last:
                        p2 = pG.next()
                        p2v = hv(p2)
                        for h in range(H):
                            self.mm(p2, p2v[:, h, :], PT, PT[:, h, :], P, P[:, h, :])
                    p2t = pG.next()
                    p2tv = hv(p2t)
                    for h in range(H):
                        self.mm(p2t, p2tv[:, h, :], P, P[:, h, :], PT, PT[:, h, :])
                    nPT = sq_r[i].next()
                    self.cp('act', nPT, nPT[:], p2t, p2tv)
                    if not last:
                        nP = sq_r[i].next()
                        self.cp('act', nP, nP[:], p2, p2v)
                        P = nP
                    PT = nPT
                    px = pG.next()
                    pxv = hv(px)
                    for h in range(H):
                        self.mm(px, pxv[:, h, :], PT, PT[:, h, :], X, X[:, h, :])
                    pxt = pG.next()
                    pxtv = hv(pxt)
                    for h in range(H):
                        self.mm(pxt, pxtv[:, h, :], X, X[:, h, :], PT, PT[:, h, :])
                    nX = X_r[i].next()
                    self.tt('dve', nX, nX[:], X, X[:], px, pxv, ALU.add)
                    nXT = X_r[i].next()
                    self.tt('dve', nXT, nXT[:], XT, XT[:], pxt, pxtv, ALU.add)
                    X, XT = nX, nXT
                    yield
                p1 = pG.next()
                for h in range(H):
                    self.mm(p1, hv(p1)[:, h, :], PToff, PToff[:, h, :], X, X[:, h, :])
                T1 = X_r[i].next()
                self.cp('act', T1, T1[:], p1, hv(p1))
                yield
                p2_ = pG.next()
                for h in range(H):
                    self.mm(p2_, hv(p2_)[:, h, :], XT, XT[:, h, :], T1, T1[:, h, :])
                Xf = Xf_r[i].next()
                self.tt('dve', Xf, Xf[:], X, X[:], p2_, hv(p2_), ALU.add)
                X = Xf
                st[i] = (Msb, X)
                if step == 0 and i == 0 and 'dbg_scan' in cfg.debug:
                    self.st(self.dbgT, self.dbgT[:, 0, 0:H * 256], Msb, Msb[:].rearrange("p h a t -> p (h a t)"))
                    self.st(self.dbgT, self.dbgT[:, 1, 0:H * 64], X, X[:].rearrange("p h t -> p (h t)"))
                    self.st(self.dbgT, self.dbgT[:, 4, 0:H * 64], PT0, PT0[:].rearrange("p h t -> p (h t)"))
            gens = [stage_a(*c) for c in cur]
            alive = list(gens)
            while alive:
                nxt = []
                for g_ in alive:
                    try:
                        next(g_)
                        nxt.append(g_)
                    except StopIteration:
                        pass
                alive = nxt
            wts = {}
            for (i, s_, d, ci, fm, tm, dm) in cur:
                Msb, X = st[i]
                p = pG.next()
                for h in range(H):
                    self.mm(p, wv(p)[:, h, :], fm, fm[:, ki['As'], h, :], Sb_t[i], Sb_t[i][:, h, :], start=True, stop=False)
                    self.mm(p, wv(p)[:, h, :], Msb, Msb[:, h, 1, 0:64], tm, tm[:, ti_['V'], h * Vd:(h + 1) * Vd], start=False, stop=True)
                wt = wt_r.next()
                self.cp('act', wt, wt[:], p, wv(p))
                wts[i] = wt
                if step == 0 and i == 0 and 'dbg_scan' in cfg.debug:
                    self.st(self.dbgT, self.dbgT[:, 2, 0:H * Vd], wt, wt[:].rearrange("p h t -> p (h t)"))
            us = {}
            for (i, s_, d, ci, fm, tm, dm) in cur:
                Msb, X = st[i]
                p = pG.next()
                for h in range(H):
                    self.mm(p, wv(p)[:, h, :], X, X[:, h, :], wts[i], wts[i][:, h, :])
                u = u_r.next()
                self.cp('dve', u, u[:], p, wv(p))
                us[i] = u
                if step == 0 and i == 0 and 'dbg_scan' in cfg.debug:
                    self.st(self.dbgT, self.dbgT[:, 3, 0:H * Vd], u, u[:].rearrange("p h t -> p (h t)"))
            for (i, s_, d, ci, fm, tm, dm) in cur:
                Msb, X = st[i]
                u = us[i]
                p = pG.next()
                for h in range(H):
                    V_h = tm[:, ti_['V'], h * Vd:(h + 1) * Vd]
                    self.mm(p, yv(p)[:, h, :], Sb_t[i], Sb_t[i][:, h, :], fm, fm[:, ki['Rs'], h, :], start=True, stop=False)
                    self.mm(p, yv(p)[:, h, :], u, u[:, h, :], Msb, Msb[:, h, 0, 64:128], start=False, stop=False)
                    self.mm(p, yv(p)[:, h, :], tm, V_h, Msb, Msb[:, h, 1, 64:128], start=False, stop=True)
                y = y_r.next()
                self.cp('act', y, y[:], p, yv(p))
                spec['store_y'](y, s_, d, ci)
                p = pG.next()
                for h in range(H):
                    V_h = tm[:, ti_['V'], h * Vd:(h + 1) * Vd]
                    self.mm(p, sv(p)[:, h, :], tm, tm[:, ti_['Bend'], h * Kd:(h + 1) * Kd], u, u[:, h, :], start=True, stop=False)
                    self.mm(p, sv(p)[:, h, :], tm, tm[:, ti_['Kend'], h * Kd:(h + 1) * Kd], tm, V_h, start=False, stop=True)
                S = S_t[i]
                self.tt('dve', S, S[:], S, S[:], gc_t[i], gc_t[i][:, :, ci:ci + 1].to_broadcast([Kd, H, Vd]), ALU.mult)
                self.tt('dve', S, S[:], S, S[:], p, sv(p), ALU.add)
                self.cp('act', Sb_t[i], Sb_t[i][:], S, S[:])
        ph.close()

    def phase_rwkv_scan(self, l):
        cfg = self.cfg
        S = cfg.S

        def load_fm(fm, s_, d, ci):
            for k in range(4):
                self.ld(fm, fm[:, k, :, :], self.rw_fm, self.rw_fm[s_, d, k, :, ci * 64:(ci + 1) * 64].rearrange("(h k) t -> k h t", k=64), key=k)

        def load_tm(tm, s_, d, ci):
            for j, ti in enumerate((2 * d, 2 * d + 1, 4)):
                self.ld(tm, tm[:, j, :], self.rw_tm, self.rw_tm[s_, ti, ci * 64:(ci + 1) * 64, :], key=j)

        def load_gc(gc, s_, d):
            self.ld(gc, gc[:], self.rw_gc, self.rw_gc[s_, d, :, :].rearrange("(h k) c -> k h c", k=64))

        def store_y(y, s_, d, ci):
            self.st(self.rw_y, self.rw_y[s_, d, :, ci * 64:(ci + 1) * 64].rearrange("(h v) t -> v h t", v=64), y, y[:])

        def load_fm2(fm, s_, d, ci):
            for j, k in enumerate((1, 2, 3, 0)):
                self.ld(fm, fm[:, j, :, :], self.rw_fm, self.rw_fm[s_, d, k, :, ci * 64:(ci + 1) * 64].rearrange("(h k) t -> k h t", k=64), key=j)
        spec = dict(H=8, Kd=64, Vd=64, kinds=None, nfm=4, ki=dict(K=0, B=1, A=2, R=3, As=2, Rs=3), ntm=3,
                    ti=dict(Kend=0, Bend=1, V=2), chains=[(s_, d) for s_ in range(S) for d in range(2)],
                    load_fm=load_fm2, load_tm=load_tm, load_gc=load_gc, store_y=store_y)
        self.dplr_scan(spec)

    def phase_rwkv_post(self, l):
        cfg = self.cfg
        S = cfg.S
        ph = Phase(self.s)
        pp = ph.sb('pp', [128, 9, 4], F32)
        self.ld(pp, pp[:], self.inp['rwkv_pT'], self.inp['rwkv_pT'][:, l, :, :])
        lneps = ph.sb('lneps', [128, 1], F32)
        self.s.op('dve', lambda e: e.memset(lneps[:], 64e-5), [], [lneps])
        yr = ph.ring('yy', [128, 2, 4, 512], F32, 2)
        bgr = ph.ring('bg', [128, 2, 4, 512], BF16, 2)
        osr = ph.ring('os', [128, 4, 512], BF16, 2)
        o_r = ph.ring('o', [128, 512], F32, 2)
        ob_r = ph.ring('ob', [128, 512], BF16, 2)
        t_r = ph.ring('t', [128, 512], F32, 4)
        pm = ph.ring('pm', [128, 512], F32, 2, psum=True)
        pv = ph.ring('pv', [128, 512], F32, 2, psum=True)
        for s_ in range(S):
            for (seg, t0, n, g0) in cfg.tiles(512):
                if seg == 0 and l == cfg.L - 1:
                    continue
                yy = yr.next()
                for d in range(2):
                    self.ld(yy, yy[:, d, :, 0:n], self.rw_y, self.rw_y[s_, d, :, g0:g0 + n].rearrange("(c p) t -> p c t", p=128), key=d)
                bg = bgr.next()
                for k in range(2):
                    self.ld(bg, bg[:, k, :, 0:n], self.rw_bg, self.rw_bg[s_, k, :, g0:g0 + n].rearrange("(c p) t -> p c t", p=128), key=k)
                self.fl()
                os_ = osr.next()
                for c in range(4):
                    o = o_r.next()
                    self.tt('dve', o, o[:, 0:n], yy, yy[:, 0, c, 0:n], yy, yy[:, 1, c, 0:n], ALU.add)
                    ob = ob_r.next()
                    self.cp('act', ob, ob[:, 0:n], o, o[:, 0:n])
                    p = pm.next()
                    self.mm(p, p[:, 0:n], self.onesblk, self.onesblk[:], ob, ob[:, 0:n])
                    mt = t_r.next()
                    self.act(mt, mt[:, 0:n], p, p[:, 0:n], AF.Copy, scale=-1.0 / 64)
                    self.tt('dve', o, o[:, 0:n], o, o[:, 0:n], mt, mt[:, 0:n], ALU.add)
                    sq = ob_r.next()
                    self.act(sq, sq[:, 0:n], o, o[:, 0:n], AF.Square)
                    p2 = pv.next()
                    self.mm(p2, p2[:, 0:n], self.onesblk, self.onesblk[:], sq, sq[:, 0:n])
                    t1 = t_r.next()
                    self.act(t1, t1[:, 0:n], p2, p2[:, 0:n], AF.Ln, bias=lneps[:, 0:1], scale=1.0 / 64, extra=[lneps])
                    t2 = t_r.next()
                    self.act(t2, t2[:, 0:n], t1, t1[:, 0:n], AF.Exp, scale=-0.5)
                    self.stt(o, o[:, 0:n], o, o[:, 0:n], pp[:, 7, c:c + 1], t2, t2[:, 0:n], ALU.mult, ALU.mult, extra=[pp])
                    self.stt(o, o[:, 0:n], o, o[:, 0:n], pp[:, 8, c:c + 1], bg, bg[:, 0, c, 0:n], ALU.add, ALU.add, extra=[pp])
                    self.tt('dve', os_, os_[:, c, 0:n], o, o[:, 0:n], bg, bg[:, 1, c, 0:n], ALU.mult)
                self.st(self.yaT, self.yaT[s_, :, g0:g0 + n].rearrange("(c p) t -> p c t", p=128), os_, os_[:, :, 0:n])
        ph.close()

    def phase_gdn_proj(self, l):
        cfg = self.cfg
        S = cfg.S
        s = self.s
        ph = Phase(s)
        self.wstage = ph.ring('wstg', [128, 8, 256], F32, 2)
        r = lambda ap: ap.rearrange("(kc p) c -> p kc c", p=128)
        win = self.inp['w_in']
        wu = ph.sb('wu', [128, 8, 2048], BF16)
        self.load_w(ph, wu, lambda a, b: wu[:, :, a:b], win, lambda a, b: r(win[l, :, 1536 + a:1536 + b]), 2048, 8, piece=256)
        wab = ph.sb('wab', [128, 8, 16], BF16)
        for j, (nm, d) in enumerate((('gdn_w_alpha', 0), ('gdn_w_alpha', 1), ('gdn_w_beta', 0), ('gdn_w_beta', 1))):
            src = self.inp[nm]
            self.load_w(ph, wab, lambda a, b, j=j: wab[:, :, 4 * j + a:4 * j + b], src, lambda a, b, src=src, d=d: r(src[l, d, :, a:b]), 4, 8, piece=256)
        hr = ph.ring('hb', [128, 8, 512], BF16, 2)
        ust = ph.ring('ust', [128, 16, 512], BF16, 2)
        abr = ph.ring('ab', [16, 512], F32, 2)
        pu = ph.ring('pu', [128, 512], F32, 4, psum=True)
        for s_ in range(S):
            for (seg, t0, n, g0) in cfg.tiles(512):
                hb = hr.next()
                c0 = cfg.pcol(seg, t0, 2)
                self.ld(hb, hb[:, :, 0:n], self.hT, self.hT[s_, :, c0:c0 + n].rearrange("(kc p) t -> p kc t", p=128))
                self.fl()
                us = ust.next()
                for cc in range(16):
                    p = pu.next()
                    for kc in range(8):
                        self.mm(p, p[:, 0:n], wu, wu[:, kc, cc * 128:(cc + 1) * 128], hb, hb[:, kc, 0:n], start=(kc == 0), stop=(kc == 7))
                    if cc < 12:
                        self.cp('act' if cc % 2 else 'dve', us, us[:, cc, 0:n], p, p[:, 0:n])
                    else:
                        self.act(us, us[:, cc, 0:n], p, p[:, 0:n], AF.Silu)
                p = pu.next()
                for kc in range(8):
                    self.mm(p, p[0:16, 0:n], wab, wab[:, kc, :], hb, hb[:, kc, 0:n], start=(kc == 0), stop=(kc == 7))
                ab = abr.next()
                self.cp('act', ab, ab[:, 0:n], p, p[0:16, 0:n])
                self.st(self.gd_u, self.gd_u[s_, :, c0:c0 + n].rearrange("(c p) t -> p c t", p=128), us, us[:, 0:12, 0:n], key=0)
                self.st(self.gd_z, self.gd_z[s_, :, g0:g0 + n].rearrange("(c p) t -> p c t", p=128), us, us[:, 12:16, 0:n], key=1)
                self.st(self.gd_ab, self.gd_ab[s_, :, g0:g0 + n], ab, ab[:, 0:n])
        ph.close()

    def phase_gdn_prep(self, l):
        cfg = self.cfg
        S = cfg.S
        s = self.s
        ph = Phase(s)
        cw = ph.sb('cw', [128, 12, 5], F32)
        self.ld(cw, cw[:], self.inp['gdn_convT'], self.inp['gdn_convT'][:, l, :, :])
        dg = ph.sb('dg', [128, 12, 5, 128], BF16)
        for cc in range(12):
            for j in range(5):
                self.ts('dve', dg, dg[:, cc, j, :], self.ident, self.ident[:], cw[:, cc, j:j + 1], extra=[cw])
        rowp = ph.sb('rowp', [16, 4], F32)
        self.ld(rowp, rowp[:, 0:3], self.inp['gdn_rowp'], self.inp['gdn_rowp'][:, l, :])
        self.act(rowp, rowp[:, 3:4], rowp, rowp[:, 1:2], AF.Exp)
        self.ts('dve', rowp, rowp[:, 3:4], rowp, rowp[:, 3:4], -1.0)
        selb = ph.sb('selb', [16, 16, 128], BF16)
        self.ld(selb, selb[:], self.inp['c_selb'], self.inp['c_selb'][:, :, :])
        self32 = ph.sb('self32', [16, 16, 64], F32)
        self.ld(self32, self32[:], self.inp['c_self'], self.inp['c_self'][:, :, :])
        id16 = ph.sb('id16', [16, 16], F32)
        self.ld(id16, id16[:], self.inp['c_ident64'], self.inp['c_ident64'][0:16, 0:16])
        nmask = ph.sb('nmask', [64, 2, 2, 64], F32)
        self.ld(nmask, nmask[:], self.inp['c_negmask'], self.inp['c_negmask'][:, :, :, :])
        cm16 = ph.sb('cm16', [16, 512], F32)
        self.ld(cm16, cm16[:], self.inp['c_chunkmask'], self.inp['c_chunkmask'][0:16, :])
        one16 = ph.sb('one16', [16, 1], F32)
        s.op('dve', lambda e: e.memset(one16[:], 1.0), [], [one16])
        ur = ph.ring('ub', [128, 12, 516], BF16, 2)
        abr = ph.ring('ab', [16, 512], F32, 2)
        R16 = lambda nm, k=1: ph.ring(nm, [16, 512], F32, k)
        e_r, g_r, G_r, Te_r, Hi_r, Gi_r, ga_r, ee_r, be_r = (R16('e'), R16('g'), R16('G'), R16('Te'), R16('Hi'), R16('Gi'),
                                                            R16('ga'), R16('ee'), R16('be'))
        rb_r = ph.ring('rb', [16, 2, 512], BF16, 1)
        gcr = ph.ring('gcr', [16, 8], F32, 2)
        grep_r = ph.ring('grep', [64, 8, 512], F32, 1)
        gcol_r = ph.ring('gcol', [64, 8, 16], F32, 1)
        tcol_r = ph.ring('tcol', [128, 2, 4, 16], F32, 1)
        dd_r = ph.ring('dd', [64, 4, 2, 64], F32, 2)
        dm_r = ph.ring('dmo', [64, 4, 2, 64], BF16, 3)
        qkv_r = ph.ring('qkv', [128, 512], F32, 3)
        kn_r = ph.ring('kn', [128, 512], BF16, 2)
        qn_r = ph.ring('qn', [128, 512], BF16, 2)
        vb_r = ph.ring('vb', [128, 512], BF16, 2)
        sq_r = ph.ring('sq', [128, 512], BF16, 2)
        t_r = ph.ring('t', [128, 512], F32, 3)
        rep_r = ph.ring('rep', [128, 2, 512], F32, 2)
        fst = ph.ring('fst', [128, 8, 512], BF16, 2)
        tms = ph.ring('tms', [128, 4, 4, 512], BF16, 1)
        pc = ph.ring('pc', [128, 512], F32, 5, psum=True)
        ptp = ph.ring('ptp', [128, 4, 128], BF16, 2, psum=True)
        for s_ in range(S):
            for (seg, t0, n, g0) in cfg.tiles(512):
                nb, nch = n // 128, n // 64
                ub = ur.next()
                c0 = cfg.pcol(seg, t0, 2)
                self.ld(ub, ub[:, :, 0:n + 4], self.gd_u, self.gd_u[s_, :, c0 - 2:c0 + n + 2].rearrange("(c p) t -> p c t", p=128))
                ab = abr.next()
                self.ld(ab, ab[:, 0:n], self.gd_ab, self.gd_ab[s_, :, g0:g0 + n])
                self.fl()
                e, g, G, Te, Hi, Gi, ga, ee, be = (x.next() for x in (e_r, g_r, G_r, Te_r, Hi_r, Gi_r, ga_r, ee_r, be_r))
                self.act(e, e[:, 0:n], ab, ab[:, 0:n], AF.Exp, bias=rowp[:, 0:1], extra=[rowp])
                self.act(g, g[:, 0:n], e, e[:, 0:n], AF.Ln, bias=one16[:, 0:1], extra=[one16])
                self.ts('dve', g, g[:, 0:n], g, g[:, 0:n], rowp[:, 3:4], extra=[rowp])
                self.act(be, be[:, 0:n], ab, ab[:, 0:n], AF.Sigmoid)
                s.op('dve', lambda e_: e_.tensor_tensor_scan(out=G[:, 0:n], data0=cm16[:, 0:n], data1=g[:, 0:n], initial=0.0,
                                                              op0=ALU.mult, op1=ALU.add), [cm16, g], [G])
                c3 = lambda t: t[:, 0:n].rearrange("p (c t) -> p c t", t=64)
                tot_b = c3(G)[:, :, 63:64].to_broadcast([16, nch, 64])
                self.tt('dve', Te, c3(Te), G, tot_b, G, c3(G), ALU.subtract)
                self.tt('dve', Hi, Hi[:, 0:n], Te, Te[:, 0:n], g, g[:, 0:n], ALU.add)
                self.tt('dve', Hi, Hi[:, 0:n], Hi, Hi[:, 0:n], G, G[:, 0:n], ALU.subtract)
                self.stt(Gi, Gi[:, 0:n], Hi, Hi[:, 0:n], rowp[:, 2:3], G, G[:, 0:n], ALU.mult, ALU.add, extra=[rowp])
                self.tt('dve', Te, c3(Te), G, tot_b, Gi, c3(Gi), ALU.subtract)
                self.act(ga, ga[:, 0:n], Gi, Gi[:, 0:n], AF.Exp)
                self.act(ee, ee[:, 0:n], Te, Te[:, 0:n], AF.Exp)
                gc = gcr.next()
                self.act(gc, gc[:, 0:nch], G, c3(G)[:, :, 63], AF.Exp)
                self.st(self.gd_gc, self.gd_gc[s_, :, g0 // 64:g0 // 64 + nch], gc, gc[:, 0:nch])
                rb = rb_r.next()
                self.cp('dve', rb, rb[:, 0, 0:n], ga, ga[:, 0:n])
                self.cp('dve', rb, rb[:, 1, 0:n], be, be[:, 0:n])
                grep = grep_r.next()
                for rr_ in range(8):
                    p = pc.next()
                    self.mm(p, p[0:64, 0:n], self32, self32[:, rr_, :], Gi, Gi[:, 0:n])
                    self.cp('act' if rr_ % 2 else 'dve', grep, grep[:, rr_, 0:n], p, p[0:64, 0:n])
                gcol = gcol_r.next()
                p = pc.next()
                for ch in range(nch):
                    self.mm(p, p[0:64, ch * 16:(ch + 1) * 16], Gi, Gi[:, ch * 64:(ch + 1) * 64], id16, id16[:, :])
                self.cp('act', gcol, gcol[:, 0:nch, :], p, p[0:64, 0:nch * 16].rearrange("p (c r) -> p c r", r=16))
                tcol = tcol_r.next()
                p = pc.next()
                for k_, src in enumerate((ee, be)):
                    for tb in range(nb):
                        self.mm(p, p[:, (k_ * 4 + tb) * 16:(k_ * 4 + tb + 1) * 16], src, src[:, tb * 128:(tb + 1) * 128], id16, id16[:, :])
                for k_ in range(2):
                    self.cp('act', tcol, tcol[:, k_, 0:nb, :], p, p[:, k_ * 64:k_ * 64 + nb * 16].rearrange("p (b r) -> p b r", r=16))
                for d in range(2):
                    for ch in range(nch):
                        dd = dd_r.next()
                        gsl = grep[:, d * 4:(d + 1) * 4, ch * 64:(ch + 1) * 64]
                        gcb = gcol[:, ch, d * 4:(d + 1) * 4].unsqueeze(2).to_broadcast([64, 4, 64])
                        self.tt('dve', dd, dd[:, :, 0, :], grep, gsl, gcol, gcb, ALU.subtract)
                        self.tt('dve', dd, dd[:, :, 1, :], dd, dd[:, :, 0, :], nmask,
                                nmask[:, d, 1, :].unsqueeze(1).to_broadcast([64, 4, 64]), ALU.add)
                        self.tt('dve', dd, dd[:, :, 0, :], dd, dd[:, :, 0, :], nmask,
                                nmask[:, d, 0, :].unsqueeze(1).to_broadcast([64, 4, 64]), ALU.add)
                        dm = dm_r.next()
                        self.act(dm, dm[:], dd, dd[:], AF.Exp)
                        self.st(self.gd_dm, self.gd_dm[s_, d, g0 // 64 + ch, :, :], dm, dm[:].rearrange("p h a t -> p (h a t)"))
                tm = tms.next()
                for h in range(4):
                    fs = fst.next()
                    outs3 = []
                    for grp in range(3):
                        cc = grp * 4 + h
                        p = pc.next()
                        for j in range(5):
                            self.mm(p, p[:, 0:n], dg, dg[:, cc, j, :], ub, ub[:, cc, j:j + n], start=(j == 0), stop=(j == 4))
                        o = qkv_r.next()
                        self.act(o, o[:, 0:n], p, p[:, 0:n], AF.Silu)
                        outs3.append(o)
                    q_, k_, v_ = outs3
                    kn, qn, vb = kn_r.next(), qn_r.next(), vb_r.next()
                    for src, dst, scl in ((q_, qn, 128 ** -0.5), (k_, kn, 1.0)):
                        sq = sq_r.next()
                        self.act(sq, sq[:, 0:n], src, src[:, 0:n], AF.Square)
                        p = pc.next()
                        self.mm(p, p[:, 0:n], self.ones128, self.ones128[:], sq, sq[:, 0:n])
                        t1 = t_r.next()
                        self.act(t1, t1[:, 0:n], p, p[:, 0:n], AF.Ln, bias=self.epsc[:, 0:1], extra=[self.epsc])
                        t2 = t_r.next()
                        self.act(t2, t2[:, 0:n], t1, t1[:, 0:n], AF.Exp, scale=-0.5)
                        self.stt(dst, dst[:, 0:n], src, src[:, 0:n], scl, t2, t2[:, 0:n], ALU.mult, ALU.mult)
                    self.cp('dve', vb, vb[:, 0:n], v_, v_[:, 0:n])
                    self.cp('act', fs, fs[:, 0, 0:n], kn, kn[:, 0:n])
                    self.cp('act', fs, fs[:, 1, 0:n], qn, qn[:, 0:n])
                    for d in range(2):
                        rep = rep_r.next()
                        for k2, (slot, row) in enumerate(((1, 8 + 4 * d + h), (0, 4 * d + h))):
                            p = pc.next()
                            self.mm(p, p[:, 0:n], selb, selb[:, row, :], rb, rb[:, slot, 0:n])
                            self.cp('act', rep, rep[:, k2, 0:n], p, p[:, 0:n])
                        b0 = 2 + 3 * d
                        self.stt(fs, fs[:, b0, 0:n], kn, kn[:, 0:n], -1.0, rep, rep[:, 0, 0:n], ALU.mult, ALU.mult)
                        self.tt('dve', fs, fs[:, b0 + 1, 0:n], fs, fs[:, b0, 0:n], rep, rep[:, 1, 0:n], ALU.mult)
                        self.tt('dve', fs, fs[:, b0 + 2, 0:n], qn, qn[:, 0:n], rep, rep[:, 1, 0:n], ALU.mult)
                    self.st(self.gd_fm, self.gd_fm[s_, :, h * 128:(h + 1) * 128, g0:g0 + n].rearrange("k p t -> p k t"), fs, fs[:, :, 0:n])
                    for src, base, slot in ((kn, 0, 0), (vb, 1, 1)):
                        pt = ptp.next()
                        for tb in range(nb):
                            s.op('pe', lambda e_: e_.transpose(pt[:, tb, :], src[:, tb * 128:(tb + 1) * 128], self.ident[:]),
                                 [src, self.ident], [pt])
                        for d in range(2):
                            for tb in range(nb):
                                row = (4 * d + h) if slot == 0 else (8 + 4 * d + h)
                                self.act(tm, tm[:, 2 * d + base, tb, h * 128:(h + 1) * 128], pt, pt[:, tb, :], AF.Copy,
                                         scale=tcol[:, slot, tb, row:row + 1], extra=[tcol])
                for ti in range(4):
                    self.st(self.gd_tm, self.gd_tm[s_, ti, g0:g0 + n, :].rearrange("(tb p) f -> p tb f", p=128), tm, tm[:, ti, 0:nb, :], key=ti)
        ph.close()

    def phase_gdn_scan(self, l):
        cfg = self.cfg
        S = cfg.S
        NC = cfg.TT // 64

        def load_fm(fm, s_, d, ci):
            for j, k in enumerate((0, 2 + 3 * d, 1, 3 + 3 * d, 4 + 3 * d)):
                self.ld(fm, fm[:, j, :, :], self.gd_fm, self.gd_fm[s_, k, :, ci * 64:(ci + 1) * 64].rearrange("(h k) t -> k h t", k=128), key=j)

        def load_tm(tm, s_, d, ci):
            for j in range(2):
                self.ld(tm, tm[:, j, :], self.gd_tm, self.gd_tm[s_, 2 * d + j, ci * 64:(ci + 1) * 64, :], key=j)

        def load_dm(dm, s_, d, ci):
            self.ld(dm, dm[:].rearrange("p h t -> p (h t)"), self.gd_dm, self.gd_dm[s_, d, ci, :, :])

        def load_gc(gc, s_, d):
            self.ld(gc, gc[:], self.gd_gc, self.gd_gc[s_:s_ + 1, d * 4:(d + 1) * 4, :].to_broadcast([128, 4, NC]))

        def store_y(y, s_, d, ci):
            self.st(self.gd_y, self.gd_y[s_, d, :, ci * 64:(ci + 1) * 64].rearrange("(h v) t -> v h t", v=128), y, y[:])
        spec = dict(H=4, Kd=128, Vd=128, kinds=None, nfm=5, ki=dict(K=0, B=0, A=1, R=2, As=3, Rs=4), ntm=2,
                    ti=dict(Kend=0, Bend=0, V=1), chains=[(s_, d) for s_ in range(S) for d in range(2)],
                    load_fm=load_fm, load_tm=load_tm, load_gc=load_gc, load_dm=load_dm, store_y=store_y, dmat=True)
        self.dplr_scan(spec)

    def phase_gdn_post(self, l):
        cfg = self.cfg
        S = cfg.S
        ph = Phase(self.s)
        gn = ph.sb('gn', [128, 1], F32)
        self.ld(gn, gn[:], self.inp['gdn_normT'], self.inp['gdn_normT'][l, :, :])
        yr = ph.ring('yy', [128, 2, 4, 512], F32, 2)
        zr = ph.ring('z', [128, 4, 512], BF16, 2)
        osr = ph.ring('os', [128, 4, 512], BF16, 2)
        o_r = ph.ring('o', [128, 512], F32, 2)
        sq_r = ph.ring('sq', [128, 512], BF16, 2)
        t_r = ph.ring('t', [128, 512], F32, 4)
        pm = ph.ring('pm', [128, 512], F32, 2, psum=True)
        for s_ in range(S):
            for (seg, t0, n, g0) in cfg.tiles(512):
                if seg == 0 and l == cfg.L - 1:
                    continue
                yy = yr.next()
                for d in range(2):
                    self.ld(yy, yy[:, d, :, 0:n], self.gd_y, self.gd_y[s_, d, :, g0:g0 + n].rearrange("(c p) t -> p c t", p=128), key=d)
                z = zr.next()
                self.ld(z, z[:, :, 0:n], self.gd_z, self.gd_z[s_, :, g0:g0 + n].rearrange("(c p) t -> p c t", p=128))
                self.fl()
                os_ = osr.next()
                for c in range(4):
                    o = o_r.next()
                    self.tt('dve', o, o[:, 0:n], yy, yy[:, 0, c, 0:n], yy, yy[:, 1, c, 0:n], ALU.add)
                    sq = sq_r.next()
                    self.act(sq, sq[:, 0:n], o, o[:, 0:n], AF.Square)
                    p = pm.next()
                    self.mm(p, p[:, 0:n], self.ones128, self.ones128[:], sq, sq[:, 0:n])
                    t1 = t_r.next()
                    self.act(t1, t1[:, 0:n], p, p[:, 0:n], AF.Ln, bias=self.epsc[:, 0:1], scale=1.0 / 128, extra=[self.epsc])
                    t2 = t_r.next()
                    self.act(t2, t2[:, 0:n], t1, t1[:, 0:n], AF.Exp, scale=-0.5)
                    self.stt(o, o[:, 0:n], o, o[:, 0:n], gn[:, 0:1], t2, t2[:, 0:n], ALU.mult, ALU.mult, extra=[gn])
                    self.tt('dve', os_, os_[:, c, 0:n], o, o[:, 0:n], z, z[:, c, 0:n], ALU.mult)
                self.st(self.ybT, self.ybT[s_, :, g0:g0 + n].rearrange("(c p) t -> p c t", p=128), os_, os_[:, :, 0:n])
        ph.close()

    def phase_merge(self, l):
        cfg = self.cfg
        S = cfg.S
        s = self.s
        ph = Phase(s)
        NT = 256
        self.wstage = ph.ring('wstg', [128, 8, 512], F32, 2)
        r = lambda ap: ap.rearrange("(kc p) c -> p kc c", p=128)
        wg = ph.sb('wg', [128, 8, 3072], BF16)
        wgs = self.inp['w_gate']
        self.load_w(ph, wg, lambda a, b: wg[:, :, a:b], wgs, lambda a, b: r(wgs[l, :, a:b]), 3072, 8)
        wu = []
        for j, nm in enumerate(('w_up_a', 'w_up_b', 'w_up_c')):
            w = ph.sb(nm, [128, 4, 1024], BF16)
            src = self.inp[nm]
            self.load_w(ph, w, lambda a, b, w=w: w[:, :, a:b], src, lambda a, b, src=src: r(src[l, :, a:b]), 1024, 4)
            wu.append(w)
        wo = ph.sb('wo', [128, 8, 1024], BF16)
        wos = self.inp['w_out']
        self.load_w(ph, wo, lambda a, b: wo[:, :, a:b], wos, lambda a, b: r(wos[l, :, a:b]), 1024, 8)
        bg = ph.sb('bg', [128, 24], F32)
        self.ld(bg, bg[:], self.inp['b_gateT'], self.inp['b_gateT'][:, l, :])
        hr = ph.ring('hb', [128, 8, NT], BF16, 2)
        yr = ph.ring('y', [128, 3, 4, NT], BF16, 2)
        xr = ph.ring('x', [128, 8, NT], F32, 2)
        mbr = ph.ring('mb', [128, 8, NT], BF16, 1)
        gtr = ph.ring('gt', [128, NT], F32, 3)
        tmr = ph.ring('tm', [128, NT], F32, 4)
        pg = ph.ring('pg', [128, NT], F32, 3, psum=True)
        pu = ph.ring('pu', [128, NT], F32, 3, psum=True)
        pw = ph.ring('pw', [128, NT], F32, 2, psum=True)
        xs = self.xsrc(l)
        ysrc = (self.yaT, self.ybT, self.ycT)
        for s_ in range(S):
            for (seg, t0, n, g0) in cfg.tiles(NT):
                if seg == 0 and l == cfg.L - 1:
                    continue
                which = S if seg == 0 else s_
                hb = hr.next()
                c0 = cfg.pcol(seg, t0, 2)
                self.ld(hb, hb[:, :, 0:n], self.hT, self.hT[s_, :, c0:c0 + n].rearrange("(kc p) t -> p kc t", p=128))
                y = yr.next()
                for j in range(3):
                    self.ld(y, y[:, j, :, 0:n], ysrc[j], ysrc[j][s_, :, g0:g0 + n].rearrange("(kc p) t -> p kc t", p=128), key=j)
                xt = xr.next()
                self.ld(xt, xt[:, :, 0:n], xs, xs[s_, :, g0:g0 + n].rearrange("(kc p) t -> p kc t", p=128))
                self.fl()
                mb = mbr.next()
                for m in range(8):
                    macc = tmr.next()
                    for j in range(3):
                        p = pg.next()
                        for kc in range(8):
                            self.mm(p, p[:, 0:n], wg, wg[:, kc, j * 1024 + m * 128:j * 1024 + (m + 1) * 128], hb, hb[:, kc, 0:n],
                                    start=(kc == 0), stop=(kc == 7))
                        gt = gtr.next()
                        self.act(gt, gt[:, 0:n], p, p[:, 0:n], AF.Sigmoid, bias=bg[:, j * 8 + m:j * 8 + m + 1], extra=[bg])
                        p2 = pu.next()
                        for kc in range(4):
                            self.mm(p2, p2[:, 0:n], wu[j], wu[j][:, kc, m * 128:(m + 1) * 128], y, y[:, j, kc, 0:n],
                                    start=(kc == 0), stop=(kc == 3))
                        if j == 0:
                            self.tt('dve', macc, macc[:, 0:n], gt, gt[:, 0:n], p2, p2[:, 0:n], ALU.mult)
                        else:
                            t = tmr.next()
                            self.tt('dve', t, t[:, 0:n], gt, gt[:, 0:n], p2, p2[:, 0:n], ALU.mult)
                            if j == 1:
                                self.tt('dve', macc, macc[:, 0:n], macc, macc[:, 0:n], t, t[:, 0:n], ALU.add)
                            else:
                                self.tt('dve', mb, mb[:, m, 0:n], macc, macc[:, 0:n], t, t[:, 0:n], ALU.add)
                for m in range(8):
                    p = pw.next()
                    for kc in range(8):
                        self.mm(p, p[:, 0:n], wo, wo[:, kc, m * 128:(m + 1) * 128], mb, mb[:, kc, 0:n], start=(kc == 0), stop=(kc == 7))
                    t = tmr.next()
                    self.act(t, t[:, 0:n], p, p[:, 0:n], AF.Copy, scale=self.MOD[:, l, 16 + m, which:which + 1], extra=[self.MOD])
                    self.tt('dve', xt, xt[:, m, 0:n], xt, xt[:, m, 0:n], t, t[:, 0:n], ALU.add)
                self.st(self.xmid, self.xmid[s_, :, g0:g0 + n].rearrange("(kc p) t -> p kc t", p=128), xt, xt[:, :, 0:n])
        ph.close()

    def phase_ffn(self, l):
        cfg = self.cfg
        S = cfg.S
        s = self.s
        ph = Phase(s)
        NT = 256
        NF = 22
        self.wstage = ph.ring('wstg', [128, 8, 256], F32, 2)
        r = lambda ap: ap.rearrange("(kc p) c -> p kc c", p=128)
        w1 = ph.sb('w1', [128, 8, 2816], BF16)
        w3 = ph.sb('w3', [128, 8, 2816], BF16)
        w2 = ph.sb('w2', [128, NF, 1024], BF16)
        for w, nm in ((w1, 'ffn_w1'), (w3, 'ffn_w3')):
            src = self.inp[nm]
            self.load_w(ph, w, lambda a, b, w=w: w[:, :, a:b], src, lambda a, b, src=src: r(src[l, :, a:b]), 2816, 8, piece=256)
        src2 = self.inp['ffn_w2']
        for f0 in range(0, NF, 8):
            f1 = min(NF, f0 + 8)
            self.load_w(ph, w2, lambda a, b, f0=f0, f1=f1: w2[:, f0:f1, a:b], src2,
                        lambda a, b, f0=f0, f1=f1: r(src2[l, f0 * 128:f1 * 128, a:b]), 1024, f1 - f0, piece=256)
        xr = ph.ring('x', [128, 8, NT], F32, 2)
        hr = ph.ring('hb', [128, 8, NT], BF16, 1)
        sqr = ph.ring('sq', [128, 8, NT], BF16, 1)
        hid = ph.ring('hid', [128, NF, NT], BF16, 1)
        tmr = ph.ring('tm', [128, NT], F32, 4)
        psr = ph.ring('pss', [128, NT], F32, 1, psum=True)
        p1r = ph.ring('p1', [128, NT], F32, 2, psum=True)
        p3r = ph.ring('p3', [128, NT], F32, 2, psum=True)
        p2r = ph.ring('p2', [128, NT], F32, 2, psum=True)
        for s_ in range(S):
            for (seg, t0, n, g0) in cfg.tiles(NT):
                if seg == 0 and l == cfg.L - 1:
                    continue
                which = S if seg == 0 else s_
                xt = xr.next()
                self.ld(xt, xt[:, :, 0:n], self.xmid, self.xmid[s_, :, g0:g0 + n].rearrange("(kc p) t -> p kc t", p=128))
                self.fl()
                hb = hr.next()
                self.norm_mod(ph, xt, n, self.A2, lambda kc: self.A2[:, kc, which:which + 1],
                              self.MOD, self.modcol(l, 3, which), hb, sqr, psr, tmr)
                hd = hid.next()
                for f in range(NF):
                    p1 = p1r.next()
                    for kc in range(8):
                        self.mm(p1, p1[:, 0:n], w1, w1[:, kc, f * 128:(f + 1) * 128], hb, hb[:, kc, 0:n], start=(kc == 0), stop=(kc == 7))
                    p3 = p3r.next()
                    for kc in range(8):
                        self.mm(p3, p3[:, 0:n], w3, w3[:, kc, f * 128:(f + 1) * 128], hb, hb[:, kc, 0:n], start=(kc == 0), stop=(kc == 7))
                    a = tmr.next()
                    self.act(a, a[:, 0:n], p1, p1[:, 0:n], AF.Silu)
                    self.tt('dve', hd, hd[:, f, 0:n], a, a[:, 0:n], p3, p3[:, 0:n], ALU.mult)
                for m in range(8):
                    p = p2r.next()
                    for f in range(NF):
                        self.mm(p, p[:, 0:n], w2, w2[:, f, m * 128:(m + 1) * 128], hd, hd[:, f, 0:n], start=(f == 0), stop=(f == NF - 1))
                    t = tmr.next()
                    self.act(t, t[:, 0:n], p, p[:, 0:n], AF.Copy, scale=self.MOD[:, l, 40 + m, which:which + 1], extra=[self.MOD])
                    self.tt('dve', xt, xt[:, m, 0:n], xt, xt[:, m, 0:n], t, t[:, 0:n], ALU.add)
                self.st(self.xcur, self.xcur[s_, :, g0:g0 + n].rearrange("(kc p) t -> p kc t", p=128), xt, xt[:, :, 0:n])
        ph.close()

    def phase_final(self):
        cfg = self.cfg
        S, TC = cfg.S, cfg.TC
        ph = Phase(self.s)
        gf = ph.sb('gf', [128, 8], F32)
        self.ld(gf, gf[:], self.inp['final_normT'], self.inp['final_normT'][:, :])
        xr = ph.ring('x', [128, 8, 512], F32, 2)
        hr = ph.ring('ho', [128, 8, 512], F32, 2)
        sqr = ph.ring('sq', [128, 8, 512], BF16, 1)
        tmr = ph.ring('tm', [128, 512], F32, 4)
        psr = ph.ring('pss', [128, 512], F32, 2, psum=True)
        for s_ in range(S):
            for (seg, t0, n, g0) in cfg.tiles(512):
                if seg == 0:
                    continue
                xt = xr.next()
                self.ld(xt, xt[:, :, 0:n], self.xcur, self.xcur[s_, :, g0:g0 + n].rearrange("(kc p) t -> p kc t", p=128))
                self.fl()
                ho = hr.next()
                self.norm_mod(ph, xt, n, gf, lambda kc: gf[:, kc:kc + 1], None, None, ho, sqr, psr, tmr)
                self.st(self.outT, self.outT[s_, :, t0:t0 + n].rearrange("(kc p) t -> p kc t", p=128), ho, ho[:, :, 0:n])
        ph.close()


def pmajor(v, nk):
    v = np.asarray(v)
    lead = v.shape[:-1]
    a = v.reshape(*lead, nk, 128)
    return np.ascontiguousarray(np.moveaxis(a, -1, 0))


def rope_tables(TC, TL, grid_w=64, theta=10000.0, dh=64):
    half = dh // 2
    t = np.arange(TL)
    row = (t // grid_w).astype(np.float32)
    col = (t % grid_w).astype(np.float32)
    inv = (theta ** (-np.arange(0, half, 2, dtype=np.float32) / half)).astype(np.float32)
    ang = np.concatenate([row[:, None] * inv, col[:, None] * inv], axis=-1).astype(np.float32)
    cos, sin = np.cos(ang), np.sin(ang)
    tab = np.zeros((128, 2, TC + TL), np.float32)
    tab[:, 0, :TC] = 1.0
    for p in range(128):
        d = p % 64
        i = d // 2
        tab[p, 0, TC:] = cos[:, i]
        tab[p, 1, TC:] = (-sin[:, i]) if d % 2 == 0 else sin[:, i]
    return tab


def host_prep(inputs, cfg, core, b0):
    S, TC, TL, L = cfg.S, cfg.TC, cfg.TL, cfg.L
    f32 = np.float32
    m = {}
    x = np.asarray(inputs['x'])[b0:b0 + S]
    ctx = np.asarray(inputs['ctx'])[b0:b0 + S]
    m['xin'] = np.ascontiguousarray(np.concatenate([ctx, x], axis=1).transpose(0, 2, 1)).astype(f32)
    cc = np.concatenate([np.asarray(inputs['c'])[b0:b0 + S], np.asarray(inputs['c_ctx'])[None, :]], axis=0)
    m['cT'] = np.ascontiguousarray(cc.T.reshape(8, 128, S + 1).transpose(1, 0, 2)).astype(f32)
    m['ada_bT'] = np.ascontiguousarray(np.asarray(inputs['ada_b'])[:L].reshape(L, 48, 128).transpose(2, 0, 1)).astype(f32)
    m['norm1T'] = np.ascontiguousarray(np.asarray(inputs['norm1'])[:L].reshape(L, 8, 128).transpose(2, 0, 1)).astype(f32)
    m['norm2T'] = np.ascontiguousarray(np.asarray(inputs['norm2'])[:L].reshape(L, 8, 128).transpose(2, 0, 1)).astype(f32)
    m['final_normT'] = np.ascontiguousarray(np.asarray(inputs['final_norm']).reshape(8, 128).T).astype(f32)
    m['b_gateT'] = np.ascontiguousarray(np.asarray(inputs['b_gate'])[:L].reshape(L, 24, 128).transpose(2, 0, 1)).astype(f32)
    for nm in WEIGHT_NAMES:
        m[nm] = np.ascontiguousarray(np.asarray(inputs[nm])[:L]).astype(f32)
    QO = 3 * 512 + 4 * 512
    perm = np.arange(640) ^ 1
    m['w_in_perm'] = np.ascontiguousarray(m['w_in'][:, :, QO:QO + 640][:, :, perm])
    gq = np.asarray(inputs['attn_q_norm'])[:L]
    gk = np.asarray(inputs['attn_k_norm'])[:L]
    p64 = np.arange(64) ^ 1
    g = np.stack([np.tile(gq, (1, 2)), np.tile(gq[:, p64], (1, 2)), np.tile(gk, (1, 2)), np.tile(gk[:, p64], (1, 2))], axis=-1)
    m['attn_gT'] = np.ascontiguousarray(g.transpose(1, 0, 2)).astype(f32)
    m['attn_gB'] = np.ascontiguousarray(np.stack([gq, gk], axis=1)).astype(f32)
    P9 = np.stack([np.asarray(inputs['rwkv_w0'])[:L, 0], np.asarray(inputs['rwkv_w0'])[:L, 1],
                   np.asarray(inputs['rwkv_a0'])[:L, 0], np.asarray(inputs['rwkv_a0'])[:L, 1],
                   np.asarray(inputs['rwkv_k_k'])[:L], np.asarray(inputs['rwkv_k_a'])[:L],
                   np.asarray(inputs['rwkv_r_k'])[:L].reshape(L, 512), np.asarray(inputs['rwkv_lnx_w'])[:L],
                   np.asarray(inputs['rwkv_lnx_b'])[:L]], axis=1)
    m['rwkv_pT'] = np.ascontiguousarray(P9.reshape(L, 9, 4, 128).transpose(3, 0, 1, 2)).astype(f32)
    m['rwkv_mu_xT'] = np.ascontiguousarray(np.asarray(inputs['rwkv_mu_x'])[:L].reshape(L, 3, 8, 128).transpose(3, 0, 1, 2)).astype(f32)
    m['rwkv_mu_rkv'] = np.ascontiguousarray(np.asarray(inputs['rwkv_mu_rkv'])[:L].reshape(L, 1536)).astype(f32)
    cm = np.ones((128, 512), f32)
    cm[:, ::64] = 0.0
    m['c_chunkmask'] = cm
    jj, tt_ = np.meshgrid(np.arange(64), np.arange(64), indexing='ij')
    sm = np.zeros((64, 2, 128), f32)
    sm[:, 0, 0:64] = (jj < tt_)
    sm[:, 0, 64:128] = (jj <= tt_)
    sm[:, 1, 0:64] = (jj > tt_)
    sm[:, 1, 64:128] = (jj >= tt_)
    m['c_scanmask'] = sm
    m['c_ident64'] = np.eye(64, dtype=f32)
    bmk = np.zeros((64, 2, 64), f32)
    bmk[:, 0, :] = ((jj // 32) == (tt_ // 32))
    bmk[:, 1, :] = 1.0 - bmk[:, 0, :]
    m['c_blkmask'] = bmk
    m['gdn_convT'] = np.ascontiguousarray(np.asarray(inputs['gdn_conv'])[:L].reshape(L, 12, 128, 5).transpose(2, 0, 1, 3)).astype(f32)
    rp = np.zeros((16, L, 3), f32)
    rp[0:8, :, 0] = np.asarray(inputs['gdn_dt_bias'])[:L].reshape(L, 8).T
    rp[0:8, :, 1] = np.asarray(inputs['gdn_A_log'])[:L].reshape(L, 8).T
    rp[4:8, :, 2] = 1.0
    m['gdn_rowp'] = rp
    m['gdn_normT'] = np.ascontiguousarray(np.asarray(inputs['gdn_norm'])[:L][:, :, None]).astype(f32)
    sb_ = np.zeros((16, 16, 128), f32)
    for r_ in range(16):
        sb_[r_, r_, :] = 1.0
    m['c_selb'] = sb_.astype(ml_dtypes.bfloat16)
    m['c_self'] = np.ascontiguousarray(sb_[:, :, :64])
    nm_ = np.zeros((64, 2, 2, 64), f32)
    NEG = -30000.0
    nm_[:, 0, 0] = np.where(jj < tt_, 0.0, NEG)
    nm_[:, 0, 1] = np.where(jj <= tt_, 0.0, NEG)
    nm_[:, 1, 0] = np.where(jj > tt_, 0.0, NEG)
    nm_[:, 1, 1] = np.where(jj >= tt_, 0.0, NEG)
    m['c_negmask'] = nm_
    m['c_ident'] = np.eye(128, dtype=f32).astype(ml_dtypes.bfloat16)
    ob = np.zeros((128, 128), f32)
    ob[:64, :64] = 1
    ob[64:, 64:] = 1
    m['c_onesblk'] = ob.astype(ml_dtypes.bfloat16)
    m['c_rope'] = rope_tables(TC, TL)
    return m


def input_shapes(m):
    out = {}
    for k, v in m.items():
        out[k] = (v.shape, BF16 if v.dtype == ml_dtypes.bfloat16 else F32)
    return out


_CACHE = {}


def run(inputs, cfg, n_cores):
    maps = [host_prep(inputs, cfg, c, c * cfg.S) for c in range(n_cores)]
    key = (cfg.S, cfg.TC, cfg.TL, cfg.L, tuple(sorted(cfg.debug)))
    kern = Kern(cfg, input_shapes(maps[0]))
    nc = kern.build()
    res = run_bass_kernel_spmd(nc, maps, core_ids=list(range(n_cores)))
    return kern, res


def kernel(**inputs):
    cfg = Cfg(S=2, TC=256, TL=4096, L=4)
    kern, res = run(inputs, cfg, 8)
    outs = [np.asarray(r['outT']).transpose(0, 2, 1) for r in res.results]
    return np.ascontiguousarray(np.concatenate(outs, axis=0)).astype(np.float32)
```

```python
import math
from contextlib import ExitStack
import numpy as np
import ml_dtypes
import concourse.bass as bass
import concourse.mybir as mybir
from concourse.bass_utils import run_bass_kernel_spmd

F32 = mybir.dt.float32
BF16 = mybir.dt.bfloat16
AF = mybir.ActivationFunctionType
ALU = mybir.AluOpType

ENGS = ['pe', 'act', 'dve', 'pool', 'sp']
RMS_EPS = 1e-6


class Res:
    dram = False

    def __init__(self, name):
        self.name = name
        self.lw = {}
        self.rd = {}
        self.sems = {}


class Tile(Res):
    def __init__(self, name, t):
        super().__init__(name)
        self.t = t

    def __getitem__(self, idx):
        return self.t[idx]


class DRes(Res):
    dram = True

    def __init__(self, name, ap):
        super().__init__(name)
        self.ap = ap

    def __getitem__(self, idx):
        return self.ap[idx]


class Sched:
    def __init__(self, nc, stack):
        self.nc = nc
        self.stack = stack
        self.eng = {'pe': nc.tensor, 'act': nc.scalar, 'dve': nc.vector, 'pool': nc.gpsimd, 'sp': nc.sync}
        self.cnt = {e: 0 for e in ENGS}
        self.esem = {e: stack.enter_context(nc.semaphore('es_' + e)) for e in ENGS if e != 'sp'}
        self.known = {e: {} for e in ENGS}
        self.dma_sems = []
        self.free_sems = []
        self.nwaits = 0
        self.nops = 0
        self.uid = 0
        self.pending = []
        self.trace = None

    def defer(self, out, in_, reads, writes, semres, key, kw):
        self.pending.append((out, in_, list(reads), list(writes), semres, key, kw))

    def flush(self):
        p, self.pending = self.pending, []
        for (out, in_, reads, writes, semres, key, kw) in p:
            self.dma('sp', out, in_, reads, writes, semres, key=key, **kw)

    def _conflict(self, reads, writes):
        for (_o, _i, pr, pw, _s, _key, _k) in self.pending:
            for x in writes:
                if any(x is y for y in pr) or any(x is y for y in pw):
                    return True
            for x in reads:
                if any(x is y for y in pw):
                    return True
        return False

    def sb(self, stack, name, shape, dtype):
        self.uid += 1
        t = stack.enter_context(self.nc.sbuf_tensor(f"{name}_{self.uid}", list(shape), dtype))
        return Tile(name, t)

    def ps(self, stack, name, shape, dtype=F32):
        self.uid += 1
        t = stack.enter_context(self.nc.psum_tensor(f"{name}_{self.uid}", list(shape), dtype))
        return Tile(name, t)

    def _waits(self, eng, reads, writes, dma=False):
        w = {}
        for r in reads:
            for sem, val in r.lw.items():
                if w.get(sem, 0) < val:
                    w[sem] = val
            if getattr(r, 'psum', False):
                for sem, val in r.rd.items():
                    if w.get(sem, 0) < val:
                        w[sem] = val
        for r in writes:
            for d in (r.lw, r.rd):
                for sem, val in d.items():
                    if w.get(sem, 0) < val:
                        w[sem] = val
        own = self.esem.get(eng)
        kn = self.known[eng]
        e = self.eng[eng]
        for sem, val in w.items():
            if sem is own and eng == 'pe' and not dma:
                continue
            if kn.get(sem, 0) >= val:
                continue
            kn[sem] = val
            e.wait_ge(sem, val)
            self.nwaits += 1
            if self.trace is not None:
                self.trace.append(f"  {eng} WAIT {sem.name}>={val}")

    def _post(self, ev_sem, ev_val, reads, writes, dma=False):
        for r in reads:
            if r.rd.get(ev_sem, 0) < ev_val:
                r.rd[ev_sem] = ev_val
        for r in writes:
            if r.dram or dma:
                if r.lw.get(ev_sem, 0) < ev_val:
                    r.lw[ev_sem] = ev_val
            else:
                r.lw = {ev_sem: ev_val}
                r.rd = {}

    def op(self, eng, fn, reads=(), writes=()):
        if self.pending and self._conflict(reads, writes):
            self.flush()
        self._waits(eng, reads, writes)
        ins = fn(self.eng[eng])
        self.cnt[eng] += 1
        sem = self.esem[eng]
        ins.then_inc(sem, 1)
        if self.trace is not None:
            self.trace.append(f"{eng} OP reads={[r.name for r in reads]} writes={[r.name for r in writes]} -> {sem.name}={self.cnt[eng]}")
        self._post(sem, self.cnt[eng], reads, writes)
        self.nops += 1
        return ins

    def dma(self, q, out, in_, reads, writes, semres, key=0, **kw):
        if self.pending and self._conflict(reads, writes):
            self.flush()
        self._waits(q, reads, writes, dma=True)
        slot = semres.sems.get(key)
        if slot is None:
            if self.free_sems:
                slot = self.free_sems.pop()
            else:
                self.uid += 1
                slot = [self.stack.enter_context(self.nc.semaphore(f"ds_{self.uid}")), 0]
            semres.sems[key] = slot
            self.dma_sems.append(slot)
        sem = slot[0]
        if slot[1] > 0 and self.known[q].get(sem, 0) < slot[1]:
            self.known[q][sem] = slot[1]
            self.eng[q].wait_ge(sem, slot[1])
            self.nwaits += 1
        ins = self.eng[q].dma_start(out=out, in_=in_, **kw)
        slot[1] += 16
        ins.then_inc(sem, 16)
        if self.trace is not None:
            self.trace.append(f"{q} DMA reads={[r.name for r in reads]} writes={[r.name for r in writes]} -> {sem.name}#{sem.num}={slot[1]}")
        self._post(sem, slot[1], reads, writes, dma=True)
        self.nops += 1
        return ins

    def barrier(self):
        self.flush()
        evs = {}
        for e, sem in self.esem.items():
            if self.cnt[e] > 0:
                evs[sem] = self.cnt[e]
        for slot in self.dma_sems:
            if slot[1] > 0:
                evs[slot[0]] = slot[1]
        for eng in ENGS:
            own = self.esem.get(eng)
            kn = self.known[eng]
            for sem, val in evs.items():
                if sem is own:
                    continue
                if kn.get(sem, 0) >= val:
                    continue
                kn[sem] = val
                self.eng[eng].wait_ge(sem, val)
                self.nwaits += 1
                if self.trace is not None:
                    self.trace.append(f"  {eng} BWAIT {sem.name}>={val}")


class Phase:
    def __init__(self, s):
        self.s = s
        self.stack = ExitStack()
        self.tiles = []
        if not hasattr(s, 'open_ph'):
            s.open_ph = []
        s.open_ph.append(self)

    def sb(self, name, shape, dtype):
        t = self.s.sb(self.stack, name, shape, dtype)
        self.tiles.append(t)
        return t

    def ps(self, name, shape, dtype=F32):
        t = self.s.ps(self.stack, name, shape, dtype)
        t.psum = True
        self.tiles.append(t)
        return t

    def ring(self, name, shape, dtype, n, psum=False):
        return Ring([(self.ps if psum else self.sb)(f"{name}{i}", shape, dtype) for i in range(n)])

    def close(self):
        s = self.s
        s.barrier()
        for t in self.tiles:
            for slot in t.sems.values():
                s.free_sems.append(slot)
                s.dma_sems.remove(slot)
            t.sems = {}
        self.stack.close()
        s.open_ph.remove(self)


class Ring:
    def __init__(self, tiles):
        self.tiles = tiles
        self.i = 0

    def next(self):
        t = self.tiles[self.i % len(self.tiles)]
        self.i += 1
        return t


class Cfg:
    def __init__(self, S=2, TC=256, TL=4096, L=4, debug=()):
        self.S, self.TC, self.TL, self.L = S, TC, TL, L
        self.TT = TC + TL
        self.debug = set(debug)

    def tiles(self, nt):
        out = []
        for t0 in range(0, self.TC, nt):
            out.append((0, t0, min(nt, self.TC - t0), t0))
        for t0 in range(0, self.TL, nt):
            out.append((1, t0, min(nt, self.TL - t0), self.TC + t0))
        return out

    def pcol(self, seg, t, halo):
        return (halo + t) if seg == 0 else (self.TC + 3 * halo + t)


WEIGHT_NAMES = ['ada_w', 'w_in', 'rwkv_w1', 'rwkv_w2', 'rwkv_a1', 'rwkv_a2', 'rwkv_g1', 'rwkv_g2',
                'gdn_w_alpha', 'gdn_w_beta', 'w_up_a', 'w_up_b', 'w_up_c', 'w_gate', 'w_out',
                'ffn_w1', 'ffn_w3', 'ffn_w2']


class Kern:
    def __init__(self, cfg, shapes):
        self.cfg = cfg
        self.nc = nc = bass.Bass("TRN2", target_bir_lowering=False)
        self.inp = {}
        for name, (shape, dt) in shapes.items():
            self.inp[name] = DRes(name, nc.dram_tensor(name, list(shape), dt, kind="ExternalInput").ap())
        self.scr = {}
        self.dbg_outs = []

    def dscr(self, name, shape, dtype):
        kind = "ExternalOutput" if name in self.cfg.debug else "Internal"
        r = DRes(name, self.nc.dram_tensor(name, list(shape), dtype, kind=kind).ap())
        if name in self.cfg.debug:
            self.dbg_outs.append(name)
        self.scr[name] = r
        return r

    def mm(self, ps, out, lt, lhsT, rt, rhs, start=True, stop=True):
        self.s.op('pe', lambda e: e.matmul(out, lhsT=lhsT, rhs=rhs, start=start, stop=stop),
                  [lt, rt] if lt is not rt else [lt], [ps])

    def mmr(self, ps, out, lt, lhsT, rt, rhs, start=True, stop=True):
        if getattr(self.cfg, 'fp32r', False):
            lhsT = lhsT.bitcast(mybir.dt.float32r)
            rhs = rhs.bitcast(mybir.dt.float32r)
        self.mm(ps, out, lt, lhsT, rt, rhs, start=start, stop=stop)

    def act(self, ot, out, it, in_, func, bias=0.0, scale=1.0, extra=(), eng='act'):
        kw = {}
        if not (isinstance(bias, float) and bias == 0.0):
            kw['bias'] = bias
        if not (isinstance(scale, float) and scale == 1.0):
            kw['scale'] = scale
        self.s.op('act', lambda e: e.activation(out=out, in_=in_, func=func, **kw), [it] + list(extra), [ot])

    def tt(self, eng, ot, out, t0, in0, t1, in1, op):
        self.s.op(eng, lambda e: e.tensor_tensor(out=out, in0=in0, in1=in1, op=op), [t0, t1], [ot])

    def ts(self, eng, ot, out, t0, in0, s1, s2=None, op0=ALU.mult, op1=None, extra=()):
        if op1 is None:
            self.s.op(eng, lambda e: e.tensor_scalar(out=out, in0=in0, scalar1=s1, scalar2=None, op0=op0),
                      [t0] + list(extra), [ot])
        else:
            self.s.op(eng, lambda e: e.tensor_scalar(out=out, in0=in0, scalar1=s1, scalar2=s2, op0=op0, op1=op1),
                      [t0] + list(extra), [ot])

    def stt(self, ot, out, t0, in0, sc, t1, in1, op0, op1, extra=()):
        self.s.op('dve', lambda e: e.scalar_tensor_tensor(out=out, in0=in0, scalar=sc, in1=in1, op0=op0, op1=op1),
                  [t0, t1] + list(extra), [ot])

    def cp(self, eng, ot, out, it, in_):
        if eng == 'act':
            self.s.op('act', lambda e: e.activation(out=out, in_=in_, func=AF.Copy), [it], [ot])
        else:
            self.s.op(eng, lambda e: e.tensor_copy(out=out, in_=in_), [it], [ot])

    def ld(self, t, out, src, in_, q='sp', key=0):
        self.s.dma(q, out, in_, [src], [t], t, key=key)

    def st(self, dst, out, t, in_, q='sp', key=0, **kw):
        import os
        if dst.name in os.environ.get('NOST', '').split(','):
            return
        self.s.defer(out, in_, [t], [dst], t, key, kw)

    def fl(self):
        self.s.flush()

    def load_w(self, ph, wt, dst_fn, src, src_fn, ncols, rows_kc, piece=512, row_scale=None, col_scale=None,
               cast_eng=('dve', 'act')):
        i = 0
        for c0 in range(0, ncols, piece):
            c1 = min(ncols, c0 + piece)
            stg = self.wstage.next()
            sv = stg[:, 0:rows_kc, 0:c1 - c0]
            self.ld(stg, sv, src, src_fn(c0, c1), q='sp')
            if col_scale is not None:
                cst, cfn = col_scale
                for kc in range(rows_kc):
                    self.tt('dve', stg, stg[:, kc, 0:c1 - c0], stg, stg[:, kc, 0:c1 - c0], cst, cfn(c0, c1), ALU.mult)
            if row_scale is not None:
                rst, rfn = row_scale
                for kc in range(rows_kc):
                    self.ts('dve', wt, dst_fn(c0, c1)[:, kc, :], stg, stg[:, kc, 0:c1 - c0], rfn(kc), extra=[rst])
            else:
                eng = cast_eng[i % len(cast_eng)]
                self.cp(eng, wt, dst_fn(c0, c1), stg, sv)
            i += 1

    def build(self):
        cfg = self.cfg
        nc = self.nc
        S, L, TT, TC, TL = cfg.S, cfg.L, cfg.TT, cfg.TC, cfg.TL
        with ExitStack() as gstack:
            self.s = s = Sched(nc, gstack)
            if getattr(cfg, 'trace', False):
                s.trace = []
            self.xcur = self.dscr('xcur', [S, 1024, TT], F32)
            self.xmid = self.dscr('xmid', [S, 1024, TT], F32)
            self.hT = self.dscr('hT', [S, 1024, TT + 8], BF16)
            self.qT = self.dscr('qT', [S, 512, TT], BF16)
            self.kT = self.dscr('kT', [S, 128, TT], BF16)
            self.vtok = self.dscr('vtok', [S, TT, 130], BF16)
            self.yaT = self.dscr('yaT', [S, 512, TT], BF16)
            self.ybT = self.dscr('ybT', [S, 512, TT], BF16)
            self.ycT = self.dscr('ycT', [S, 512, TT], BF16)
            self.dbgT = self.dscr('dbg_scan', [64, 8, 2048], BF16)
            self.gd_u = self.dscr('gd_u', [S, 1536, TT + 8], BF16)
            self.gd_z = self.dscr('gd_z', [S, 512, TT], BF16)
            self.gd_ab = self.dscr('gd_ab', [S, 16, TT], F32)
            self.gd_gc = self.dscr('gd_gc', [S, 16, TT // 64], F32)
            self.gd_dm = self.dscr('gd_dm', [S, 2, TT // 64, 64, 512], BF16)
            self.gd_fm = self.dscr('gd_fm', [S, 8, 512, TT], BF16)
            self.gd_tm = self.dscr('gd_tm', [S, 4, TT, 512], BF16)
            self.gd_y = self.dscr('gd_y', [S, 2, 512, TT], F32)
            self.rw_fm = self.dscr('rw_fm', [S, 2, 4, 512, TT], BF16)
            self.rw_tm = self.dscr('rw_tm', [S, 5, TT, 512], BF16)
            self.rw_gc = self.dscr('rw_gc', [S, 2, 512, TT // 64], F32)
            self.rw_bg = self.dscr('rw_bg', [S, 2, 512, TT], BF16)
            self.rw_y = self.dscr('rw_y', [S, 2, 512, TT], F32)
            self.outT = DRes('outT', nc.dram_tensor('outT', [S, 1024, TL], F32, kind="ExternalOutput").ap())
            self.G = gph = Phase(s)
            self.ident = gph.sb('ident', [128, 128], BF16)
            self.ones128 = gph.sb('ones128', [128, 128], BF16)
            self.onesblk = gph.sb('onesblk', [128, 128], BF16)
            self.onesf = gph.sb('onesf', [128, 128], F32)
            self.MOD = gph.sb('MOD', [128, L, 48, S + 1], F32)
            self.epsc = gph.sb('epsc', [128, 1], F32)
            self.ld(self.ident, self.ident[:], self.inp['c_ident'], self.inp['c_ident'][:, :])
            self.ld(self.onesblk, self.onesblk[:], self.inp['c_onesblk'], self.inp['c_onesblk'][:, :])
            s.op('dve', lambda e: e.memset(self.ones128[:], 1.0), [], [self.ones128])
            s.op('dve', lambda e: e.memset(self.onesf[:], 1.0), [], [self.onesf])
            s.op('dve', lambda e: e.memset(self.epsc[:], RMS_EPS), [], [self.epsc])
            try:
                self.zero_pads()
                self.chk('pads')
                self.phase_mod()
                self.chk('mod')
                for l in range(L):
                    self.layer(l)
                self.phase_final()
            except StopIteration:
                for p in reversed(list(s.open_ph)):
                    if p is not gph:
                        p.close()
            gph.close()
            s.barrier()
        return nc

    def chk(self, name):
        if getattr(self.cfg, 'stop', None) == name:
            raise StopIteration

    def zero_pads(self):
        cfg = self.cfg
        ph = Phase(self.s)
        z = ph.sb('z', [128, 12, 2], BF16)
        self.s.op('dve', lambda e: e.memset(z[:], 0.0), [], [z])
        for s_ in range(cfg.S):
            for i, c in enumerate((0, cfg.TC + 2, cfg.TC + 4, cfg.TT + 6)):
                self.st(self.hT, self.hT[s_, :, c:c + 2].rearrange("(kc p) t -> p kc t", p=128), z, z[:, 0:8, 0:2], key=i)
                self.st(self.gd_u, self.gd_u[s_, :, c:c + 2].rearrange("(kc p) t -> p kc t", p=128), z, z[:, 0:12, 0:2], key=4 + i)
        ph.close()

    def phase_mod(self):
        cfg = self.cfg
        S, L = cfg.S, cfg.L
        s = self.s
        ph = Phase(s)
        cT = ph.sb('cT', [128, 8, S + 1], F32)
        sc = ph.sb('sc', [128, 8, S + 1], F32)
        bia = ph.sb('bia', [128, L, 48], F32)
        wst = ph.ring('adaw', [128, 8, 1024], F32, 2)
        pm = ph.ring('pm', [128, 8, 4], F32, 2, psum=True)
        self.ld(cT, cT[:], self.inp['cT'], self.inp['cT'][:, :, :])
        self.ld(bia, bia[:], self.inp['ada_bT'], self.inp['ada_bT'][:, :, :])
        self.act(sc, sc[:], cT, cT[:], AF.Silu)
        adaw = self.inp['ada_w']
        for l in range(L):
            for pc in range(6):
                w = wst.next()
                self.ld(w, w[:], adaw, adaw[l, :, pc * 1024:(pc + 1) * 1024].rearrange("(kc p) c -> p kc c", p=128))
                p = pm.next()
                for m in range(8):
                    for kc in range(8):
                        self.mm(p, p[:, m, 0:S + 1], w, w[:, kc, m * 128:(m + 1) * 128], sc, sc[:, kc, :],
                                start=(kc == 0), stop=(kc == 7))
                self.tt('dve', self.MOD, self.MOD[:, l, pc * 8:(pc + 1) * 8, :], p, p[:, :, 0:S + 1],
                        bia, bia[:, l, pc * 8:(pc + 1) * 8].unsqueeze(2).to_broadcast([128, 8, S + 1]), ALU.add)
        ph.close()

    def norm_mod(self, ph, xt, n, A, Acol, B, Bcol, hb, sqr, psr, tmr):
        sq = sqr.next()
        self.act(sq, sq[:, :, 0:n], xt, xt[:, :, 0:n], AF.Square)
        pss = psr.next()
        for kc in range(8):
            self.mm(pss, pss[:, 0:n], self.ones128, self.ones128[:], sq, sq[:, kc, 0:n], start=(kc == 0), stop=(kc == 7))
        if not hasattr(ph, 'rsr'):
            ph.rsr = ph.ring('rs', [128, 512], F32, 2)
        lnv = ph.rsr.next()
        self.act(lnv, lnv[:, 0:n], pss, pss[:, 0:n], AF.Ln, bias=self.epsc[:, 0:1], scale=1.0 / 1024, extra=[self.epsc])
        rstd = ph.rsr.next()
        self.act(rstd, rstd[:, 0:n], lnv, lnv[:, 0:n], AF.Exp, scale=-0.5)
        for kc in range(8):
            tmp = tmr.next()
            self.stt(tmp, tmp[:, 0:n], xt, xt[:, kc, 0:n], Acol(kc), rstd, rstd[:, 0:n], ALU.mult, ALU.mult, extra=[A])
            if B is None:
                self.cp('act', hb, hb[:, kc, 0:n], tmp, tmp[:, 0:n])
            else:
                self.act(hb, hb[:, kc, 0:n], tmp, tmp[:, 0:n], AF.Identity, bias=Bcol(kc), extra=[B])

    def layer_consts(self, l):
        cfg = self.cfg
        S = cfg.S
        ph = self.LC
        s = self.s
        g12 = ph.sb('g12', [128, 2, 8], F32)
        self.ld(g12, g12[:, 0, :], self.inp['norm1T'], self.inp['norm1T'][:, l, :])
        self.ld(g12, g12[:, 1, :], self.inp['norm2T'], self.inp['norm2T'][:, l, :], key=1)
        self.A1 = ph.sb('A1', [128, 8, S + 1], F32)
        self.A2 = ph.sb('A2', [128, 8, S + 1], F32)
        for A, j, gi in ((self.A1, 1, 0), (self.A2, 4, 1)):
            self.ts('dve', A, A[:], self.MOD, self.MOD[:, l, j * 8:(j + 1) * 8, :], 1.0, op0=ALU.add)
            self.tt('dve', A, A[:], A, A[:], g12, g12[:, gi, :].unsqueeze(2).to_broadcast([128, 8, S + 1]), ALU.mult)

    def modcol(self, l, j, which):
        return lambda kc: self.MOD[:, l, j * 8 + kc, which:which + 1]

    def layer(self, l):
        cfg = self.cfg
        self.LC = Phase(self.s)
        self.layer_consts(l)
        use = getattr(cfg, 'use', (1, 1, 1))
        try:
            self.phase_norm1(l)
            self.chk('norm1')
            for j, yt in enumerate((self.yaT, self.ybT, self.ycT)):
                if not use[j]:
                    self.zero_y(yt)
            if use[0]:
                self.phase_rwkv_proj(l)
                self.chk('rwkv_proj')
                self.phase_rwkv_scan(l)
                self.chk('rwkv_scan')
                self.phase_rwkv_post(l)
                self.chk('rwkv_post')
            if use[1]:
                self.phase_gdn_proj(l)
                self.chk('gdn_proj')
                self.phase_gdn_prep(l)
                self.chk('gdn_prep')
                self.phase_gdn_scan(l)
                self.chk('gdn_scan')
                self.phase_gdn_post(l)
                self.chk('gdn_post')
            if use[2]:
                self.phase_attn_proj(l)
                self.chk('attn_proj')
                self.phase_attn(l)
                self.chk('attn')
            self.phase_merge(l)
            self.chk('merge')
            self.phase_ffn(l)
            self.chk('ffn')
        except StopIteration:
            raise
        self.LC.close()

    def zero_y(self, yt):
        cfg = self.cfg
        ph = Phase(self.s)
        z = ph.sb('z', [128, 4, 512], BF16)
        self.s.op('dve', lambda e: e.memset(z[:], 0.0), [], [z])
        for s_ in range(cfg.S):
            for (seg, t0, n, g0) in cfg.tiles(512):
                self.st(yt, yt[s_, :, g0:g0 + n].rearrange("(kc p) t -> p kc t", p=128), z, z[:, :, 0:n])
        ph.close()

    def xsrc(self, l):
        return self.inp['xin'] if l == 0 else self.xcur

    def phase_norm1(self, l):
        cfg = self.cfg
        S = cfg.S
        ph = Phase(self.s)
        xr = ph.ring('x', [128, 8, 512], F32, 2)
        hr = ph.ring('hb', [128, 8, 512], BF16, 2)
        sqr = ph.ring('sq', [128, 8, 512], BF16, 1)
        tmr = ph.ring('tm', [128, 512], F32, 4)
        psr = ph.ring('pss', [128, 512], F32, 2, psum=True)
        xs = self.xsrc(l)
        for s_ in range(S):
            for (seg, t0, n, g0) in cfg.tiles(512):
                which = S if seg == 0 else s_
                xt = xr.next()
                self.ld(xt, xt[:, :, 0:n], xs, xs[s_, :, g0:g0 + n].rearrange("(kc p) t -> p kc t", p=128))
                self.fl()
                hb = hr.next()
                self.norm_mod(ph, xt, n, self.A1, lambda kc: self.A1[:, kc, which:which + 1],
                              self.MOD, self.modcol(l, 0, which), hb, sqr, psr, tmr)
                c0 = cfg.pcol(seg, t0, 2)
                self.st(self.hT, self.hT[s_, :, c0:c0 + n].rearrange("(kc p) t -> p kc t", p=128), hb, hb[:, :, 0:n])
        ph.close()

    def phase_attn_proj(self, l):
        cfg = self.cfg
        S = cfg.S
        s = self.s
        ph = Phase(s)
        self.wstage = ph.ring('wstg', [128, 8, 512], F32, 2)
        win = self.inp['w_in']
        winp = self.inp['w_in_perm']
        QO = 3 * 512 + 4 * 512
        wq = ph.sb('wq', [128, 8, 512], BF16)
        wqs = ph.sb('wqs', [128, 8, 512], BF16)
        wk = ph.sb('wk', [128, 8, 128], BF16)
        wks = ph.sb('wks', [128, 8, 128], BF16)
        wv = ph.sb('wv', [128, 8, 128], BF16)
        r = lambda ap: ap.rearrange("(kc p) c -> p kc c", p=128)
        self.load_w(ph, wq, lambda a, b: wq[:, :, a:b], win, lambda a, b: r(win[l, :, QO + a:QO + b]), 512, 8)
        self.load_w(ph, wqs, lambda a, b: wqs[:, :, a:b], winp, lambda a, b: r(winp[l, :, a:b]), 512, 8)
        self.load_w(ph, wk, lambda a, b: wk[:, :, a:b], win, lambda a, b: r(win[l, :, QO + 512 + a:QO + 512 + b]), 128, 8)
        self.load_w(ph, wks, lambda a, b: wks[:, :, a:b], winp, lambda a, b: r(winp[l, :, 512 + a:512 + b]), 128, 8)
        self.load_w(ph, wv, lambda a, b: wv[:, :, a:b], win, lambda a, b: r(win[l, :, QO + 640 + a:QO + 640 + b]), 128, 8)
        self.chk('ap_w')
        gn = ph.sb('gn', [128, 4], F32)
        self.ld(gn, gn[:], self.inp['attn_gT'], self.inp['attn_gT'][:, l, :])
        hr = ph.ring('hb', [128, 8, 512], BF16, 2)
        csr = ph.ring('cs', [128, 2, 512], F32, 2)
        qst = ph.ring('qst', [128, 5, 512], BF16, 2)
        vst = ph.ring('vst', [128, 4, 130], BF16, 2)
        for v in vst.tiles:
            s.op('dve', lambda e: e.memset(v[:], 1.0), [], [v])
        sqr = ph.ring('sq', [128, 512], BF16, 2)
        tmr = ph.ring('tm', [128, 512], F32, 8)
        pq = ph.ring('pq', [128, 512], F32, 4, psum=True)
        pn = ph.ring('pn', [128, 512], F32, 2, psum=True)
        pv = ph.ring('pv', [128, 128], F32, 2, psum=True)
        cst = self.inp['c_rope']
        for s_ in range(S):
            for (seg, t0, n, g0) in cfg.tiles(512):
                hb = hr.next()
                c0 = cfg.pcol(seg, t0, 2)
                self.ld(hb, hb[:, :, 0:n], self.hT, self.hT[s_, :, c0:c0 + n].rearrange("(kc p) t -> p kc t", p=128))
                cs = csr.next()
                self.ld(cs, cs[:, :, 0:n], cst, cst[:, :, g0:g0 + n])
                self.fl()
                qs = qst.next()
                for c in range(5):
                    w0, w1, col, gi = (wq, wqs, c * 128, 0) if c < 4 else (wk, wks, 0, 2)
                    p0 = pq.next()
                    for kc in range(8):
                        self.mm(p0, p0[:, 0:n], w0, w0[:, kc, col:col + 128], hb, hb[:, kc, 0:n], start=(kc == 0), stop=(kc == 7))
                    p1 = pq.next()
                    for kc in range(8):
                        self.mm(p1, p1[:, 0:n], w1, w1[:, kc, col:col + 128], hb, hb[:, kc, 0:n], start=(kc == 0), stop=(kc == 7))
                    import os
                    SK = os.environ.get('SKIP', '')
                    if 'post' in SK:
                        self.cp('dve', qs, qs[:, c, 0:n], p0, p0[:, 0:n])
                        self.cp('act', sqr.next(), sqr.tiles[0][:, 0:n], p1, p1[:, 0:n])
                        continue
                    sq = sqr.next()
                    if 'nosq' in SK:
                        t0_ = tmr.next()
                        self.cp('act', t0_, t0_[:, 0:n], p0, p0[:, 0:n])
                        self.act(sq, sq[:, 0:n], t0_, t0_[:, 0:n], AF.Square)
                    else:
                        self.act(sq, sq[:, 0:n], p0, p0[:, 0:n], AF.Square)
                    pss = pn.next()
                    self.mm(pss, pss[:, 0:n], self.onesblk, self.onesblk[:], sq, sq[:, 0:n])
                    lnv = tmr.next()
                    self.act(lnv, lnv[:, 0:n], pss, pss[:, 0:n], AF.Ln, bias=self.epsc[:, 0:1], scale=1.0 / 64, extra=[self.epsc])
                    rstd = tmr.next()
                    self.act(rstd, rstd[:, 0:n], lnv, lnv[:, 0:n], AF.Exp, scale=-0.5)
                    a = tmr.next()
                    b = tmr.next()
                    self.act(a, a[:, 0:n], p0, p0[:, 0:n], AF.Copy, scale=gn[:, gi:gi + 1], extra=[gn])
                    self.act(b, b[:, 0:n], p1, p1[:, 0:n], AF.Copy, scale=gn[:, gi + 1:gi + 2], extra=[gn])
                    self.tt('dve', a, a[:, 0:n], a, a[:, 0:n], cs, cs[:, 0, 0:n], ALU.mult)
                    self.tt('dve', b, b[:, 0:n], b, b[:, 0:n], cs, cs[:, 1, 0:n], ALU.mult)
                    self.tt('dve', a, a[:, 0:n], a, a[:, 0:n], b, b[:, 0:n], ALU.add)
                    self.tt('dve', qs, qs[:, c, 0:n], a, a[:, 0:n], rstd, rstd[:, 0:n], ALU.mult)
                self.chk('ap_q')
                if not getattr(cfg, 'skipq', False):
                    self.st(self.qT, self.qT[s_, :, g0:g0 + n].rearrange("(c p) t -> p c t", p=128), qs, qs[:, 0:4, 0:n])
                self.chk('ap_qs1')
                if getattr(cfg, 'ksem', False):
                    if not hasattr(ph, 'ksems'):
                        ph.ksems = {}
                    kr = ph.ksems.setdefault(id(qs), Tile('ksem', None))
                    if kr not in ph.tiles:
                        ph.tiles.append(kr)
                    self.s.defer(self.kT[s_, :, g0:g0 + n], qs[:, 4, 0:n], [qs], [self.kT], kr, 0, {})
                else:
                    self.st(self.kT, self.kT[s_, :, g0:g0 + n], qs, qs[:, 4, 0:n], key=1)
                self.chk('ap_qst')
                vs = vst.next()
                nb = n // 128
                for tb in range(nb if 'v' not in os.environ.get('SKIP', '') else 0):
                    p = pv.next()
                    for kc in range(8):
                        self.mm(p, p[:, :], hb, hb[:, kc, tb * 128:(tb + 1) * 128], wv, wv[:, kc, :], start=(kc == 0), stop=(kc == 7))
                    self.cp('act', vs, vs[:, tb, :].rearrange("p (g d) -> p g d", g=2)[:, :, 0:64], p, p[:, :].rearrange("p (g d) -> p g d", g=2))
                self.chk('ap_v1')
                self.st(self.vtok, self.vtok[s_, g0:g0 + n, :].rearrange("(tb p) c -> p tb c", p=128), vs, vs[:, 0:nb, :])
        ph.close()

    def phase_attn(self, l):
        cfg = self.cfg
        S, TT, TC = cfg.S, cfg.TT, cfg.TC
        s = self.s
        ph = Phase(s)
        NST = TT // 128
        gq = ph.sb('gq', [128, 2, 64], F32)
        self.ld(gq, gq[:], self.inp['attn_gB'], self.inp['attn_gB'][l:l + 1, :, :].to_broadcast([128, 2, 64]))
        mx = ph.sb('mx', [128, 2], F32)
        s.op('dve', lambda e: e.tensor_reduce(out=mx[:], in_=gq[:], axis=mybir.AxisListType.X, op=ALU.max,
                                              apply_absolute_value=True), [gq], [mx])
        negM = ph.sb('negM', [128, 1], F32)
        self.stt(negM, negM[:], mx, mx[:, 0:1], -8.0, mx, mx[:, 1:2], ALU.mult, ALU.mult)
        kt = ph.sb('kt', [64, 2, TT], BF16)
        va = ph.sb('va', [128, NST, 130], BF16)
        qr = ph.ring('q', [64, 8, 512], BF16, 2)
        ptr = ph.ring('pt', [128, 512], BF16, 4)
        ysr = ph.ring('ys', [64, 8, 512], BF16, 2)
        osr = ph.ring('os', [64, 512], F32, 2)
        rcr = ph.ring('rc', [65, 512], F32, 2)
        psr = ph.ring('ps', [128, 512], F32, 4, psum=True)
        por = ph.ring('po', [65, 512], F32, 2, psum=True)
        pbr = ph.ring('pb', [64, 512], F32, 1, psum=True)
        for s_ in range(S):
            self.ld(kt, kt[:], self.kT, self.kT[s_].rearrange("(g d) t -> d g t", g=2))
            self.ld(va, va[:], self.vtok, self.vtok[s_].rearrange("(tb p) c -> p tb c", p=128))
            for (seg, t0, n, g0) in cfg.tiles(512):
                if seg == 0 and l == cfg.L - 1:
                    continue
                q = qr.next()
                self.ld(q, q[:, :, 0:n], self.qT, self.qT[s_, :, g0:g0 + n].rearrange("(h d) t -> d h t", d=64))
                self.fl()
                nst = (TC // 128) if seg == 0 else NST
                ys = ysr.next()
                for h in range(8):
                    g = h // 4
                    po = por.next()
                    for st in range(nst):
                        p = psr.next()
                        self.mm(p, p[:, 0:n], kt, kt[:, g, st * 128:(st + 1) * 128], q, q[:, h, 0:n])
                        pt = ptr.next()
                        self.act(pt, pt[:, 0:n], p, p[:, 0:n], AF.Exp, bias=negM[:, 0:1], scale=0.125, extra=[negM])
                        self.mm(po, po[0:65, 0:n], va, va[:, st, g * 65:(g + 1) * 65], pt, pt[:, 0:n],
                                start=(st == 0), stop=(st == nst - 1))
                    rc = rcr.next()
                    s.op('dve', lambda e: e.reciprocal(out=rc[64:65, 0:n], in_=po[64:65, 0:n]), [po], [rc])
                    os_ = osr.next()
                    self.cp('act', os_, os_[:, 0:n], po, po[0:64, 0:n])
                    pb = pbr.next()
                    self.mm(pb, pb[:, 0:n], self.onesf, self.onesf[64:65, 0:64], rc, rc[64:65, 0:n])
                    self.tt('dve', ys, ys[:, h, 0:n], os_, os_[:, 0:n], pb, pb[:, 0:n], ALU.mult)
                self.st(self.ycT, self.ycT[s_, :, g0:g0 + n].rearrange("(h d) t -> d h t", d=64), ys, ys[:, :, 0:n])
        ph.close()

    def phase_rwkv_proj(self, l):
        cfg = self.cfg
        S, TT = cfg.S, cfg.TT
        s = self.s
        ph = Phase(s)
        CDEC = math.exp(-0.5)
        wc = ph.sb('wc', [128, 8, 1536], BF16)
        ws = ph.sb('ws', [128, 8, 1536], BF16)
        cmask = ph.sb('cmask', [128, 512], F32)
        pp = ph.sb('pp', [128, 9, 4], F32)
        l1c = ph.sb('l1c', [128, 8, 416], BF16)
        l1s = ph.sb('l1s', [128, 8, 416], BF16)
        w2w = ph.sb('w2w', [128, 1, 512], BF16)
        w2a = ph.sb('w2a', [128, 1, 512], BF16)
        g2a = ph.sb('g2a', [128, 1, 512], BF16)
        g2b = ph.sb('g2b', [32, 1, 512], BF16)
        omka = ph.sb('omka', [128, 4], F32)
        outer = ph
        ph = Phase(s)
        self.wstage = ph.ring('wstg', [128, 8, 256], F32, 2)
        r = lambda ap: ap.rearrange("(kc p) c -> p kc c", p=128)
        win = self.inp['w_in']
        self.ld(pp, pp[:], self.inp['rwkv_pT'], self.inp['rwkv_pT'][:, l, :, :])
        self.ts('dve', omka, omka[:], pp, pp[:, 5, :], -1.0, 1.0, op0=ALU.mult, op1=ALU.add)
        mux = ph.sb('mux', [128, 3, 8], F32)
        self.ld(mux, mux[:], self.inp['rwkv_mu_xT'], self.inp['rwkv_mu_xT'][:, l, :, :])
        mxc = ph.sb('mxc', [128, 3, 8], F32)
        mxs = ph.sb('mxs', [128, 3, 8], F32)
        self.ts('dve', mxc, mxc[:], mux, mux[:], -1.0, 1.0, op0=ALU.mult, op1=ALU.add)
        self.ts('dve', mxs, mxs[:], mux, mux[:], 0.5, op0=ALU.mult)
        mur = ph.sb('mur', [128, 1536], F32)
        self.ld(mur, mur[:], self.inp['rwkv_mu_rkv'], self.inp['rwkv_mu_rkv'][l:l + 1, :].to_broadcast([128, 1536]))
        murc = ph.sb('murc', [128, 1536], F32)
        self.ts('dve', murc, murc[:], mur, mur[:], -1.0, 1.0, op0=ALU.mult, op1=ALU.add)
        self.ts('dve', mur, mur[:], mur, mur[:], 0.5, op0=ALU.mult)
        self.load_w(ph, wc, lambda a, b: wc[:, :, a:b], win, lambda a, b: r(win[l, :, a:b]), 1536, 8, piece=256,
                    col_scale=(murc, lambda a, b: murc[:, a:b]))
        self.load_w(ph, ws, lambda a, b: ws[:, :, a:b], win, lambda a, b: r(win[l, :, a:b]), 1536, 8, piece=256,
                    col_scale=(mur, lambda a, b: mur[:, a:b]))
        srcs = [(self.inp['rwkv_w1'], 0, 0, 0), (self.inp['rwkv_w1'], 1, 64, 0), (self.inp['rwkv_a1'], 0, 128, 1),
                (self.inp['rwkv_a1'], 1, 192, 1)]
        for (src, d, off, mi) in srcs:
            for (dst, sc) in ((l1c, mxc), (l1s, mxs)):
                self.load_w(ph, dst, lambda a, b, dst=dst, off=off: dst[:, :, off + a:off + b], src,
                            lambda a, b, src=src, d=d: r(src[l, d, :, a:b]), 64, 8, piece=256,
                            row_scale=(sc, lambda kc, sc=sc, mi=mi: sc[:, mi, kc:kc + 1]))
        g1 = self.inp['rwkv_g1']
        for (dst, sc) in ((l1c, mxc), (l1s, mxs)):
            self.load_w(ph, dst, lambda a, b, dst=dst: dst[:, :, 256 + a:256 + b], g1, lambda a, b: r(g1[l, :, a:b]), 160, 8,
                        piece=256, row_scale=(sc, lambda kc, sc=sc: sc[:, 2, kc:kc + 1]))
        for dst, nm in ((w2w, 'rwkv_w2'), (w2a, 'rwkv_a2')):
            src = self.inp[nm]
            self.load_w(ph, dst, lambda a, b, dst=dst: dst[:, :, a:b], src,
                        lambda a, b, src=src: src[l, :, :, a:b].rearrange("d (o k) c -> (d k) o c", o=1), 512, 1, piece=256)
        g2 = self.inp['rwkv_g2']
        self.load_w(ph, g2a, lambda a, b: g2a[:, :, a:b], g2, lambda a, b: g2[l, 0:128, a:b].rearrange("(o k) c -> k o c", o=1), 512, 1, piece=256)
        stg = self.wstage.next()
        self.ld(stg, stg[0:32, 0, 0:256], g2, g2[l, 128:160, 0:256])
        self.cp('dve', g2b, g2b[:, 0, 0:256], stg, stg[0:32, 0, 0:256])
        stg = self.wstage.next()
        self.ld(stg, stg[0:32, 0, 0:256], g2, g2[l, 128:160, 256:512])
        self.cp('dve', g2b, g2b[:, 0, 256:512], stg, stg[0:32, 0, 0:256])
        self.ld(cmask, cmask[:], self.inp['c_chunkmask'], self.inp['c_chunkmask'][:, :])
        ph.close()
        ph = outer
        self.chk('rp_w')
        hr = ph.ring('hb', [128, 8, 516], BF16, 2)
        hsr = ph.ring('hs', [128, 8, 512], BF16, 1)
        l1o = ph.ring('l1o', [128, 4, 512], BF16, 1)
        F = lambda nm, k=1: ph.ring(nm, [128, 512], F32, k)
        B = lambda nm, k=1: ph.ring(nm, [128, 512], BF16, k)
        rsb_r, ksb_r, vsb_r, kk_r = F('rsb'), F('ksb'), F('vsb'), F('kk')
        vbf_r = B('vbf')
        sg_r, a_r, G_r, Ge_r, Te_r, Hi_r = F('sg', 2), F('a_', 2), F('G', 2), F('Ge', 2), F('Te', 2), F('Hi', 1)
        E_r = F('E', 5)
        kd_r, bd_r, tmp_r = F('kd', 2), F('bd', 2), F('tmp', 4)
        sq_r = B('sq', 2)
        fst = ph.ring('fst', [128, 8, 512], BF16, 1)
        kb_r = B('kb', 4)
        tms = ph.ring('tms', [128, 5, 4, 512], BF16, 1)
        ost = ph.ring('ost', [128, 2, 512], BF16, 2)
        gcs = ph.ring('gcs', [128, 2, 8], F32, 2)
        pp_r = ph.ring('pp', [128, 512], F32, 6, psum=True)
        pt_r = ph.ring('ptr', [128, 4, 128], BF16, 2, psum=True)
        for s_ in range(S):
            for (seg, t0, n, g0) in cfg.tiles(512):
                nb = n // 128
                nch = n // 64
                hb = hr.next()
                c0 = cfg.pcol(seg, t0, 2)
                self.ld(hb, hb[:, :, 0:n + 4], self.hT, self.hT[s_, :, c0 - 2:c0 + n + 2].rearrange("(kc p) t -> p kc t", p=128))
                self.fl()
                hs = hsr.next()
                self.tt('dve', hs, hs[:, :, 0:n], hb, hb[:, :, 1:1 + n], hb, hb[:, :, 3:3 + n], ALU.add)
                hc = lambda kc: hb[:, kc, 2:2 + n]

                def proj(p, pap, wA, wB, col, m):
                    for kc in range(8):
                        self.mm(p, pap, wA, wA[:, kc, col:col + m], hb, hc(kc), start=(kc == 0), stop=False)
                    for kc in range(8):
                        self.mm(p, pap, wB, wB[:, kc, col:col + m], hs, hs[:, kc, 0:n], start=False, stop=(kc == 7))
                lo = l1o.next()
                for i, (col, m, fn) in enumerate(((0, 128, AF.Tanh), (128, 128, AF.Copy), (256, 128, AF.Sigmoid), (384, 32, AF.Sigmoid))):
                    p = pp_r.next()
                    proj(p, p[0:m, 0:n], l1c, l1s, col, m)
                    self.act(lo, lo[0:m, i, 0:n], p, p[0:m, 0:n], fn)
                self.chk('rp_l1')
                tm = tms.next()
                for c in range(4):
                    cs_ = slice(c * 128, (c + 1) * 128)
                    pr, pk, pv = pp_r.next(), pp_r.next(), pp_r.next()
                    proj(pr, pr[:, 0:n], wc, ws, c * 128, 128)
                    proj(pk, pk[:, 0:n], wc, ws, 512 + c * 128, 128)
                    proj(pv, pv[:, 0:n], wc, ws, 1024 + c * 128, 128)
                    self.chk('rp_p')
                    rsb, ksb, vsb, vbf = rsb_r.next(), ksb_r.next(), vsb_r.next(), vbf_r.next()
                    self.cp('act', rsb, rsb[:, 0:n], pr, pr[:, 0:n])
                    self.cp('act', ksb, ksb[:, 0:n], pk, pk[:, 0:n])
                    self.cp('act', vsb, vsb[:, 0:n], pv, pv[:, 0:n])
                    self.cp('dve', vbf, vbf[:, 0:n], pv, pv[:, 0:n])
                    self.chk('rp_cp')
                    kk = kk_r.next()
                    self.ts('dve', kk, kk[:, 0:n], ksb, ksb[:, 0:n], pp[:, 4, c:c + 1], extra=[pp])
                    self.chk('rp_ts')
                    sq = sq_r.next()
                    self.act(sq, sq[:, 0:n], kk, kk[:, 0:n], AF.Square)
                    pss = pp_r.next()
                    self.mm(pss, pss[:, 0:n], self.onesblk, self.onesblk[:], sq, sq[:, 0:n])
                    t1 = tmp_r.next()
                    self.act(t1, t1[:, 0:n], pss, pss[:, 0:n], AF.Ln, bias=self.epsc[:, 0:1], extra=[self.epsc])
                    t2 = tmp_r.next()
                    self.act(t2, t2[:, 0:n], t1, t1[:, 0:n], AF.Exp, scale=-0.5)
                    self.tt('dve', kk, kk[:, 0:n], kk, kk[:, 0:n], t2, t2[:, 0:n], ALU.mult)
                    self.chk('rp_kk')
                    fs = fst.next()
                    ksum = None
                    gc = gcs.next()
                    for d in range(2):
                        b0 = 64 * d
                        pw, pa = pp_r.next(), pp_r.next()
                        self.mm(pw, pw[:, 0:n], w2w, w2w[b0:b0 + 64, 0, cs_], lo, lo[b0:b0 + 64, 0, 0:n])
                        self.mm(pa, pa[:, 0:n], w2a, w2a[b0:b0 + 64, 0, cs_], lo, lo[b0:b0 + 64, 1, 0:n])
                        sg, a_ = sg_r.next(), a_r.next()
                        self.act(sg, sg[:, 0:n], pw, pw[:, 0:n], AF.Sigmoid, bias=pp[:, 0 + d, c:c + 1], extra=[pp])
                        self.act(a_, a_[:, 0:n], pa, pa[:, 0:n], AF.Sigmoid, bias=pp[:, 2 + d, c:c + 1], extra=[pp])
                        G, Ge, Te = G_r.next(), Ge_r.next(), Te_r.next()
                        s.op('dve', lambda e: e.tensor_tensor_scan(out=G[:, 0:n], data0=cmask[:, 0:n], data1=sg[:, 0:n],
                                                                    initial=0.0, op0=ALU.mult, op1=ALU.add), [cmask, sg], [G])
                        self.tt('dve', Ge, Ge[:, 0:n], G, G[:, 0:n], sg, sg[:, 0:n], ALU.subtract)
                        Gv = G[:, 0:n].rearrange("p (c t) -> p c t", t=64)
                        self.tt('dve', Te, Te[:, 0:n].rearrange("p (c t) -> p c t", t=64), G,
                                Gv[:, :, 63:64].to_broadcast([128, nch, 64]), G, Gv, ALU.subtract)
                        self.act(gc, gc[:, d, 0:nch], G, Gv[:, :, 63], AF.Exp, scale=-CDEC)
                        if d == 0:
                            inc, exc, toend = G, Ge, Te
                        else:
                            Hi = Hi_r.next()
                            self.tt('dve', Hi, Hi[:, 0:n], Te, Te[:, 0:n], sg, sg[:, 0:n], ALU.add)
                            inc, exc, toend = Hi, Te, Ge
                        E1, E2, E3, E4 = E_r.next(), E_r.next(), E_r.next(), E_r.next()
                        self.act(E1, E1[:, 0:n], inc, inc[:, 0:n], AF.Exp, scale=-CDEC)
                        self.act(E2, E2[:, 0:n], inc, inc[:, 0:n], AF.Exp, scale=CDEC)
                        self.act(E3, E3[:, 0:n], exc, exc[:, 0:n], AF.Exp, scale=-CDEC)
                        self.act(E4, E4[:, 0:n], toend, toend[:, 0:n], AF.Exp, scale=-CDEC)
                        self.chk('rp_exp')
                        kd, bd = kd_r.next(), bd_r.next()
                        self.ts('dve', kd, kd[:, 0:n], a_, a_[:, 0:n], pp[:, 5, c:c + 1], omka[:, c:c + 1], op0=ALU.mult, op1=ALU.add,
                                extra=[pp, omka])
                        self.tt('dve', kd, kd[:, 0:n], kd, kd[:, 0:n], ksb, ksb[:, 0:n], ALU.mult)
                        self.tt('dve', bd, bd[:, 0:n], kk, kk[:, 0:n], a_, a_[:, 0:n], ALU.mult)
                        self.tt('dve', fs, fs[:, d * 4 + 0, 0:n], rsb, rsb[:, 0:n], E1, E1[:, 0:n], ALU.mult)
                        self.tt('dve', fs, fs[:, d * 4 + 1, 0:n], kd, kd[:, 0:n], E2, E2[:, 0:n], ALU.mult)
                        self.tt('dve', fs, fs[:, d * 4 + 2, 0:n], bd, bd[:, 0:n], E2, E2[:, 0:n], ALU.mult)
                        self.stt(fs, fs[:, d * 4 + 3, 0:n], kk, kk[:, 0:n], -1.0, E3, E3[:, 0:n], ALU.mult, ALU.mult)
                        ke, be = kb_r.next(), kb_r.next()
                        self.tt('dve', ke, ke[:, 0:n], kd, kd[:, 0:n], E4, E4[:, 0:n], ALU.mult)
                        self.tt('dve', be, be[:, 0:n], bd, bd[:, 0:n], E4, E4[:, 0:n], ALU.mult)
                        self.chk('rp_fs')
                        for ti, src in ((2 * d, ke), (2 * d + 1, be)):
                            pt = pt_r.next()
                            for tb in range(nb):
                                s.op('pe', lambda e: e.transpose(pt[:, tb, :], src[:, tb * 128:(tb + 1) * 128], self.ident[:]),
                                     [src, self.ident], [pt])
                            self.cp('act', tm, tm[:, ti, 0:nb, cs_], pt, pt[:, 0:nb, :])
                        if d == 0:
                            ksum = tmp_r.next()
                            self.cp('dve', ksum, ksum[:, 0:n], kd, kd[:, 0:n])
                        else:
                            self.tt('dve', ksum, ksum[:, 0:n], ksum, ksum[:, 0:n], kd, kd[:, 0:n], ALU.add)
                    pt = pt_r.next()
                    for tb in range(nb):
                        s.op('pe', lambda e: e.transpose(pt[:, tb, :], vbf[:, tb * 128:(tb + 1) * 128], self.ident[:]),
                             [vbf, self.ident], [pt])
                    self.cp('act', tm, tm[:, 4, 0:nb, cs_], pt, pt[:, 0:nb, :])
                    sq2 = sq_r.next()
                    self.stt(sq2, sq2[:, 0:n], ksum, ksum[:, 0:n], pp[:, 6, c:c + 1], rsb, rsb[:, 0:n], ALU.mult, ALU.mult, extra=[pp])
                    pb = pp_r.next()
                    self.mm(pb, pb[:, 0:n], self.onesblk, self.onesblk[:], sq2, sq2[:, 0:n])
                    os_ = ost.next()
                    self.tt('dve', os_, os_[:, 0, 0:n], vsb, vsb[:, 0:n], pb, pb[:, 0:n], ALU.mult)
                    pg = pp_r.next()
                    self.mm(pg, pg[:, 0:n], g2a, g2a[:, 0, cs_], lo, lo[:, 2, 0:n], start=True, stop=False)
                    self.mm(pg, pg[:, 0:n], g2b, g2b[:, 0, cs_], lo, lo[0:32, 3, 0:n], start=False, stop=True)
                    self.cp('act', os_, os_[:, 1, 0:n], pg, pg[:, 0:n])
                    self.chk('rp_c0')
                    for d in range(2):
                        self.st(self.rw_fm, self.rw_fm[s_, d, :, c * 128:(c + 1) * 128, g0:g0 + n].rearrange("k p t -> p k t"),
                                fs, fs[:, d * 4:(d + 1) * 4, 0:n], key=d)
                    self.st(self.rw_bg, self.rw_bg[s_, :, c * 128:(c + 1) * 128, g0:g0 + n].rearrange("k p t -> p k t"), os_, os_[:, :, 0:n])
                    self.st(self.rw_gc, self.rw_gc[s_, :, c * 128:(c + 1) * 128, g0 // 64:g0 // 64 + nch].rearrange("d p c -> p d c"),
                            gc, gc[:, :, 0:nch])
                for ti in range(5):
                    self.st(self.rw_tm, self.rw_tm[s_, ti, g0:g0 + n, :].rearrange("(tb p) f -> p tb f", p=128), tm, tm[:, ti, 0:nb, :], key=ti)
        ph.close()

    def dplr_scan(self, spec):
        cfg = self.cfg
        s = self.s
        H, Kd, Vd = spec['H'], spec['Kd'], spec['Vd']
        NCC, NCL = cfg.TC // 64, cfg.TL // 64
        NC = NCC + NCL
        ph = Phase(s)
        kinds = spec['kinds']
        nk = spec['nfm']
        ki = spec['ki']
        ntm = spec['ntm']
        ti_ = spec['ti']
        chains = spec['chains']
        nchain = len(chains)
        masks = ph.sb('masks', [64, 2, 128], F32)
        self.ld(masks, masks[:], self.inp['c_scanmask'], self.inp['c_scanmask'][:, :, :])
        identf = ph.sb('identf', [64, 64], F32)
        self.ld(identf, identf[:], self.inp['c_ident64'], self.inp['c_ident64'][:, :])
        fm_r = [ph.ring(f'fm{i}', [Kd, nk, H, 64], BF16, 2) for i in range(nchain)]
        tm_r = [ph.ring(f'tm{i}', [64, ntm, H * max(Kd, Vd)], BF16, 2) for i in range(nchain)]
        dm_r = [ph.ring(f'dm{i}', [64, H, 128], BF16, 2) for i in range(nchain)] if spec.get('dmat') else None
        gc_t = [ph.sb(f'gc{i}', [Kd, H, NC], F32) for i in range(nchain)]
        S_t = [ph.sb(f'S{i}', [Kd, H, Vd], F32) for i in range(nchain)]
        Sb_t = [ph.sb(f'Sb{i}', [Kd, H, Vd], BF16) for i in range(nchain)]
        M_r = [ph.ring(f'Msb{i}', [64, H, 2, 128], BF16, 2) for i in range(nchain)]
        sq_r = [ph.ring(f'sqm{i}', [64, H, 64], F32, 5) for i in range(nchain)]
        X_r = [ph.ring(f'X{i}', [64, H, 64], F32, 3) for i in range(nchain)]
        off_r = [ph.ring(f'off{i}', [64, H, 64], F32, 1) for i in range(nchain)]
        bm = ph.sb('bm', [64, 2, 64], F32)
        self.ld(bm, bm[:], self.inp['c_blkmask'], self.inp['c_blkmask'][:, :, :])
        Xf_r = [ph.ring(f'Xf{i}', [64, H, 64], BF16, 2) for i in range(nchain)]
        wt_r = ph.ring('wt', [64, H, Vd], BF16, nchain)
        u_r = ph.ring('u', [64, H, Vd], BF16, nchain)
        y_r = ph.ring('y', [Vd, H, 64], F32, 2)
        pM = ph.ring('pM', [64, 2, 2, 128], F32, 2, psum=True)
        pG = ph.ring('pG', [128, 512], F32, 6, psum=True)
        hv = lambda p: p[0:64, 0:H * 64].rearrange("p (h t) -> p h t", h=H)
        wv = lambda p: p[0:64, 0:H * Vd].rearrange("p (h t) -> p h t", h=H)
        yv = lambda p: p[0:Vd, 0:H * 64].rearrange("p (h t) -> p h t", h=H)
        sv = lambda p: p[0:Kd, 0:H * Vd].rearrange("p (h t) -> p h t", h=H)
        for i, (s_, d) in enumerate(chains):
            spec['load_gc'](gc_t[i], s_, d)
            s.op('dve', lambda e: e.memset(S_t[i][:], 0.0), [], [S_t[i]])
            s.op('dve', lambda e: e.memset(Sb_t[i][:], 0.0), [], [Sb_t[i]])
        order = {0: list(range(NC)), 1: list(range(NCC - 1, -1, -1)) + list(range(NC - 1, NCC - 1, -1))}
        for step in range(NC):
            cur = []
            for i, (s_, d) in enumerate(chains):
                ci = order[d][step]
                fm = fm_r[i].next()
                tm = tm_r[i].next()
                spec['load_fm'](fm, s_, d, ci)
                spec['load_tm'](tm, s_, d, ci)
                dm = None
                if dm_r is not None:
                    dm = dm_r[i].next()
                    spec['load_dm'](dm, s_, d, ci)
                cur.append((i, s_, d, ci, fm, tm, dm))
            self.fl()
            st = {}
            def stage_a(i, s_, d, ci, fm, tm, dm):
                Msb = M_r[i].next()
                AR = lambda h: fm[:, ki['A']:ki['A'] + 2, h, :]
                for h0 in range(0, H, 2):
                    p = pM.next()
                    for hh in range(2):
                        h = h0 + hh
                        self.mm(p, p[:, hh, 0, :].rearrange("p (a t) -> p a t", a=2), fm, fm[:, ki['B'], h, :], fm, AR(h))
                        self.mm(p, p[:, hh, 1, :].rearrange("p (a t) -> p a t", a=2), fm, fm[:, ki['K'], h, :], fm, AR(h))
                    if dm is None:
                        self.tt('dve', Msb, Msb[:, h0:h0 + 2, :, :], p, p[:, :, :, :], masks,
                                masks[:, d, :].unsqueeze(1).unsqueeze(1).to_broadcast([64, 2, 2, 128]), ALU.mult)
                    else:
                        self.tt('dve', Msb, Msb[:, h0:h0 + 2, :, :], p, p[:, :, :, :], dm,
                                dm[:, h0:h0 + 2, :].unsqueeze(2).to_broadcast([64, 2, 2, 128]), ALU.mult)
                yield
                bmb = bm[:, 0, :].unsqueeze(1).to_broadcast([64, H, 64])
                bmcb = bm[:, 1, :].unsqueeze(1).to_broadcast([64, H, 64])
                idb = identf[:].unsqueeze(1).to_broadcast([64, H, 64])
                P0 = sq_r[i].next()
                self.cp('act', P0, P0[:], Msb, Msb[:, :, 0, 0:64])
                pt = pG.next()
                for h in range(H):
                    self.mmr(pt, hv(pt)[:, h, :], P0, P0[:, h, :], identf, identf[:, :])
                PT = sq_r[i].next()
                self.tt('dve', PT, PT[:], pt, hv(pt), bm, bmb, ALU.mult)
                PToff = off_r[i].next()
                self.tt('dve', PToff, PToff[:], pt, hv(pt), bm, bmcb, ALU.mult)
                P = sq_r[i].next()
                self.tt('dve', P, P[:], P0, P0[:], bm, bmb, ALU.mult)
                X = X_r[i].next()
                self.tt('dve', X, X[:], P, P[:], identf, idb, ALU.add)
                yield
                for lev in range(1, 5):
                    last = lev == 4
                    if not last:
                        p2 = pG.next()
                        p2v = hv(p2)
                        for h in range(H):
                            self.mmr(p2, p2v[:, h, :], PT, PT[:, h, :], P, P[:, h, :])
                    p2t = pG.next()
                    p2tv = hv(p2t)
                    for h in range(H):
                        self.mmr(p2t, p2tv[:, h, :], P, P[:, h, :], PT, PT[:, h, :])
                    nPT = sq_r[i].next()
                    self.cp('act', nPT, nPT[:], p2t, p2tv)
                    if not last:
                        nP = sq_r[i].next()
                        self.cp('act', nP, nP[:], p2, p2v)
                        P = nP
                    PT = nPT
                    px = pG.next()
                    pxv = hv(px)
                    for h in range(H):
                        self.mmr(px, pxv[:, h, :], PT, PT[:, h, :], X, X[:, h, :])
                    nX = X_r[i].next()
                    self.tt('dve', nX, nX[:], X, X[:], px, pxv, ALU.add)
                    X = nX
                    yield
                pxt = pG.next()
                for h in range(H):
                    self.mmr(pxt, hv(pxt)[:, h, :], X, X[:, h, :], identf, identf[:, :])
                XT = X_r[i].next()
                self.cp('act', XT, XT[:], pxt, hv(pxt))
                p1 = pG.next()
                for h in range(H):
                    self.mmr(p1, hv(p1)[:, h, :], PToff, PToff[:, h, :], X, X[:, h, :])
                T1 = X_r[i].next()
                self.cp('act', T1, T1[:], p1, hv(p1))
                yield
                p2_ = pG.next()
                for h in range(H):
                    self.mmr(p2_, hv(p2_)[:, h, :], XT, XT[:, h, :], T1, T1[:, h, :])
                Xf = Xf_r[i].next()
                self.tt('dve', Xf, Xf[:], X, X[:], p2_, hv(p2_), ALU.add)
                X = Xf
                st[i] = (Msb, X)
                if step == 0 and i == 0 and 'dbg_scan' in cfg.debug:
                    self.st(self.dbgT, self.dbgT[:, 0, 0:H * 256], Msb, Msb[:].rearrange("p h a t -> p (h a t)"))
                    self.st(self.dbgT, self.dbgT[:, 1, 0:H * 64], X, X[:].rearrange("p h t -> p (h t)"))
                    self.st(self.dbgT, self.dbgT[:, 4, 0:H * 64], PT0, PT0[:].rearrange("p h t -> p (h t)"))
            gens = [stage_a(*c) for c in cur]
            alive = list(gens)
            while alive:
                nxt = []
                for g_ in alive:
                    try:
                        next(g_)
                        nxt.append(g_)
                    except StopIteration:
                        pass
                alive = nxt
            wts = {}
            for (i, s_, d, ci, fm, tm, dm) in cur:
                Msb, X = st[i]
                p = pG.next()
                for h in range(H):
                    self.mm(p, wv(p)[:, h, :], fm, fm[:, ki['As'], h, :], Sb_t[i], Sb_t[i][:, h, :], start=True, stop=False)
                    self.mm(p, wv(p)[:, h, :], Msb, Msb[:, h, 1, 0:64], tm, tm[:, ti_['V'], h * Vd:(h + 1) * Vd], start=False, stop=True)
                wt = wt_r.next()
                self.cp('act', wt, wt[:], p, wv(p))
                wts[i] = wt
                if step == 0 and i == 0 and 'dbg_scan' in cfg.debug:
                    self.st(self.dbgT, self.dbgT[:, 2, 0:H * Vd], wt, wt[:].rearrange("p h t -> p (h t)"))
            us = {}
            for (i, s_, d, ci, fm, tm, dm) in cur:
                Msb, X = st[i]
                p = pG.next()
                for h in range(H):
                    self.mm(p, wv(p)[:, h, :], X, X[:, h, :], wts[i], wts[i][:, h, :])
                u = u_r.next()
                self.cp('dve', u, u[:], p, wv(p))
                us[i] = u
                if step == 0 and i == 0 and 'dbg_scan' in cfg.debug:
                    self.st(self.dbgT, self.dbgT[:, 3, 0:H * Vd], u, u[:].rearrange("p h t -> p (h t)"))
            for (i, s_, d, ci, fm, tm, dm) in cur:
                Msb, X = st[i]
                u = us[i]
                p = pG.next()
                for h in range(H):
                    V_h = tm[:, ti_['V'], h * Vd:(h + 1) * Vd]
                    self.mm(p, yv(p)[:, h, :], Sb_t[i], Sb_t[i][:, h, :], fm, fm[:, ki['Rs'], h, :], start=True, stop=False)
                    self.mm(p, yv(p)[:, h, :], u, u[:, h, :], Msb, Msb[:, h, 0, 64:128], start=False, stop=False)
                    self.mm(p, yv(p)[:, h, :], tm, V_h, Msb, Msb[:, h, 1, 64:128], start=False, stop=True)
                y = y_r.next()
                self.cp('act', y, y[:], p, yv(p))
                spec['store_y'](y, s_, d, ci)
                p = pG.next()
                for h in range(H):
                    V_h = tm[:, ti_['V'], h * Vd:(h + 1) * Vd]
                    self.mm(p, sv(p)[:, h, :], tm, tm[:, ti_['Bend'], h * Kd:(h + 1) * Kd], u, u[:, h, :], start=True, stop=False)
                    self.mm(p, sv(p)[:, h, :], tm, tm[:, ti_['Kend'], h * Kd:(h + 1) * Kd], tm, V_h, start=False, stop=True)
                S = S_t[i]
                self.tt('dve', S, S[:], S, S[:], gc_t[i], gc_t[i][:, :, ci:ci + 1].to_broadcast([Kd, H, Vd]), ALU.mult)
                self.tt('dve', S, S[:], S, S[:], p, sv(p), ALU.add)
                self.cp('act', Sb_t[i], Sb_t[i][:], S, S[:])
        ph.close()

    def phase_rwkv_scan(self, l):
        cfg = self.cfg
        S = cfg.S

        def load_fm(fm, s_, d, ci):
            for k in range(4):
                self.ld(fm, fm[:, k, :, :], self.rw_fm, self.rw_fm[s_, d, k, :, ci * 64:(ci + 1) * 64].rearrange("(h k) t -> k h t", k=64), key=k)

        def load_tm(tm, s_, d, ci):
            for j, ti in enumerate((2 * d, 2 * d + 1, 4)):
                self.ld(tm, tm[:, j, :], self.rw_tm, self.rw_tm[s_, ti, ci * 64:(ci + 1) * 64, :], key=j)

        def load_gc(gc, s_, d):
            self.ld(gc, gc[:], self.rw_gc, self.rw_gc[s_, d, :, :].rearrange("(h k) c -> k h c", k=64))

        def store_y(y, s_, d, ci):
            self.st(self.rw_y, self.rw_y[s_, d, :, ci * 64:(ci + 1) * 64].rearrange("(h v) t -> v h t", v=64), y, y[:])

        def load_fm2(fm, s_, d, ci):
            for j, k in enumerate((1, 2, 3, 0)):
                self.ld(fm, fm[:, j, :, :], self.rw_fm, self.rw_fm[s_, d, k, :, ci * 64:(ci + 1) * 64].rearrange("(h k) t -> k h t", k=64), key=j)
        spec = dict(H=8, Kd=64, Vd=64, kinds=None, nfm=4, ki=dict(K=0, B=1, A=2, R=3, As=2, Rs=3), ntm=3,
                    ti=dict(Kend=0, Bend=1, V=2), chains=[(s_, d) for s_ in range(S) for d in range(2)],
                    load_fm=load_fm2, load_tm=load_tm, load_gc=load_gc, store_y=store_y)
        self.dplr_scan(spec)

    def phase_rwkv_post(self, l):
        cfg = self.cfg
        S = cfg.S
        ph = Phase(self.s)
        pp = ph.sb('pp', [128, 9, 4], F32)
        self.ld(pp, pp[:], self.inp['rwkv_pT'], self.inp['rwkv_pT'][:, l, :, :])
        lneps = ph.sb('lneps', [128, 1], F32)
        self.s.op('dve', lambda e: e.memset(lneps[:], 64e-5), [], [lneps])
        yr = ph.ring('yy', [128, 2, 4, 512], F32, 2)
        bgr = ph.ring('bg', [128, 2, 4, 512], BF16, 2)
        osr = ph.ring('os', [128, 4, 512], BF16, 2)
        o_r = ph.ring('o', [128, 512], F32, 2)
        ob_r = ph.ring('ob', [128, 512], BF16, 2)
        t_r = ph.ring('t', [128, 512], F32, 4)
        pm = ph.ring('pm', [128, 512], F32, 2, psum=True)
        pv = ph.ring('pv', [128, 512], F32, 2, psum=True)
        for s_ in range(S):
            for (seg, t0, n, g0) in cfg.tiles(512):
                if seg == 0 and l == cfg.L - 1:
                    continue
                yy = yr.next()
                for d in range(2):
                    self.ld(yy, yy[:, d, :, 0:n], self.rw_y, self.rw_y[s_, d, :, g0:g0 + n].rearrange("(c p) t -> p c t", p=128), key=d)
                bg = bgr.next()
                for k in range(2):
                    self.ld(bg, bg[:, k, :, 0:n], self.rw_bg, self.rw_bg[s_, k, :, g0:g0 + n].rearrange("(c p) t -> p c t", p=128), key=k)
                self.fl()
                os_ = osr.next()
                for c in range(4):
                    o = o_r.next()
                    self.tt('dve', o, o[:, 0:n], yy, yy[:, 0, c, 0:n], yy, yy[:, 1, c, 0:n], ALU.add)
                    ob = ob_r.next()
                    self.cp('act', ob, ob[:, 0:n], o, o[:, 0:n])
                    p = pm.next()
                    self.mm(p, p[:, 0:n], self.onesblk, self.onesblk[:], ob, ob[:, 0:n])
                    mt = t_r.next()
                    self.act(mt, mt[:, 0:n], p, p[:, 0:n], AF.Copy, scale=-1.0 / 64)
                    self.tt('dve', o, o[:, 0:n], o, o[:, 0:n], mt, mt[:, 0:n], ALU.add)
                    sq = ob_r.next()
                    self.act(sq, sq[:, 0:n], o, o[:, 0:n], AF.Square)
                    p2 = pv.next()
                    self.mm(p2, p2[:, 0:n], self.onesblk, self.onesblk[:], sq, sq[:, 0:n])
                    t1 = t_r.next()
                    self.act(t1, t1[:, 0:n], p2, p2[:, 0:n], AF.Ln, bias=lneps[:, 0:1], scale=1.0 / 64, extra=[lneps])
                    t2 = t_r.next()
                    self.act(t2, t2[:, 0:n], t1, t1[:, 0:n], AF.Exp, scale=-0.5)
                    self.stt(o, o[:, 0:n], o, o[:, 0:n], pp[:, 7, c:c + 1], t2, t2[:, 0:n], ALU.mult, ALU.mult, extra=[pp])
                    self.stt(o, o[:, 0:n], o, o[:, 0:n], pp[:, 8, c:c + 1], bg, bg[:, 0, c, 0:n], ALU.add, ALU.add, extra=[pp])
                    self.tt('dve', os_, os_[:, c, 0:n], o, o[:, 0:n], bg, bg[:, 1, c, 0:n], ALU.mult)
                self.st(self.yaT, self.yaT[s_, :, g0:g0 + n].rearrange("(c p) t -> p c t", p=128), os_, os_[:, :, 0:n])
        ph.close()

    def phase_gdn_proj(self, l):
        cfg = self.cfg
        S = cfg.S
        s = self.s
        ph = Phase(s)
        self.wstage = ph.ring('wstg', [128, 8, 256], F32, 2)
        r = lambda ap: ap.rearrange("(kc p) c -> p kc c", p=128)
        win = self.inp['w_in']
        wu = ph.sb('wu', [128, 8, 2048], BF16)
        self.load_w(ph, wu, lambda a, b: wu[:, :, a:b], win, lambda a, b: r(win[l, :, 1536 + a:1536 + b]), 2048, 8, piece=256)
        wab = ph.sb('wab', [128, 8, 16], BF16)
        for j, (nm, d) in enumerate((('gdn_w_alpha', 0), ('gdn_w_alpha', 1), ('gdn_w_beta', 0), ('gdn_w_beta', 1))):
            src = self.inp[nm]
            self.load_w(ph, wab, lambda a, b, j=j: wab[:, :, 4 * j + a:4 * j + b], src, lambda a, b, src=src, d=d: r(src[l, d, :, a:b]), 4, 8, piece=256)
        hr = ph.ring('hb', [128, 8, 512], BF16, 2)
        ust = ph.ring('ust', [128, 16, 512], BF16, 2)
        abr = ph.ring('ab', [16, 512], F32, 2)
        pu = ph.ring('pu', [128, 512], F32, 4, psum=True)
        for s_ in range(S):
            for (seg, t0, n, g0) in cfg.tiles(512):
                hb = hr.next()
                c0 = cfg.pcol(seg, t0, 2)
                self.ld(hb, hb[:, :, 0:n], self.hT, self.hT[s_, :, c0:c0 + n].rearrange("(kc p) t -> p kc t", p=128))
                self.fl()
                us = ust.next()
                for cc in range(16):
                    p = pu.next()
                    for kc in range(8):
                        self.mm(p, p[:, 0:n], wu, wu[:, kc, cc * 128:(cc + 1) * 128], hb, hb[:, kc, 0:n], start=(kc == 0), stop=(kc == 7))
                    if cc < 12:
                        self.cp('act' if cc % 2 else 'dve', us, us[:, cc, 0:n], p, p[:, 0:n])
                    else:
                        self.act(us, us[:, cc, 0:n], p, p[:, 0:n], AF.Silu)
                p = pu.next()
                for kc in range(8):
                    self.mm(p, p[0:16, 0:n], wab, wab[:, kc, :], hb, hb[:, kc, 0:n], start=(kc == 0), stop=(kc == 7))
                ab = abr.next()
                self.cp('act', ab, ab[:, 0:n], p, p[0:16, 0:n])
                self.st(self.gd_u, self.gd_u[s_, :, c0:c0 + n].rearrange("(c p) t -> p c t", p=128), us, us[:, 0:12, 0:n], key=0)
                self.st(self.gd_z, self.gd_z[s_, :, g0:g0 + n].rearrange("(c p) t -> p c t", p=128), us, us[:, 12:16, 0:n], key=1)
                self.st(self.gd_ab, self.gd_ab[s_, :, g0:g0 + n], ab, ab[:, 0:n])
        ph.close()

    def phase_gdn_prep(self, l):
        cfg = self.cfg
        S = cfg.S
        s = self.s
        ph = Phase(s)
        cw = ph.sb('cw', [128, 12, 5], F32)
        self.ld(cw, cw[:], self.inp['gdn_convT'], self.inp['gdn_convT'][:, l, :, :])
        dg = ph.sb('dg', [128, 12, 5, 128], BF16)
        for cc in range(12):
            for j in range(5):
                self.ts('dve', dg, dg[:, cc, j, :], self.ident, self.ident[:], cw[:, cc, j:j + 1], extra=[cw])
        rowp = ph.sb('rowp', [16, 4], F32)
        self.ld(rowp, rowp[:, 0:3], self.inp['gdn_rowp'], self.inp['gdn_rowp'][:, l, :])
        self.act(rowp, rowp[:, 3:4], rowp, rowp[:, 1:2], AF.Exp)
        self.ts('dve', rowp, rowp[:, 3:4], rowp, rowp[:, 3:4], -1.0)
        selb = ph.sb('selb', [16, 16, 128], BF16)
        self.ld(selb, selb[:], self.inp['c_selb'], self.inp['c_selb'][:, :, :])
        self32 = ph.sb('self32', [16, 16, 64], F32)
        self.ld(self32, self32[:], self.inp['c_self'], self.inp['c_self'][:, :, :])
        id16 = ph.sb('id16', [16, 16], F32)
        self.ld(id16, id16[:], self.inp['c_ident64'], self.inp['c_ident64'][0:16, 0:16])
        nmask = ph.sb('nmask', [64, 2, 2, 64], F32)
        self.ld(nmask, nmask[:], self.inp['c_negmask'], self.inp['c_negmask'][:, :, :, :])
        cm16 = ph.sb('cm16', [16, 512], F32)
        self.ld(cm16, cm16[:], self.inp['c_chunkmask'], self.inp['c_chunkmask'][0:16, :])
        one16 = ph.sb('one16', [16, 1], F32)
        s.op('dve', lambda e: e.memset(one16[:], 1.0), [], [one16])
        ur = ph.ring('ub', [128, 12, 516], BF16, 2)
        abr = ph.ring('ab', [16, 512], F32, 2)
        R16 = lambda nm, k=1: ph.ring(nm, [16, 512], F32, k)
        e_r, g_r, G_r, Te_r, Hi_r, Gi_r, ga_r, ee_r, be_r = (R16('e'), R16('g'), R16('G'), R16('Te'), R16('Hi'), R16('Gi'),
                                                            R16('ga'), R16('ee'), R16('be'))
        rb_r = ph.ring('rb', [16, 2, 512], BF16, 1)
        gcr = ph.ring('gcr', [16, 8], F32, 2)
        grep_r = ph.ring('grep', [64, 8, 512], F32, 1)
        gcol_r = ph.ring('gcol', [64, 8, 16], F32, 1)
        tcol_r = ph.ring('tcol', [128, 2, 4, 16], F32, 1)
        dd_r = ph.ring('dd', [64, 4, 2, 64], F32, 2)
        dm_r = ph.ring('dmo', [64, 4, 2, 64], BF16, 3)
        qkv_r = ph.ring('qkv', [128, 512], F32, 3)
        kn_r = ph.ring('kn', [128, 512], BF16, 2)
        qn_r = ph.ring('qn', [128, 512], BF16, 2)
        vb_r = ph.ring('vb', [128, 512], BF16, 2)
        sq_r = ph.ring('sq', [128, 512], BF16, 2)
        t_r = ph.ring('t', [128, 512], F32, 3)
        rep_r = ph.ring('rep', [128, 2, 512], F32, 2)
        fst = ph.ring('fst', [128, 8, 512], BF16, 2)
        tms = ph.ring('tms', [128, 4, 4, 512], BF16, 1)
        pc = ph.ring('pc', [128, 512], F32, 5, psum=True)
        ptp = ph.ring('ptp', [128, 4, 128], BF16, 2, psum=True)
        for s_ in range(S):
            for (seg, t0, n, g0) in cfg.tiles(512):
                nb, nch = n // 128, n // 64
                ub = ur.next()
                c0 = cfg.pcol(seg, t0, 2)
                self.ld(ub, ub[:, :, 0:n + 4], self.gd_u, self.gd_u[s_, :, c0 - 2:c0 + n + 2].rearrange("(c p) t -> p c t", p=128))
                ab = abr.next()
                self.ld(ab, ab[:, 0:n], self.gd_ab, self.gd_ab[s_, :, g0:g0 + n])
                self.fl()
                e, g, G, Te, Hi, Gi, ga, ee, be = (x.next() for x in (e_r, g_r, G_r, Te_r, Hi_r, Gi_r, ga_r, ee_r, be_r))
                self.act(e, e[:, 0:n], ab, ab[:, 0:n], AF.Exp, bias=rowp[:, 0:1], extra=[rowp])
                self.act(g, g[:, 0:n], e, e[:, 0:n], AF.Ln, bias=one16[:, 0:1], extra=[one16])
                self.ts('dve', g, g[:, 0:n], g, g[:, 0:n], rowp[:, 3:4], extra=[rowp])
                self.act(be, be[:, 0:n], ab, ab[:, 0:n], AF.Sigmoid)
                s.op('dve', lambda e_: e_.tensor_tensor_scan(out=G[:, 0:n], data0=cm16[:, 0:n], data1=g[:, 0:n], initial=0.0,
                                                              op0=ALU.mult, op1=ALU.add), [cm16, g], [G])
                c3 = lambda t: t[:, 0:n].rearrange("p (c t) -> p c t", t=64)
                tot_b = c3(G)[:, :, 63:64].to_broadcast([16, nch, 64])
                self.tt('dve', Te, c3(Te), G, tot_b, G, c3(G), ALU.subtract)
                self.tt('dve', Hi, Hi[:, 0:n], Te, Te[:, 0:n], g, g[:, 0:n], ALU.add)
                self.tt('dve', Hi, Hi[:, 0:n], Hi, Hi[:, 0:n], G, G[:, 0:n], ALU.subtract)
                self.stt(Gi, Gi[:, 0:n], Hi, Hi[:, 0:n], rowp[:, 2:3], G, G[:, 0:n], ALU.mult, ALU.add, extra=[rowp])
                self.tt('dve', Te, c3(Te), G, tot_b, Gi, c3(Gi), ALU.subtract)
                self.act(ga, ga[:, 0:n], Gi, Gi[:, 0:n], AF.Exp)
                self.act(ee, ee[:, 0:n], Te, Te[:, 0:n], AF.Exp)
                gc = gcr.next()
                self.act(gc, gc[:, 0:nch], G, c3(G)[:, :, 63], AF.Exp)
                self.st(self.gd_gc, self.gd_gc[s_, :, g0 // 64:g0 // 64 + nch], gc, gc[:, 0:nch])
                rb = rb_r.next()
                self.cp('dve', rb, rb[:, 0, 0:n], ga, ga[:, 0:n])
                self.cp('dve', rb, rb[:, 1, 0:n], be, be[:, 0:n])
                grep = grep_r.next()
                for rr_ in range(8):
                    p = pc.next()
                    self.mm(p, p[0:64, 0:n], self32, self32[:, rr_, :], Gi, Gi[:, 0:n])
                    self.cp('act' if rr_ % 2 else 'dve', grep, grep[:, rr_, 0:n], p, p[0:64, 0:n])
                gcol = gcol_r.next()
                p = pc.next()
                for ch in range(nch):
                    self.mm(p, p[0:64, ch * 16:(ch + 1) * 16], Gi, Gi[:, ch * 64:(ch + 1) * 64], id16, id16[:, :])
                self.cp('act', gcol, gcol[:, 0:nch, :], p, p[0:64, 0:nch * 16].rearrange("p (c r) -> p c r", r=16))
                tcol = tcol_r.next()
                p = pc.next()
                for k_, src in enumerate((ee, be)):
                    for tb in range(nb):
                        self.mm(p, p[:, (k_ * 4 + tb) * 16:(k_ * 4 + tb + 1) * 16], src, src[:, tb * 128:(tb + 1) * 128], id16, id16[:, :])
                for k_ in range(2):
                    self.cp('act', tcol, tcol[:, k_, 0:nb, :], p, p[:, k_ * 64:k_ * 64 + nb * 16].rearrange("p (b r) -> p b r", r=16))
                for d in range(2):
                    for ch in range(nch):
                        dd = dd_r.next()
                        gsl = grep[:, d * 4:(d + 1) * 4, ch * 64:(ch + 1) * 64]
                        gcb = gcol[:, ch, d * 4:(d + 1) * 4].unsqueeze(2).to_broadcast([64, 4, 64])
                        self.tt('dve', dd, dd[:, :, 0, :], grep, gsl, gcol, gcb, ALU.subtract)
                        self.tt('dve', dd, dd[:, :, 1, :], dd, dd[:, :, 0, :], nmask,
                                nmask[:, d, 1, :].unsqueeze(1).to_broadcast([64, 4, 64]), ALU.add)
                        self.tt('dve', dd, dd[:, :, 0, :], dd, dd[:, :, 0, :], nmask,
                                nmask[:, d, 0, :].unsqueeze(1).to_broadcast([64, 4, 64]), ALU.add)
                        dm = dm_r.next()
                        self.act(dm, dm[:], dd, dd[:], AF.Exp)
                        self.st(self.gd_dm, self.gd_dm[s_, d, g0 // 64 + ch, :, :], dm, dm[:].rearrange("p h a t -> p (h a t)"))
                tm = tms.next()
                for h in range(4):
                    fs = fst.next()
                    outs3 = []
                    for grp in range(3):
                        cc = grp * 4 + h
                        p = pc.next()
                        for j in range(5):
                            self.mm(p, p[:, 0:n], dg, dg[:, cc, j, :], ub, ub[:, cc, j:j + n], start=(j == 0), stop=(j == 4))
                        o = qkv_r.next()
                        self.act(o, o[:, 0:n], p, p[:, 0:n], AF.Silu)
                        outs3.append(o)
                    q_, k_, v_ = outs3
                    kn, qn, vb = kn_r.next(), qn_r.next(), vb_r.next()
                    for src, dst, scl in ((q_, qn, 128 ** -0.5), (k_, kn, 1.0)):
                        sq = sq_r.next()
                        self.act(sq, sq[:, 0:n], src, src[:, 0:n], AF.Square)
                        p = pc.next()
                        self.mm(p, p[:, 0:n], self.ones128, self.ones128[:], sq, sq[:, 0:n])
                        t1 = t_r.next()
                        self.act(t1, t1[:, 0:n], p, p[:, 0:n], AF.Ln, bias=self.epsc[:, 0:1], extra=[self.epsc])
                        t2 = t_r.next()
                        self.act(t2, t2[:, 0:n], t1, t1[:, 0:n], AF.Exp, scale=-0.5)
                        self.stt(dst, dst[:, 0:n], src, src[:, 0:n], scl, t2, t2[:, 0:n], ALU.mult, ALU.mult)
                    self.cp('dve', vb, vb[:, 0:n], v_, v_[:, 0:n])
                    self.cp('act', fs, fs[:, 0, 0:n], kn, kn[:, 0:n])
                    self.cp('act', fs, fs[:, 1, 0:n], qn, qn[:, 0:n])
                    for d in range(2):
                        rep = rep_r.next()
                        for k2, (slot, row) in enumerate(((1, 8 + 4 * d + h), (0, 4 * d + h))):
                            p = pc.next()
                            self.mm(p, p[:, 0:n], selb, selb[:, row, :], rb, rb[:, slot, 0:n])
                            self.cp('act', rep, rep[:, k2, 0:n], p, p[:, 0:n])
                        b0 = 2 + 3 * d
                        self.stt(fs, fs[:, b0, 0:n], kn, kn[:, 0:n], -1.0, rep, rep[:, 0, 0:n], ALU.mult, ALU.mult)
                        self.tt('dve', fs, fs[:, b0 + 1, 0:n], fs, fs[:, b0, 0:n], rep, rep[:, 1, 0:n], ALU.mult)
                        self.tt('dve', fs, fs[:, b0 + 2, 0:n], qn, qn[:, 0:n], rep, rep[:, 1, 0:n], ALU.mult)
                    self.st(self.gd_fm, self.gd_fm[s_, :, h * 128:(h + 1) * 128, g0:g0 + n].rearrange("k p t -> p k t"), fs, fs[:, :, 0:n])
                    for src, base, slot in ((kn, 0, 0), (vb, 1, 1)):
                        pt = ptp.next()
                        for tb in range(nb):
                            s.op('pe', lambda e_: e_.transpose(pt[:, tb, :], src[:, tb * 128:(tb + 1) * 128], self.ident[:]),
                                 [src, self.ident], [pt])
                        for d in range(2):
                            for tb in range(nb):
                                row = (4 * d + h) if slot == 0 else (8 + 4 * d + h)
                                self.act(tm, tm[:, 2 * d + base, tb, h * 128:(h + 1) * 128], pt, pt[:, tb, :], AF.Copy,
                                         scale=tcol[:, slot, tb, row:row + 1], extra=[tcol])
                for ti in range(4):
                    self.st(self.gd_tm, self.gd_tm[s_, ti, g0:g0 + n, :].rearrange("(tb p) f -> p tb f", p=128), tm, tm[:, ti, 0:nb, :], key=ti)
        ph.close()

    def phase_gdn_scan(self, l):
        cfg = self.cfg
        S = cfg.S
        NC = cfg.TT // 64

        def load_fm(fm, s_, d, ci):
            for j, k in enumerate((0, 2 + 3 * d, 1, 3 + 3 * d, 4 + 3 * d)):
                self.ld(fm, fm[:, j, :, :], self.gd_fm, self.gd_fm[s_, k, :, ci * 64:(ci + 1) * 64].rearrange("(h k) t -> k h t", k=128), key=j)

        def load_tm(tm, s_, d, ci):
            for j in range(2):
                self.ld(tm, tm[:, j, :], self.gd_tm, self.gd_tm[s_, 2 * d + j, ci * 64:(ci + 1) * 64, :], key=j)

        def load_dm(dm, s_, d, ci):
            self.ld(dm, dm[:].rearrange("p h t -> p (h t)"), self.gd_dm, self.gd_dm[s_, d, ci, :, :])

        def load_gc(gc, s_, d):
            self.ld(gc, gc[:], self.gd_gc, self.gd_gc[s_:s_ + 1, d * 4:(d + 1) * 4, :].to_broadcast([128, 4, NC]))

        def store_y(y, s_, d, ci):
            self.st(self.gd_y, self.gd_y[s_, d, :, ci * 64:(ci + 1) * 64].rearrange("(h v) t -> v h t", v=128), y, y[:])
        spec = dict(H=4, Kd=128, Vd=128, kinds=None, nfm=5, ki=dict(K=0, B=0, A=1, R=2, As=3, Rs=4), ntm=2,
                    ti=dict(Kend=0, Bend=0, V=1), chains=[(s_, d) for s_ in range(S) for d in range(2)],
                    load_fm=load_fm, load_tm=load_tm, load_gc=load_gc, load_dm=load_dm, store_y=store_y, dmat=True)
        self.dplr_scan(spec)

    def phase_gdn_post(self, l):
        cfg = self.cfg
        S = cfg.S
        ph = Phase(self.s)
        gn = ph.sb('gn', [128, 1], F32)
        self.ld(gn, gn[:], self.inp['gdn_normT'], self.inp['gdn_normT'][l, :, :])
        yr = ph.ring('yy', [128, 2, 4, 512], F32, 2)
        zr = ph.ring('z', [128, 4, 512], BF16, 2)
        osr = ph.ring('os', [128, 4, 512], BF16, 2)
        o_r = ph.ring('o', [128, 512], F32, 2)
        sq_r = ph.ring('sq', [128, 512], BF16, 2)
        t_r = ph.ring('t', [128, 512], F32, 4)
        pm = ph.ring('pm', [128, 512], F32, 2, psum=True)
        for s_ in range(S):
            for (seg, t0, n, g0) in cfg.tiles(512):
                if seg == 0 and l == cfg.L - 1:
                    continue
                yy = yr.next()
                for d in range(2):
                    self.ld(yy, yy[:, d, :, 0:n], self.gd_y, self.gd_y[s_, d, :, g0:g0 + n].rearrange("(c p) t -> p c t", p=128), key=d)
                z = zr.next()
                self.ld(z, z[:, :, 0:n], self.gd_z, self.gd_z[s_, :, g0:g0 + n].rearrange("(c p) t -> p c t", p=128))
                self.fl()
                os_ = osr.next()
                for c in range(4):
                    o = o_r.next()
                    self.tt('dve', o, o[:, 0:n], yy, yy[:, 0, c, 0:n], yy, yy[:, 1, c, 0:n], ALU.add)
                    sq = sq_r.next()
                    self.act(sq, sq[:, 0:n], o, o[:, 0:n], AF.Square)
                    p = pm.next()
                    self.mm(p, p[:, 0:n], self.ones128, self.ones128[:], sq, sq[:, 0:n])
                    t1 = t_r.next()
                    self.act(t1, t1[:, 0:n], p, p[:, 0:n], AF.Ln, bias=self.epsc[:, 0:1], scale=1.0 / 128, extra=[self.epsc])
                    t2 = t_r.next()
                    self.act(t2, t2[:, 0:n], t1, t1[:, 0:n], AF.Exp, scale=-0.5)
                    self.stt(o, o[:, 0:n], o, o[:, 0:n], gn[:, 0:1], t2, t2[:, 0:n], ALU.mult, ALU.mult, extra=[gn])
                    self.tt('dve', os_, os_[:, c, 0:n], o, o[:, 0:n], z, z[:, c, 0:n], ALU.mult)
                self.st(self.ybT, self.ybT[s_, :, g0:g0 + n].rearrange("(c p) t -> p c t", p=128), os_, os_[:, :, 0:n])
        ph.close()

    def phase_merge(self, l):
        cfg = self.cfg
        S = cfg.S
        s = self.s
        ph = Phase(s)
        NT = 256
        self.wstage = ph.ring('wstg', [128, 8, 512], F32, 2)
        r = lambda ap: ap.rearrange("(kc p) c -> p kc c", p=128)
        wg = ph.sb('wg', [128, 8, 3072], BF16)
        wgs = self.inp['w_gate']
        self.load_w(ph, wg, lambda a, b: wg[:, :, a:b], wgs, lambda a, b: r(wgs[l, :, a:b]), 3072, 8)
        wu = []
        for j, nm in enumerate(('w_up_a', 'w_up_b', 'w_up_c')):
            w = ph.sb(nm, [128, 4, 1024], BF16)
            src = self.inp[nm]
            self.load_w(ph, w, lambda a, b, w=w: w[:, :, a:b], src, lambda a, b, src=src: r(src[l, :, a:b]), 1024, 4)
            wu.append(w)
        wo = ph.sb('wo', [128, 8, 1024], BF16)
        wos = self.inp['w_out']
        self.load_w(ph, wo, lambda a, b: wo[:, :, a:b], wos, lambda a, b: r(wos[l, :, a:b]), 1024, 8)
        bg = ph.sb('bg', [128, 24], F32)
        self.ld(bg, bg[:], self.inp['b_gateT'], self.inp['b_gateT'][:, l, :])
        hr = ph.ring('hb', [128, 8, NT], BF16, 2)
        yr = ph.ring('y', [128, 3, 4, NT], BF16, 2)
        xr = ph.ring('x', [128, 8, NT], F32, 2)
        mbr = ph.ring('mb', [128, 8, NT], BF16, 1)
        gtr = ph.ring('gt', [128, NT], F32, 3)
        tmr = ph.ring('tm', [128, NT], F32, 4)
        pg = ph.ring('pg', [128, NT], F32, 3, psum=True)
        pu = ph.ring('pu', [128, NT], F32, 3, psum=True)
        pw = ph.ring('pw', [128, NT], F32, 2, psum=True)
        xs = self.xsrc(l)
        ysrc = (self.yaT, self.ybT, self.ycT)
        for s_ in range(S):
            for (seg, t0, n, g0) in cfg.tiles(NT):
                if seg == 0 and l == cfg.L - 1:
                    continue
                which = S if seg == 0 else s_
                hb = hr.next()
                c0 = cfg.pcol(seg, t0, 2)
                self.ld(hb, hb[:, :, 0:n], self.hT, self.hT[s_, :, c0:c0 + n].rearrange("(kc p) t -> p kc t", p=128))
                y = yr.next()
                for j in range(3):
                    self.ld(y, y[:, j, :, 0:n], ysrc[j], ysrc[j][s_, :, g0:g0 + n].rearrange("(kc p) t -> p kc t", p=128), key=j)
                xt = xr.next()
                self.ld(xt, xt[:, :, 0:n], xs, xs[s_, :, g0:g0 + n].rearrange("(kc p) t -> p kc t", p=128))
                self.fl()
                mb = mbr.next()
                for m in range(8):
                    macc = tmr.next()
                    for j in range(3):
                        p = pg.next()
                        for kc in range(8):
                            self.mm(p, p[:, 0:n], wg, wg[:, kc, j * 1024 + m * 128:j * 1024 + (m + 1) * 128], hb, hb[:, kc, 0:n],
                                    start=(kc == 0), stop=(kc == 7))
                        gt = gtr.next()
                        self.act(gt, gt[:, 0:n], p, p[:, 0:n], AF.Sigmoid, bias=bg[:, j * 8 + m:j * 8 + m + 1], extra=[bg])
                        p2 = pu.next()
                        for kc in range(4):
                            self.mm(p2, p2[:, 0:n], wu[j], wu[j][:, kc, m * 128:(m + 1) * 128], y, y[:, j, kc, 0:n],
                                    start=(kc == 0), stop=(kc == 3))
                        if j == 0:
                            self.tt('dve', macc, macc[:, 0:n], gt, gt[:, 0:n], p2, p2[:, 0:n], ALU.mult)
                        else:
                            t = tmr.next()
                            self.tt('dve', t, t[:, 0:n], gt, gt[:, 0:n], p2, p2[:, 0:n], ALU.mult)
                            if j == 1:
                                self.tt('dve', macc, macc[:, 0:n], macc, macc[:, 0:n], t, t[:, 0:n], ALU.add)
                            else:
                                self.tt('dve', mb, mb[:, m, 0:n], macc, macc[:, 0:n], t, t[:, 0:n], ALU.add)
                for m in range(8):
                    p = pw.next()
                    for kc in range(8):
                        self.mm(p, p[:, 0:n], wo, wo[:, kc, m * 128:(m + 1) * 128], mb, mb[:, kc, 0:n], start=(kc == 0), stop=(kc == 7))
                    t = tmr.next()
                    self.act(t, t[:, 0:n], p, p[:, 0:n], AF.Copy, scale=self.MOD[:, l, 16 + m, which:which + 1], extra=[self.MOD])
                    self.tt('dve', xt, xt[:, m, 0:n], xt, xt[:, m, 0:n], t, t[:, 0:n], ALU.add)
                self.st(self.xmid, self.xmid[s_, :, g0:g0 + n].rearrange("(kc p) t -> p kc t", p=128), xt, xt[:, :, 0:n])
        ph.close()

    def phase_ffn(self, l):
        cfg = self.cfg
        S = cfg.S
        s = self.s
        ph = Phase(s)
        NT = 256
        NF = 22
        self.wstage = ph.ring('wstg', [128, 8, 256], F32, 2)
        r = lambda ap: ap.rearrange("(kc p) c -> p kc c", p=128)
        w1 = ph.sb('w1', [128, 8, 2816], BF16)
        w3 = ph.sb('w3', [128, 8, 2816], BF16)
        w2 = ph.sb('w2', [128, NF, 1024], BF16)
        for w, nm in ((w1, 'ffn_w1'), (w3, 'ffn_w3')):
            src = self.inp[nm]
            self.load_w(ph, w, lambda a, b, w=w: w[:, :, a:b], src, lambda a, b, src=src: r(src[l, :, a:b]), 2816, 8, piece=256)
        src2 = self.inp['ffn_w2']
        for f0 in range(0, NF, 8):
            f1 = min(NF, f0 + 8)
            self.load_w(ph, w2, lambda a, b, f0=f0, f1=f1: w2[:, f0:f1, a:b], src2,
                        lambda a, b, f0=f0, f1=f1: r(src2[l, f0 * 128:f1 * 128, a:b]), 1024, f1 - f0, piece=256)
        xr = ph.ring('x', [128, 8, NT], F32, 2)
        hr = ph.ring('hb', [128, 8, NT], BF16, 1)
        sqr = ph.ring('sq', [128, 8, NT], BF16, 1)
        hid = ph.ring('hid', [128, NF, NT], BF16, 1)
        tmr = ph.ring('tm', [128, NT], F32, 4)
        psr = ph.ring('pss', [128, NT], F32, 1, psum=True)
        p1r = ph.ring('p1', [128, NT], F32, 2, psum=True)
        p3r = ph.ring('p3', [128, NT], F32, 2, psum=True)
        p2r = ph.ring('p2', [128, NT], F32, 2, psum=True)
        for s_ in range(S):
            for (seg, t0, n, g0) in cfg.tiles(NT):
                if seg == 0 and l == cfg.L - 1:
                    continue
                which = S if seg == 0 else s_
                xt = xr.next()
                self.ld(xt, xt[:, :, 0:n], self.xmid, self.xmid[s_, :, g0:g0 + n].rearrange("(kc p) t -> p kc t", p=128))
                self.fl()
                hb = hr.next()
                self.norm_mod(ph, xt, n, self.A2, lambda kc: self.A2[:, kc, which:which + 1],
                              self.MOD, self.modcol(l, 3, which), hb, sqr, psr, tmr)
                hd = hid.next()
                for f in range(NF):
                    p1 = p1r.next()
                    for kc in range(8):
                        self.mm(p1, p1[:, 0:n], w1, w1[:, kc, f * 128:(f + 1) * 128], hb, hb[:, kc, 0:n], start=(kc == 0), stop=(kc == 7))
                    p3 = p3r.next()
                    for kc in range(8):
                        self.mm(p3, p3[:, 0:n], w3, w3[:, kc, f * 128:(f + 1) * 128], hb, hb[:, kc, 0:n], start=(kc == 0), stop=(kc == 7))
                    a = tmr.next()
                    self.act(a, a[:, 0:n], p1, p1[:, 0:n], AF.Silu)
                    self.tt('dve', hd, hd[:, f, 0:n], a, a[:, 0:n], p3, p3[:, 0:n], ALU.mult)
                for m in range(8):
                    p = p2r.next()
                    for f in range(NF):
                        self.mm(p, p[:, 0:n], w2, w2[:, f, m * 128:(m + 1) * 128], hd, hd[:, f, 0:n], start=(f == 0), stop=(f == NF - 1))
                    t = tmr.next()
                    self.act(t, t[:, 0:n], p, p[:, 0:n], AF.Copy, scale=self.MOD[:, l, 40 + m, which:which + 1], extra=[self.MOD])
                    self.tt('dve', xt, xt[:, m, 0:n], xt, xt[:, m, 0:n], t, t[:, 0:n], ALU.add)
                self.st(self.xcur, self.xcur[s_, :, g0:g0 + n].rearrange("(kc p) t -> p kc t", p=128), xt, xt[:, :, 0:n])
        ph.close()

    def phase_final(self):
        cfg = self.cfg
        S, TC = cfg.S, cfg.TC
        ph = Phase(self.s)
        gf = ph.sb('gf', [128, 8], F32)
        self.ld(gf, gf[:], self.inp['final_normT'], self.inp['final_normT'][:, :])
        xr = ph.ring('x', [128, 8, 512], F32, 2)
        hr = ph.ring('ho', [128, 8, 512], F32, 2)
        sqr = ph.ring('sq', [128, 8, 512], BF16, 1)
        tmr = ph.ring('tm', [128, 512], F32, 4)
        psr = ph.ring('pss', [128, 512], F32, 2, psum=True)
        for s_ in range(S):
            for (seg, t0, n, g0) in cfg.tiles(512):
                if seg == 0:
                    continue
                xt = xr.next()
                self.ld(xt, xt[:, :, 0:n], self.xcur, self.xcur[s_, :, g0:g0 + n].rearrange("(kc p) t -> p kc t", p=128))
                self.fl()
                ho = hr.next()
                self.norm_mod(ph, xt, n, gf, lambda kc: gf[:, kc:kc + 1], None, None, ho, sqr, psr, tmr)
                self.st(self.outT, self.outT[s_, :, t0:t0 + n].rearrange("(kc p) t -> p kc t", p=128), ho, ho[:, :, 0:n])
        ph.close()


def pmajor(v, nk):
    v = np.asarray(v)
    lead = v.shape[:-1]
    a = v.reshape(*lead, nk, 128)
    return np.ascontiguousarray(np.moveaxis(a, -1, 0))


def rope_tables(TC, TL, grid_w=64, theta=10000.0, dh=64):
    half = dh // 2
    t = np.arange(TL)
    row = (t // grid_w).astype(np.float32)
    col = (t % grid_w).astype(np.float32)
    inv = (theta ** (-np.arange(0, half, 2, dtype=np.float32) / half)).astype(np.float32)
    ang = np.concatenate([row[:, None] * inv, col[:, None] * inv], axis=-1).astype(np.float32)
    cos, sin = np.cos(ang), np.sin(ang)
    tab = np.zeros((128, 2, TC + TL), np.float32)
    tab[:, 0, :TC] = 1.0
    for p in range(128):
        d = p % 64
        i = d // 2
        tab[p, 0, TC:] = cos[:, i]
        tab[p, 1, TC:] = (-sin[:, i]) if d % 2 == 0 else sin[:, i]
    return tab


def host_prep(inputs, cfg, core, b0):
    S, TC, TL, L = cfg.S, cfg.TC, cfg.TL, cfg.L
    f32 = np.float32
    m = {}
    x = np.asarray(inputs['x'])[b0:b0 + S]
    ctx = np.asarray(inputs['ctx'])[b0:b0 + S]
    m['xin'] = np.ascontiguousarray(np.concatenate([ctx, x], axis=1).transpose(0, 2, 1)).astype(f32)
    cc = np.concatenate([np.asarray(inputs['c'])[b0:b0 + S], np.asarray(inputs['c_ctx'])[None, :]], axis=0)
    m['cT'] = np.ascontiguousarray(cc.T.reshape(8, 128, S + 1).transpose(1, 0, 2)).astype(f32)
    m['ada_bT'] = np.ascontiguousarray(np.asarray(inputs['ada_b'])[:L].reshape(L, 48, 128).transpose(2, 0, 1)).astype(f32)
    m['norm1T'] = np.ascontiguousarray(np.asarray(inputs['norm1'])[:L].reshape(L, 8, 128).transpose(2, 0, 1)).astype(f32)
    m['norm2T'] = np.ascontiguousarray(np.asarray(inputs['norm2'])[:L].reshape(L, 8, 128).transpose(2, 0, 1)).astype(f32)
    m['final_normT'] = np.ascontiguousarray(np.asarray(inputs['final_norm']).reshape(8, 128).T).astype(f32)
    m['b_gateT'] = np.ascontiguousarray(np.asarray(inputs['b_gate'])[:L].reshape(L, 24, 128).transpose(2, 0, 1)).astype(f32)
    for nm in WEIGHT_NAMES:
        m[nm] = np.ascontiguousarray(np.asarray(inputs[nm])[:L]).astype(f32)
    QO = 3 * 512 + 4 * 512
    perm = np.arange(640) ^ 1
    m['w_in_perm'] = np.ascontiguousarray(m['w_in'][:, :, QO:QO + 640][:, :, perm])
    gq = np.asarray(inputs['attn_q_norm'])[:L]
    gk = np.asarray(inputs['attn_k_norm'])[:L]
    p64 = np.arange(64) ^ 1
    g = np.stack([np.tile(gq, (1, 2)), np.tile(gq[:, p64], (1, 2)), np.tile(gk, (1, 2)), np.tile(gk[:, p64], (1, 2))], axis=-1)
    m['attn_gT'] = np.ascontiguousarray(g.transpose(1, 0, 2)).astype(f32)
    m['attn_gB'] = np.ascontiguousarray(np.stack([gq, gk], axis=1)).astype(f32)
    P9 = np.stack([np.asarray(inputs['rwkv_w0'])[:L, 0], np.asarray(inputs['rwkv_w0'])[:L, 1],
                   np.asarray(inputs['rwkv_a0'])[:L, 0], np.asarray(inputs['rwkv_a0'])[:L, 1],
                   np.asarray(inputs['rwkv_k_k'])[:L], np.asarray(inputs['rwkv_k_a'])[:L],
                   np.asarray(inputs['rwkv_r_k'])[:L].reshape(L, 512), np.asarray(inputs['rwkv_lnx_w'])[:L],
                   np.asarray(inputs['rwkv_lnx_b'])[:L]], axis=1)
    m['rwkv_pT'] = np.ascontiguousarray(P9.reshape(L, 9, 4, 128).transpose(3, 0, 1, 2)).astype(f32)
    m['rwkv_mu_xT'] = np.ascontiguousarray(np.asarray(inputs['rwkv_mu_x'])[:L].reshape(L, 3, 8, 128).transpose(3, 0, 1, 2)).astype(f32)
    m['rwkv_mu_rkv'] = np.ascontiguousarray(np.asarray(inputs['rwkv_mu_rkv'])[:L].reshape(L, 1536)).astype(f32)
    cm = np.ones((128, 512), f32)
    cm[:, ::64] = 0.0
    m['c_chunkmask'] = cm
    jj, tt_ = np.meshgrid(np.arange(64), np.arange(64), indexing='ij')
    sm = np.zeros((64, 2, 128), f32)
    sm[:, 0, 0:64] = (jj < tt_)
    sm[:, 0, 64:128] = (jj <= tt_)
    sm[:, 1, 0:64] = (jj > tt_)
    sm[:, 1, 64:128] = (jj >= tt_)
    m['c_scanmask'] = sm
    m['c_ident64'] = np.eye(64, dtype=f32)
    bmk = np.zeros((64, 2, 64), f32)
    bmk[:, 0, :] = ((jj // 32) == (tt_ // 32))
    bmk[:, 1, :] = 1.0 - bmk[:, 0, :]
    m['c_blkmask'] = bmk
    m['gdn_convT'] = np.ascontiguousarray(np.asarray(inputs['gdn_conv'])[:L].reshape(L, 12, 128, 5).transpose(2, 0, 1, 3)).astype(f32)
    rp = np.zeros((16, L, 3), f32)
    rp[0:8, :, 0] = np.asarray(inputs['gdn_dt_bias'])[:L].reshape(L, 8).T
    rp[0:8, :, 1] = np.asarray(inputs['gdn_A_log'])[:L].reshape(L, 8).T
    rp[4:8, :, 2] = 1.0
    m['gdn_rowp'] = rp
    m['gdn_normT'] = np.ascontiguousarray(np.asarray(inputs['gdn_norm'])[:L][:, :, None]).astype(f32)
    sb_ = np.zeros((16, 16, 128), f32)
    for r_ in range(16):
        sb_[r_, r_, :] = 1.0
    m['c_selb'] = sb_.astype(ml_dtypes.bfloat16)
    m['c_self'] = np.ascontiguousarray(sb_[:, :, :64])
    nm_ = np.zeros((64, 2, 2, 64), f32)
    NEG = -30000.0
    nm_[:, 0, 0] = np.where(jj < tt_, 0.0, NEG)
    nm_[:, 0, 1] = np.where(jj <= tt_, 0.0, NEG)
    nm_[:, 1, 0] = np.where(jj > tt_, 0.0, NEG)
    nm_[:, 1, 1] = np.where(jj >= tt_, 0.0, NEG)
    m['c_negmask'] = nm_
    m['c_ident'] = np.eye(128, dtype=f32).astype(ml_dtypes.bfloat16)
    ob = np.zeros((128, 128), f32)
    ob[:64, :64] = 1
    ob[64:, 64:] = 1
    m['c_onesblk'] = ob.astype(ml_dtypes.bfloat16)
    m['c_rope'] = rope_tables(TC, TL)
    return m


def input_shapes(m):
    out = {}
    for k, v in m.items():
        out[k] = (v.shape, BF16 if v.dtype == ml_dtypes.bfloat16 else F32)
    return out


_CACHE = {}


def run(inputs, cfg, n_cores):
    maps = [host_prep(inputs, cfg, c, c * cfg.S) for c in range(n_cores)]
    key = (cfg.S, cfg.TC, cfg.TL, cfg.L, tuple(sorted(cfg.debug)))
    kern = Kern(cfg, input_shapes(maps[0]))
    nc = kern.build()
    res = run_bass_kernel_spmd(nc, maps, core_ids=list(range(n_cores)))
    return kern, res


def kernel(**inputs):
    cfg = Cfg(S=2, TC=256, TL=4096, L=4)
    kern, res = run(inputs, cfg, 8)
    outs = [np.asarray(r['outT']).transpose(0, 2, 1) for r in res.results]
    return np.ascontiguousarray(np.concatenate(outs, axis=0)).astype(np.float32)
```

```python
import math
from contextlib import ExitStack
import numpy as np
import ml_dtypes
import concourse.bass as bass
import concourse.mybir as mybir
from concourse.bass_utils import run_bass_kernel_spmd

F32 = mybir.dt.float32
BF16 = mybir.dt.bfloat16
AF = mybir.ActivationFunctionType
ALU = mybir.AluOpType

ENGS = ['pe', 'act', 'dve', 'pool', 'sp']
RMS_EPS = 1e-6


class Res:
    dram = False

    def __init__(self, name):
        self.name = name
        self.lw = {}
        self.rd = {}
        self.sems = {}


class Tile(Res):
    def __init__(self, name, t):
        super().__init__(name)
        self.t = t

    def __getitem__(self, idx):
        return self.t[idx]


class DRes(Res):
    dram = True

    def __init__(self, name, ap):
        super().__init__(name)
        self.ap = ap

    def __getitem__(self, idx):
        return self.ap[idx]


class Sched:
    def __init__(self, nc, stack):
        self.nc = nc
        self.stack = stack
        self.eng = {'pe': nc.tensor, 'act': nc.scalar, 'dve': nc.vector, 'pool': nc.gpsimd, 'sp': nc.sync}
        self.cnt = {e: 0 for e in ENGS}
        self.esem = {e: stack.enter_context(nc.semaphore('es_' + e)) for e in ENGS if e != 'sp'}
        self.known = {e: {} for e in ENGS}
        self.dma_sems = []
        self.free_sems = []
        self.nwaits = 0
        self.nops = 0
        self.uid = 0
        self.pending = []
        self.trace = None

    def defer(self, out, in_, reads, writes, semres, key, kw):
        self.pending.append((out, in_, list(reads), list(writes), semres, key, kw))

    def flush(self):
        p, self.pending = self.pending, []
        for (out, in_, reads, writes, semres, key, kw) in p:
            self.dma('sp', out, in_, reads, writes, semres, key=key, **kw)

    def _conflict(self, reads, writes):
        for (_o, _i, pr, pw, _s, _key, _k) in self.pending:
            for x in writes:
                if any(x is y for y in pr) or any(x is y for y in pw):
                    return True
            for x in reads:
                if any(x is y for y in pw):
                    return True
        return False

    def sb(self, stack, name, shape, dtype):
        self.uid += 1
        t = stack.enter_context(self.nc.sbuf_tensor(f"{name}_{self.uid}", list(shape), dtype))
        return Tile(name, t)

    def ps(self, stack, name, shape, dtype=F32):
        self.uid += 1
        t = stack.enter_context(self.nc.psum_tensor(f"{name}_{self.uid}", list(shape), dtype))
        return Tile(name, t)

    def _waits(self, eng, reads, writes, dma=False):
        w = {}
        for r in reads:
            for sem, val in r.lw.items():
                if w.get(sem, 0) < val:
                    w[sem] = val
            if getattr(r, 'psum', False):
                for sem, val in r.rd.items():
                    if w.get(sem, 0) < val:
                        w[sem] = val
        for r in writes:
            for d in (r.lw, r.rd):
                for sem, val in d.items():
                    if w.get(sem, 0) < val:
                        w[sem] = val
        own = self.esem.get(eng)
        kn = self.known[eng]
        e = self.eng[eng]
        for sem, val in w.items():
            if sem is own and eng == 'pe' and not dma:
                continue
            if kn.get(sem, 0) >= val:
                continue
            kn[sem] = val
            e.wait_ge(sem, val)
            self.nwaits += 1
            if self.trace is not None:
                self.trace.append(f"  {eng} WAIT {sem.name}>={val}")

    def _post(self, ev_sem, ev_val, reads, writes, dma=False):
        for r in reads:
            if r.rd.get(ev_sem, 0) < ev_val:
                r.rd[ev_sem] = ev_val
        for r in writes:
            if r.dram or dma:
                if r.lw.get(ev_sem, 0) < ev_val:
                    r.lw[ev_sem] = ev_val
            else:
                r.lw = {ev_sem: ev_val}
                r.rd = {}

    def op(self, eng, fn, reads=(), writes=()):
        if self.pending and self._conflict(reads, writes):
            self.flush()
        self._waits(eng, reads, writes)
        ins = fn(self.eng[eng])
        self.cnt[eng] += 1
        sem = self.esem[eng]
        ins.then_inc(sem, 1)
        if self.trace is not None:
            self.trace.append(f"{eng} OP reads={[r.name for r in reads]} writes={[r.name for r in writes]} -> {sem.name}={self.cnt[eng]}")
        self._post(sem, self.cnt[eng], reads, writes)
        self.nops += 1
        return ins

    def dma(self, q, out, in_, reads, writes, semres, key=0, **kw):
        if self.pending and self._conflict(reads, writes):
            self.flush()
        self._waits(q, reads, writes, dma=True)
        slot = semres.sems.get(key)
        if slot is None:
            if self.free_sems:
                slot = self.free_sems.pop()
            else:
                self.uid += 1
                slot = [self.stack.enter_context(self.nc.semaphore(f"ds_{self.uid}")), 0]
            semres.sems[key] = slot
            self.dma_sems.append(slot)
        sem = slot[0]
        if slot[1] > 0 and self.known[q].get(sem, 0) < slot[1]:
            self.known[q][sem] = slot[1]
            self.eng[q].wait_ge(sem, slot[1])
            self.nwaits += 1
        ins = self.eng[q].dma_start(out=out, in_=in_, **kw)
        slot[1] += 16
        ins.then_inc(sem, 16)
        if self.trace is not None:
            self.trace.append(f"{q} DMA reads={[r.name for r in reads]} writes={[r.name for r in writes]} -> {sem.name}#{sem.num}={slot[1]}")
        self._post(sem, slot[1], reads, writes, dma=True)
        self.nops += 1
        return ins

    def barrier(self):
        self.flush()
        evs = {}
        for e, sem in self.esem.items():
            if self.cnt[e] > 0:
                evs[sem] = self.cnt[e]
        for slot in self.dma_sems:
            if slot[1] > 0:
                evs[slot[0]] = slot[1]
        for eng in ENGS:
            own = self.esem.get(eng)
            kn = self.known[eng]
            for sem, val in evs.items():
                if sem is own:
                    continue
                if kn.get(sem, 0) >= val:
                    continue
                kn[sem] = val
                self.eng[eng].wait_ge(sem, val)
                self.nwaits += 1
                if self.trace is not None:
                    self.trace.append(f"  {eng} BWAIT {sem.name}>={val}")


class Phase:
    def __init__(self, s):
        self.s = s
        self.stack = ExitStack()
        self.tiles = []
        if not hasattr(s, 'open_ph'):
            s.open_ph = []
        s.open_ph.append(self)

    def sb(self, name, shape, dtype):
        t = self.s.sb(self.stack, name, shape, dtype)
        self.tiles.append(t)
        return t

    def ps(self, name, shape, dtype=F32):
        t = self.s.ps(self.stack, name, shape, dtype)
        t.psum = True
        self.tiles.append(t)
        return t

    def ring(self, name, shape, dtype, n, psum=False):
        return Ring([(self.ps if psum else self.sb)(f"{name}{i}", shape, dtype) for i in range(n)])

    def close(self):
        s = self.s
        s.barrier()
        for t in self.tiles:
            for slot in t.sems.values():
                s.free_sems.append(slot)
                s.dma_sems.remove(slot)
            t.sems = {}
        self.stack.close()
        s.open_ph.remove(self)


class Ring:
    def __init__(self, tiles):
        self.tiles = tiles
        self.i = 0

    def next(self):
        t = self.tiles[self.i % len(self.tiles)]
        self.i += 1
        return t


class Cfg:
    def __init__(self, S=2, TC=256, TL=4096, L=4, debug=()):
        self.S, self.TC, self.TL, self.L = S, TC, TL, L
        self.TT = TC + TL
        self.debug = set(debug)

    def tiles(self, nt):
        out = []
        for t0 in range(0, self.TC, nt):
            out.append((0, t0, min(nt, self.TC - t0), t0))
        for t0 in range(0, self.TL, nt):
            out.append((1, t0, min(nt, self.TL - t0), self.TC + t0))
        return out

    def pcol(self, seg, t, halo):
        return (halo + t) if seg == 0 else (self.TC + 3 * halo + t)


WEIGHT_NAMES = ['ada_w', 'w_in', 'rwkv_w1', 'rwkv_w2', 'rwkv_a1', 'rwkv_a2', 'rwkv_g1', 'rwkv_g2',
                'gdn_w_alpha', 'gdn_w_beta', 'w_up_a', 'w_up_b', 'w_up_c', 'w_gate', 'w_out',
                'ffn_w1', 'ffn_w3', 'ffn_w2']


class Kern:
    def __init__(self, cfg, shapes):
        self.cfg = cfg
        self.nc = nc = bass.Bass("TRN2", target_bir_lowering=False)
        self.inp = {}
        for name, (shape, dt) in shapes.items():
            self.inp[name] = DRes(name, nc.dram_tensor(name, list(shape), dt, kind="ExternalInput").ap())
        self.scr = {}
        self.dbg_outs = []

    def dscr(self, name, shape, dtype):
        kind = "ExternalOutput" if name in self.cfg.debug else "Internal"
        r = DRes(name, self.nc.dram_tensor(name, list(shape), dtype, kind=kind).ap())
        if name in self.cfg.debug:
            self.dbg_outs.append(name)
        self.scr[name] = r
        return r

    def mm(self, ps, out, lt, lhsT, rt, rhs, start=True, stop=True):
        self.s.op('pe', lambda e: e.matmul(out, lhsT=lhsT, rhs=rhs, start=start, stop=stop),
                  [lt, rt] if lt is not rt else [lt], [ps])

    def mmr(self, ps, out, lt, lhsT, rt, rhs, start=True, stop=True):
        if getattr(self.cfg, 'fp32r', False):
            lhsT = lhsT.bitcast(mybir.dt.float32r)
            rhs = rhs.bitcast(mybir.dt.float32r)
        self.mm(ps, out, lt, lhsT, rt, rhs, start=start, stop=stop)

    def act(self, ot, out, it, in_, func, bias=0.0, scale=1.0, extra=(), eng='act'):
        kw = {}
        if not (isinstance(bias, float) and bias == 0.0):
            kw['bias'] = bias
        if not (isinstance(scale, float) and scale == 1.0):
            kw['scale'] = scale
        self.s.op('act', lambda e: e.activation(out=out, in_=in_, func=func, **kw), [it] + list(extra), [ot])

    def tt(self, eng, ot, out, t0, in0, t1, in1, op):
        self.s.op(eng, lambda e: e.tensor_tensor(out=out, in0=in0, in1=in1, op=op), [t0, t1], [ot])

    def ts(self, eng, ot, out, t0, in0, s1, s2=None, op0=ALU.mult, op1=None, extra=()):
        if op1 is None:
            self.s.op(eng, lambda e: e.tensor_scalar(out=out, in0=in0, scalar1=s1, scalar2=None, op0=op0),
                      [t0] + list(extra), [ot])
        else:
            self.s.op(eng, lambda e: e.tensor_scalar(out=out, in0=in0, scalar1=s1, scalar2=s2, op0=op0, op1=op1),
                      [t0] + list(extra), [ot])

    def stt(self, ot, out, t0, in0, sc, t1, in1, op0, op1, extra=()):
        self.s.op('dve', lambda e: e.scalar_tensor_tensor(out=out, in0=in0, scalar=sc, in1=in1, op0=op0, op1=op1),
                  [t0, t1] + list(extra), [ot])

    def cp(self, eng, ot, out, it, in_):
        if eng == 'act':
            self.s.op('act', lambda e: e.activation(out=out, in_=in_, func=AF.Copy), [it], [ot])
        else:
            self.s.op(eng, lambda e: e.tensor_copy(out=out, in_=in_), [it], [ot])

    def ld(self, t, out, src, in_, q='sp', key=0):
        self.s.dma(q, out, in_, [src], [t], t, key=key)

    def st(self, dst, out, t, in_, q='sp', key=0, **kw):
        import os
        if dst.name in os.environ.get('NOST', '').split(','):
            return
        self.s.defer(out, in_, [t], [dst], t, key, kw)

    def fl(self):
        self.s.flush()

    def load_w(self, ph, wt, dst_fn, src, src_fn, ncols, rows_kc, piece=512, row_scale=None, col_scale=None,
               cast_eng=('dve', 'act')):
        i = 0
        for c0 in range(0, ncols, piece):
            c1 = min(ncols, c0 + piece)
            stg = self.wstage.next()
            sv = stg[:, 0:rows_kc, 0:c1 - c0]
            self.ld(stg, sv, src, src_fn(c0, c1), q='sp')
            if col_scale is not None:
                cst, cfn = col_scale
                for kc in range(rows_kc):
                    self.tt('dve', stg, stg[:, kc, 0:c1 - c0], stg, stg[:, kc, 0:c1 - c0], cst, cfn(c0, c1), ALU.mult)
            if row_scale is not None:
                rst, rfn = row_scale
                for kc in range(rows_kc):
                    self.ts('dve', wt, dst_fn(c0, c1)[:, kc, :], stg, stg[:, kc, 0:c1 - c0], rfn(kc), extra=[rst])
            else:
                eng = cast_eng[i % len(cast_eng)]
                self.cp(eng, wt, dst_fn(c0, c1), stg, sv)
            i += 1

    def build(self):
        cfg = self.cfg
        nc = self.nc
        S, L, TT, TC, TL = cfg.S, cfg.L, cfg.TT, cfg.TC, cfg.TL
        with ExitStack() as gstack:
            self.s = s = Sched(nc, gstack)
            if getattr(cfg, 'trace', False):
                s.trace = []
            self.xcur = self.dscr('xcur', [S, 1024, TT], F32)
            self.xmid = self.dscr('xmid', [S, 1024, TT], F32)
            self.hT = self.dscr('hT', [S, 1024, TT + 8], BF16)
            self.qT = self.dscr('qT', [S, 512, TT], BF16)
            self.kT = self.dscr('kT', [S, 128, TT], BF16)
            self.vtok = self.dscr('vtok', [S, TT, 130], BF16)
            self.yaT = self.dscr('yaT', [S, 512, TT], BF16)
            self.ybT = self.dscr('ybT', [S, 512, TT], BF16)
            self.ycT = self.dscr('ycT', [S, 512, TT], BF16)
            self.dbgT = self.dscr('dbg_scan', [64, 8, 2048], BF16)
            self.gd_u = self.dscr('gd_u', [S, 1536, TT + 8], BF16)
            self.gd_z = self.dscr('gd_z', [S, 512, TT], BF16)
            self.gd_ab = self.dscr('gd_ab', [S, 16, TT], F32)
            self.gd_gc = self.dscr('gd_gc', [S, 16, TT // 64], F32)
            self.gd_dm = self.dscr('gd_dm', [S, 2, TT // 64, 64, 512], BF16)
            self.gd_fm = self.dscr('gd_fm', [S, 8, 512, TT], BF16)
            self.gd_tm = self.dscr('gd_tm', [S, 4, TT, 512], BF16)
            self.gd_y = self.dscr('gd_y', [S, 2, 512, TT], F32)
            self.rw_fm = self.dscr('rw_fm', [S, 2, 4, 512, TT], BF16)
            self.rw_tm = self.dscr('rw_tm', [S, 5, TT, 512], BF16)
            self.rw_gc = self.dscr('rw_gc', [S, 2, 512, TT // 64], F32)
            self.rw_bg = self.dscr('rw_bg', [S, 2, 512, TT], BF16)
            self.rw_y = self.dscr('rw_y', [S, 2, 512, TT], F32)
            self.outT = DRes('outT', nc.dram_tensor('outT', [S, 1024, TL], F32, kind="ExternalOutput").ap())
            self.G = gph = Phase(s)
            self.ident = gph.sb('ident', [128, 128], BF16)
            self.ones128 = gph.sb('ones128', [128, 128], BF16)
            self.onesblk = gph.sb('onesblk', [128, 128], BF16)
            self.onesf = gph.sb('onesf', [128, 128], F32)
            self.MOD = gph.sb('MOD', [128, L, 48, S + 1], F32)
            self.epsc = gph.sb('epsc', [128, 1], F32)
            self.ld(self.ident, self.ident[:], self.inp['c_ident'], self.inp['c_ident'][:, :])
            self.ld(self.onesblk, self.onesblk[:], self.inp['c_onesblk'], self.inp['c_onesblk'][:, :])
            s.op('dve', lambda e: e.memset(self.ones128[:], 1.0), [], [self.ones128])
            s.op('dve', lambda e: e.memset(self.onesf[:], 1.0), [], [self.onesf])
            s.op('dve', lambda e: e.memset(self.epsc[:], RMS_EPS), [], [self.epsc])
            try:
                self.zero_pads()
                self.chk('pads')
                self.phase_mod()
                self.chk('mod')
                for l in range(L):
                    self.layer(l)
                self.phase_final()
            except StopIteration:
                for p in reversed(list(s.open_ph)):
                    if p is not gph:
                        p.close()
            gph.close()
            s.barrier()
        return nc

    def chk(self, name):
        if getattr(self.cfg, 'stop', None) == name:
            raise StopIteration

    def zero_pads(self):
        cfg = self.cfg
        ph = Phase(self.s)
        z = ph.sb('z', [128, 12, 2], BF16)
        self.s.op('dve', lambda e: e.memset(z[:], 0.0), [], [z])
        for s_ in range(cfg.S):
            for i, c in enumerate((0, cfg.TC + 2, cfg.TC + 4, cfg.TT + 6)):
                self.st(self.hT, self.hT[s_, :, c:c + 2].rearrange("(kc p) t -> p kc t", p=128), z, z[:, 0:8, 0:2], key=i)
                self.st(self.gd_u, self.gd_u[s_, :, c:c + 2].rearrange("(kc p) t -> p kc t", p=128), z, z[:, 0:12, 0:2], key=4 + i)
        ph.close()

    def phase_mod(self):
        cfg = self.cfg
        S, L = cfg.S, cfg.L
        s = self.s
        ph = Phase(s)
        cT = ph.sb('cT', [128, 8, S + 1], F32)
        sc = ph.sb('sc', [128, 8, S + 1], F32)
        bia = ph.sb('bia', [128, L, 48], F32)
        wst = ph.ring('adaw', [128, 8, 1024], F32, 2)
        pm = ph.ring('pm', [128, 8, 4], F32, 2, psum=True)
        self.ld(cT, cT[:], self.inp['cT'], self.inp['cT'][:, :, :])
        self.ld(bia, bia[:], self.inp['ada_bT'], self.inp['ada_bT'][:, :, :])
        self.act(sc, sc[:], cT, cT[:], AF.Silu)
        adaw = self.inp['ada_w']
        for l in range(L):
            for pc in range(6):
                w = wst.next()
                self.ld(w, w[:], adaw, adaw[l, :, pc * 1024:(pc + 1) * 1024].rearrange("(kc p) c -> p kc c", p=128))
                p = pm.next()
                for m in range(8):
                    for kc in range(8):
                        self.mm(p, p[:, m, 0:S + 1], w, w[:, kc, m * 128:(m + 1) * 128], sc, sc[:, kc, :],
                                start=(kc == 0), stop=(kc == 7))
                self.tt('dve', self.MOD, self.MOD[:, l, pc * 8:(pc + 1) * 8, :], p, p[:, :, 0:S + 1],
                        bia, bia[:, l, pc * 8:(pc + 1) * 8].unsqueeze(2).to_broadcast([128, 8, S + 1]), ALU.add)
        ph.close()

    def norm_mod(self, ph, xt, n, A, Acol, B, Bcol, hb, sqr, psr, tmr):
        sq = sqr.next()
        self.act(sq, sq[:, :, 0:n], xt, xt[:, :, 0:n], AF.Square)
        pss = psr.next()
        for kc in range(8):
            self.mm(pss, pss[:, 0:n], self.ones128, self.ones128[:], sq, sq[:, kc, 0:n], start=(kc == 0), stop=(kc == 7))
        if not hasattr(ph, 'rsr'):
            ph.rsr = ph.ring('rs', [128, 512], F32, 2)
        lnv = ph.rsr.next()
        self.act(lnv, lnv[:, 0:n], pss, pss[:, 0:n], AF.Ln, bias=self.epsc[:, 0:1], scale=1.0 / 1024, extra=[self.epsc])
        rstd = ph.rsr.next()
        self.act(rstd, rstd[:, 0:n], lnv, lnv[:, 0:n], AF.Exp, scale=-0.5)
        for kc in range(8):
            tmp = tmr.next()
            self.stt(tmp, tmp[:, 0:n], xt, xt[:, kc, 0:n], Acol(kc), rstd, rstd[:, 0:n], ALU.mult, ALU.mult, extra=[A])
            if B is None:
                self.cp('act', hb, hb[:, kc, 0:n], tmp, tmp[:, 0:n])
            else:
                self.act(hb, hb[:, kc, 0:n], tmp, tmp[:, 0:n], AF.Identity, bias=Bcol(kc), extra=[B])

    def layer_consts(self, l):
        cfg = self.cfg
        S = cfg.S
        ph = self.LC
        s = self.s
        g12 = ph.sb('g12', [128, 2, 8], F32)
        self.ld(g12, g12[:, 0, :], self.inp['norm1T'], self.inp['norm1T'][:, l, :])
        self.ld(g12, g12[:, 1, :], self.inp['norm2T'], self.inp['norm2T'][:, l, :], key=1)
        self.A1 = ph.sb('A1', [128, 8, S + 1], F32)
        self.A2 = ph.sb('A2', [128, 8, S + 1], F32)
        for A, j, gi in ((self.A1, 1, 0), (self.A2, 4, 1)):
            self.ts('dve', A, A[:], self.MOD, self.MOD[:, l, j * 8:(j + 1) * 8, :], 1.0, op0=ALU.add)
            self.tt('dve', A, A[:], A, A[:], g12, g12[:, gi, :].unsqueeze(2).to_broadcast([128, 8, S + 1]), ALU.mult)

    def modcol(self, l, j, which):
        return lambda kc: self.MOD[:, l, j * 8 + kc, which:which + 1]

    def layer(self, l):
        cfg = self.cfg
        self.LC = Phase(self.s)
        self.layer_consts(l)
        use = getattr(cfg, 'use', (1, 1, 1))
        try:
            self.phase_norm1(l)
            self.chk('norm1')
            for j, yt in enumerate((self.yaT, self.ybT, self.ycT)):
                if not use[j]:
                    self.zero_y(yt)
            if use[0]:
                self.phase_rwkv_proj(l)
                self.chk('rwkv_proj')
                self.phase_rwkv_scan(l)
                self.chk('rwkv_scan')
                self.phase_rwkv_post(l)
                self.chk('rwkv_post')
            if use[1]:
                self.phase_gdn_proj(l)
                self.chk('gdn_proj')
                self.phase_gdn_prep(l)
                self.chk('gdn_prep')
                self.phase_gdn_scan(l)
                self.chk('gdn_scan')
                self.phase_gdn_post(l)
                self.chk('gdn_post')
            if use[2]:
                self.phase_attn_proj(l)
                self.chk('attn_proj')
                self.phase_attn(l)
                self.chk('attn')
            self.phase_merge(l)
            self.chk('merge')
            self.phase_ffn(l)
            self.chk('ffn')
        except StopIteration:
            raise
        self.LC.close()

    def zero_y(self, yt):
        cfg = self.cfg
        ph = Phase(self.s)
        z = ph.sb('z', [128, 4, 512], BF16)
        self.s.op('dve', lambda e: e.memset(z[:], 0.0), [], [z])
        for s_ in range(cfg.S):
            for (seg, t0, n, g0) in cfg.tiles(512):
                self.st(yt, yt[s_, :, g0:g0 + n].rearrange("(kc p) t -> p kc t", p=128), z, z[:, :, 0:n])
        ph.close()

    def xsrc(self, l):
        return self.inp['xin'] if l == 0 else self.xcur

    def phase_norm1(self, l):
        cfg = self.cfg
        S = cfg.S
        ph = Phase(self.s)
        xr = ph.ring('x', [128, 8, 512], F32, 2)
        hr = ph.ring('hb', [128, 8, 512], BF16, 2)
        sqr = ph.ring('sq', [128, 8, 512], BF16, 1)
        tmr = ph.ring('tm', [128, 512], F32, 4)
        psr = ph.ring('pss', [128, 512], F32, 2, psum=True)
        xs = self.xsrc(l)
        for s_ in range(S):
            for (seg, t0, n, g0) in cfg.tiles(512):
                which = S if seg == 0 else s_
                xt = xr.next()
                self.ld(xt, xt[:, :, 0:n], xs, xs[s_, :, g0:g0 + n].rearrange("(kc p) t -> p kc t", p=128))
                self.fl()
                hb = hr.next()
                self.norm_mod(ph, xt, n, self.A1, lambda kc: self.A1[:, kc, which:which + 1],
                              self.MOD, self.modcol(l, 0, which), hb, sqr, psr, tmr)
                c0 = cfg.pcol(seg, t0, 2)
                self.st(self.hT, self.hT[s_, :, c0:c0 + n].rearrange("(kc p) t -> p kc t", p=128), hb, hb[:, :, 0:n])
        ph.close()

    def phase_attn_proj(self, l):
        cfg = self.cfg
        S = cfg.S
        s = self.s
        ph = Phase(s)
        self.wstage = ph.ring('wstg', [128, 8, 512], F32, 2)
        win = self.inp['w_in']
        winp = self.inp['w_in_perm']
        QO = 3 * 512 + 4 * 512
        wq = ph.sb('wq', [128, 8, 512], BF16)
        wqs = ph.sb('wqs', [128, 8, 512], BF16)
        wk = ph.sb('wk', [128, 8, 128], BF16)
        wks = ph.sb('wks', [128, 8, 128], BF16)
        wv = ph.sb('wv', [128, 8, 128], BF16)
        r = lambda ap: ap.rearrange("(kc p) c -> p kc c", p=128)
        self.load_w(ph, wq, lambda a, b: wq[:, :, a:b], win, lambda a, b: r(win[l, :, QO + a:QO + b]), 512, 8)
        self.load_w(ph, wqs, lambda a, b: wqs[:, :, a:b], winp, lambda a, b: r(winp[l, :, a:b]), 512, 8)
        self.load_w(ph, wk, lambda a, b: wk[:, :, a:b], win, lambda a, b: r(win[l, :, QO + 512 + a:QO + 512 + b]), 128, 8)
        self.load_w(ph, wks, lambda a, b: wks[:, :, a:b], winp, lambda a, b: r(winp[l, :, 512 + a:512 + b]), 128, 8)
        self.load_w(ph, wv, lambda a, b: wv[:, :, a:b], win, lambda a, b: r(win[l, :, QO + 640 + a:QO + 640 + b]), 128, 8)
        self.chk('ap_w')
        gn = ph.sb('gn', [128, 4], F32)
        self.ld(gn, gn[:], self.inp['attn_gT'], self.inp['attn_gT'][:, l, :])
        hr = ph.ring('hb', [128, 8, 512], BF16, 2)
        csr = ph.ring('cs', [128, 2, 512], F32, 2)
        qst = ph.ring('qst', [128, 5, 512], BF16, 2)
        vst = ph.ring('vst', [128, 4, 130], BF16, 2)
        for v in vst.tiles:
            s.op('dve', lambda e: e.memset(v[:], 1.0), [], [v])
        sqr = ph.ring('sq', [128, 512], BF16, 2)
        tmr = ph.ring('tm', [128, 512], F32, 8)
        pq = ph.ring('pq', [128, 512], F32, 4, psum=True)
        pn = ph.ring('pn', [128, 512], F32, 2, psum=True)
        pv = ph.ring('pv', [128, 128], F32, 2, psum=True)
        cst = self.inp['c_rope']
        for s_ in range(S):
            for (seg, t0, n, g0) in cfg.tiles(512):
                hb = hr.next()
                c0 = cfg.pcol(seg, t0, 2)
                self.ld(hb, hb[:, :, 0:n], self.hT, self.hT[s_, :, c0:c0 + n].rearrange("(kc p) t -> p kc t", p=128))
                cs = csr.next()
                self.ld(cs, cs[:, :, 0:n], cst, cst[:, :, g0:g0 + n])
                self.fl()
                qs = qst.next()
                for c in range(5):
                    w0, w1, col, gi = (wq, wqs, c * 128, 0) if c < 4 else (wk, wks, 0, 2)
                    p0 = pq.next()
                    for kc in range(8):
                        self.mm(p0, p0[:, 0:n], w0, w0[:, kc, col:col + 128], hb, hb[:, kc, 0:n], start=(kc == 0), stop=(kc == 7))
                    p1 = pq.next()
                    for kc in range(8):
                        self.mm(p1, p1[:, 0:n], w1, w1[:, kc, col:col + 128], hb, hb[:, kc, 0:n], start=(kc == 0), stop=(kc == 7))
                    import os
                    SK = os.environ.get('SKIP', '')
                    if 'post' in SK:
                        self.cp('dve', qs, qs[:, c, 0:n], p0, p0[:, 0:n])
                        self.cp('act', sqr.next(), sqr.tiles[0][:, 0:n], p1, p1[:, 0:n])
                        continue
                    sq = sqr.next()
                    if 'nosq' in SK:
                        t0_ = tmr.next()
                        self.cp('act', t0_, t0_[:, 0:n], p0, p0[:, 0:n])
                        self.act(sq, sq[:, 0:n], t0_, t0_[:, 0:n], AF.Square)
                    else:
                        self.act(sq, sq[:, 0:n], p0, p0[:, 0:n], AF.Square)
                    pss = pn.next()
                    self.mm(pss, pss[:, 0:n], self.onesblk, self.onesblk[:], sq, sq[:, 0:n])
                    lnv = tmr.next()
                    self.act(lnv, lnv[:, 0:n], pss, pss[:, 0:n], AF.Ln, bias=self.epsc[:, 0:1], scale=1.0 / 64, extra=[self.epsc])
                    rstd = tmr.next()
                    self.act(rstd, rstd[:, 0:n], lnv, lnv[:, 0:n], AF.Exp, scale=-0.5)
                    a = tmr.next()
                    b = tmr.next()
                    self.act(a, a[:, 0:n], p0, p0[:, 0:n], AF.Copy, scale=gn[:, gi:gi + 1], extra=[gn])
                    self.act(b, b[:, 0:n], p1, p1[:, 0:n], AF.Copy, scale=gn[:, gi + 1:gi + 2], extra=[gn])
                    self.tt('dve', a, a[:, 0:n], a, a[:, 0:n], cs, cs[:, 0, 0:n], ALU.mult)
                    self.tt('dve', b, b[:, 0:n], b, b[:, 0:n], cs, cs[:, 1, 0:n], ALU.mult)
                    self.tt('dve', a, a[:, 0:n], a, a[:, 0:n], b, b[:, 0:n], ALU.add)
                    self.tt('dve', qs, qs[:, c, 0:n], a, a[:, 0:n], rstd, rstd[:, 0:n], ALU.mult)
                self.chk('ap_q')
                if not getattr(cfg, 'skipq', False):
                    self.st(self.qT, self.qT[s_, :, g0:g0 + n].rearrange("(c p) t -> p c t", p=128), qs, qs[:, 0:4, 0:n])
                self.chk('ap_qs1')
                if getattr(cfg, 'ksem', False):
                    if not hasattr(ph, 'ksems'):
                        ph.ksems = {}
                    kr = ph.ksems.setdefault(id(qs), Tile('ksem', None))
                    if kr not in ph.tiles:
                        ph.tiles.append(kr)
                    self.s.defer(self.kT[s_, :, g0:g0 + n], qs[:, 4, 0:n], [qs], [self.kT], kr, 0, {})
                else:
                    self.st(self.kT, self.kT[s_, :, g0:g0 + n], qs, qs[:, 4, 0:n], key=1)
                self.chk('ap_qst')
                vs = vst.next()
                nb = n // 128
                for tb in range(nb if 'v' not in os.environ.get('SKIP', '') else 0):
                    p = pv.next()
                    for kc in range(8):
                        self.mm(p, p[:, :], hb, hb[:, kc, tb * 128:(tb + 1) * 128], wv, wv[:, kc, :], start=(kc == 0), stop=(kc == 7))
                    self.cp('act', vs, vs[:, tb, :].rearrange("p (g d) -> p g d", g=2)[:, :, 0:64], p, p[:, :].rearrange("p (g d) -> p g d", g=2))
                self.chk('ap_v1')
                self.st(self.vtok, self.vtok[s_, g0:g0 + n, :].rearrange("(tb p) c -> p tb c", p=128), vs, vs[:, 0:nb, :])
        ph.close()

    def phase_attn(self, l):
        cfg = self.cfg
        S, TT, TC = cfg.S, cfg.TT, cfg.TC
        s = self.s
        ph = Phase(s)
        NST = TT // 128
        gq = ph.sb('gq', [128, 2, 64], F32)
        self.ld(gq, gq[:], self.inp['attn_gB'], self.inp['attn_gB'][l:l + 1, :, :].to_broadcast([128, 2, 64]))
        mx = ph.sb('mx', [128, 2], F32)
        s.op('dve', lambda e: e.tensor_reduce(out=mx[:], in_=gq[:], axis=mybir.AxisListType.X, op=ALU.max,
                                              apply_absolute_value=True), [gq], [mx])
        negM = ph.sb('negM', [128, 1], F32)
        self.stt(negM, negM[:], mx, mx[:, 0:1], -8.0, mx, mx[:, 1:2], ALU.mult, ALU.mult)
        kt = ph.sb('kt', [64, 2, TT], BF16)
        va = ph.sb('va', [128, NST, 130], BF16)
        qr = ph.ring('q', [64, 8, 512], BF16, 2)
        ptr = ph.ring('pt', [128, 512], BF16, 4)
        ysr = ph.ring('ys', [64, 8, 512], BF16, 2)
        osr = ph.ring('os', [64, 512], F32, 2)
        rcr = ph.ring('rc', [65, 512], F32, 2)
        psr = ph.ring('ps', [128, 512], F32, 4, psum=True)
        por = ph.ring('po', [65, 512], F32, 2, psum=True)
        pbr = ph.ring('pb', [64, 512], F32, 1, psum=True)
        for s_ in range(S):
            self.ld(kt, kt[:], self.kT, self.kT[s_].rearrange("(g d) t -> d g t", g=2))
            self.ld(va, va[:], self.vtok, self.vtok[s_].rearrange("(tb p) c -> p tb c", p=128))
            for (seg, t0, n, g0) in cfg.tiles(512):
                if seg == 0 and l == cfg.L - 1:
                    continue
                q = qr.next()
                self.ld(q, q[:, :, 0:n], self.qT, self.qT[s_, :, g0:g0 + n].rearrange("(h d) t -> d h t", d=64))
                self.fl()
                nst = (TC // 128) if seg == 0 else NST
                ys = ysr.next()
                for h in range(8):
                    g = h // 4
                    po = por.next()
                    def score(st):
                        p_ = psr.next()
                        self.mm(p_, p_[:, 0:n], kt, kt[:, g, st * 128:(st + 1) * 128], q, q[:, h, 0:n])
                        return p_
                    p = score(0)
                    for st in range(nst):
                        p_next = score(st + 1) if st + 1 < nst else None
                        pt = ptr.next()
                        self.act(pt, pt[:, 0:n], p, p[:, 0:n], AF.Exp, bias=negM[:, 0:1], scale=0.125, extra=[negM])
                        self.mm(po, po[0:65, 0:n], va, va[:, st, g * 65:(g + 1) * 65], pt, pt[:, 0:n],
                                start=(st == 0), stop=(st == nst - 1))
                        p = p_next
                    rc = rcr.next()
                    s.op('dve', lambda e: e.reciprocal(out=rc[64:65, 0:n], in_=po[64:65, 0:n]), [po], [rc])
                    os_ = osr.next()
                    self.cp('act', os_, os_[:, 0:n], po, po[0:64, 0:n])
                    pb = pbr.next()
                    self.mm(pb, pb[:, 0:n], self.onesf, self.onesf[64:65, 0:64], rc, rc[64:65, 0:n])
                    self.tt('dve', ys, ys[:, h, 0:n], os_, os_[:, 0:n], pb, pb[:, 0:n], ALU.mult)
                self.st(self.ycT, self.ycT[s_, :, g0:g0 + n].rearrange("(h d) t -> d h t", d=64), ys, ys[:, :, 0:n])
        ph.close()

    def phase_rwkv_proj(self, l):
        cfg = self.cfg
        S, TT = cfg.S, cfg.TT
        s = self.s
        ph = Phase(s)
        CDEC = math.exp(-0.5)
        wc = ph.sb('wc', [128, 8, 1536], BF16)
        ws = ph.sb('ws', [128, 8, 1536], BF16)
        cmask = ph.sb('cmask', [128, 512], F32)
        pp = ph.sb('pp', [128, 9, 4], F32)
        l1c = ph.sb('l1c', [128, 8, 416], BF16)
        l1s = ph.sb('l1s', [128, 8, 416], BF16)
        w2w = ph.sb('w2w', [128, 1, 512], BF16)
        w2a = ph.sb('w2a', [128, 1, 512], BF16)
        g2a = ph.sb('g2a', [128, 1, 512], BF16)
        g2b = ph.sb('g2b', [32, 1, 512], BF16)
        omka = ph.sb('omka', [128, 4], F32)
        outer = ph
        ph = Phase(s)
        self.wstage = ph.ring('wstg', [128, 8, 256], F32, 2)
        r = lambda ap: ap.rearrange("(kc p) c -> p kc c", p=128)
        win = self.inp['w_in']
        self.ld(pp, pp[:], self.inp['rwkv_pT'], self.inp['rwkv_pT'][:, l, :, :])
        self.ts('dve', omka, omka[:], pp, pp[:, 5, :], -1.0, 1.0, op0=ALU.mult, op1=ALU.add)
        mux = ph.sb('mux', [128, 3, 8], F32)
        self.ld(mux, mux[:], self.inp['rwkv_mu_xT'], self.inp['rwkv_mu_xT'][:, l, :, :])
        mxc = ph.sb('mxc', [128, 3, 8], F32)
        mxs = ph.sb('mxs', [128, 3, 8], F32)
        self.ts('dve', mxc, mxc[:], mux, mux[:], -1.0, 1.0, op0=ALU.mult, op1=ALU.add)
        self.ts('dve', mxs, mxs[:], mux, mux[:], 0.5, op0=ALU.mult)
        mur = ph.sb('mur', [128, 1536], F32)
        self.ld(mur, mur[:], self.inp['rwkv_mu_rkv'], self.inp['rwkv_mu_rkv'][l:l + 1, :].to_broadcast([128, 1536]))
        murc = ph.sb('murc', [128, 1536], F32)
        self.ts('dve', murc, murc[:], mur, mur[:], -1.0, 1.0, op0=ALU.mult, op1=ALU.add)
        self.ts('dve', mur, mur[:], mur, mur[:], 0.5, op0=ALU.mult)
        self.load_w(ph, wc, lambda a, b: wc[:, :, a:b], win, lambda a, b: r(win[l, :, a:b]), 1536, 8, piece=256,
                    col_scale=(murc, lambda a, b: murc[:, a:b]))
        self.load_w(ph, ws, lambda a, b: ws[:, :, a:b], win, lambda a, b: r(win[l, :, a:b]), 1536, 8, piece=256,
                    col_scale=(mur, lambda a, b: mur[:, a:b]))
        srcs = [(self.inp['rwkv_w1'], 0, 0, 0), (self.inp['rwkv_w1'], 1, 64, 0), (self.inp['rwkv_a1'], 0, 128, 1),
                (self.inp['rwkv_a1'], 1, 192, 1)]
        for (src, d, off, mi) in srcs:
            for (dst, sc) in ((l1c, mxc), (l1s, mxs)):
                self.load_w(ph, dst, lambda a, b, dst=dst, off=off: dst[:, :, off + a:off + b], src,
                            lambda a, b, src=src, d=d: r(src[l, d, :, a:b]), 64, 8, piece=256,
                            row_scale=(sc, lambda kc, sc=sc, mi=mi: sc[:, mi, kc:kc + 1]))
        g1 = self.inp['rwkv_g1']
        for (dst, sc) in ((l1c, mxc), (l1s, mxs)):
            self.load_w(ph, dst, lambda a, b, dst=dst: dst[:, :, 256 + a:256 + b], g1, lambda a, b: r(g1[l, :, a:b]), 160, 8,
                        piece=256, row_scale=(sc, lambda kc, sc=sc: sc[:, 2, kc:kc + 1]))
        for dst, nm in ((w2w, 'rwkv_w2'), (w2a, 'rwkv_a2')):
            src = self.inp[nm]
            self.load_w(ph, dst, lambda a, b, dst=dst: dst[:, :, a:b], src,
                        lambda a, b, src=src: src[l, :, :, a:b].rearrange("d (o k) c -> (d k) o c", o=1), 512, 1, piece=256)
        g2 = self.inp['rwkv_g2']
        self.load_w(ph, g2a, lambda a, b: g2a[:, :, a:b], g2, lambda a, b: g2[l, 0:128, a:b].rearrange("(o k) c -> k o c", o=1), 512, 1, piece=256)
        stg = self.wstage.next()
        self.ld(stg, stg[0:32, 0, 0:256], g2, g2[l, 128:160, 0:256])
        self.cp('dve', g2b, g2b[:, 0, 0:256], stg, stg[0:32, 0, 0:256])
        stg = self.wstage.next()
        self.ld(stg, stg[0:32, 0, 0:256], g2, g2[l, 128:160, 256:512])
        self.cp('dve', g2b, g2b[:, 0, 256:512], stg, stg[0:32, 0, 0:256])
        self.ld(cmask, cmask[:], self.inp['c_chunkmask'], self.inp['c_chunkmask'][:, :])
        ph.close()
        ph = outer
        self.chk('rp_w')
        hr = ph.ring('hb', [128, 8, 516], BF16, 2)
        hsr = ph.ring('hs', [128, 8, 512], BF16, 1)
        l1o = ph.ring('l1o', [128, 4, 512], BF16, 1)
        F = lambda nm, k=1: ph.ring(nm, [128, 512], F32, k)
        B = lambda nm, k=1: ph.ring(nm, [128, 512], BF16, k)
        rsb_r, ksb_r, vsb_r, kk_r = F('rsb'), F('ksb'), F('vsb'), F('kk')
        vbf_r = B('vbf')
        sg_r, a_r, G_r, Ge_r, Te_r, Hi_r = F('sg', 2), F('a_', 2), F('G', 2), F('Ge', 2), F('Te', 2), F('Hi', 1)
        E_r = F('E', 5)
        kd_r, bd_r, tmp_r = F('kd', 2), F('bd', 2), F('tmp', 4)
        sq_r = B('sq', 2)
        fst = ph.ring('fst', [128, 8, 512], BF16, 1)
        kb_r = B('kb', 4)
        tms = ph.ring('tms', [128, 5, 4, 512], BF16, 1)
        ost = ph.ring('ost', [128, 2, 512], BF16, 2)
        gcs = ph.ring('gcs', [128, 2, 8], F32, 2)
        pp_r = ph.ring('pp', [128, 512], F32, 6, psum=True)
        pt_r = ph.ring('ptr', [128, 4, 128], BF16, 2, psum=True)
        for s_ in range(S):
            for (seg, t0, n, g0) in cfg.tiles(512):
                nb = n // 128
                nch = n // 64
                hb = hr.next()
                c0 = cfg.pcol(seg, t0, 2)
                self.ld(hb, hb[:, :, 0:n + 4], self.hT, self.hT[s_, :, c0 - 2:c0 + n + 2].rearrange("(kc p) t -> p kc t", p=128))
                self.fl()
                hs = hsr.next()
                self.tt('dve', hs, hs[:, :, 0:n], hb, hb[:, :, 1:1 + n], hb, hb[:, :, 3:3 + n], ALU.add)
                hc = lambda kc: hb[:, kc, 2:2 + n]

                def proj(p, pap, wA, wB, col, m):
                    for kc in range(8):
                        self.mm(p, pap, wA, wA[:, kc, col:col + m], hb, hc(kc), start=(kc == 0), stop=False)
                    for kc in range(8):
                        self.mm(p, pap, wB, wB[:, kc, col:col + m], hs, hs[:, kc, 0:n], start=False, stop=(kc == 7))
                lo = l1o.next()
                for i, (col, m, fn) in enumerate(((0, 128, AF.Tanh), (128, 128, AF.Copy), (256, 128, AF.Sigmoid), (384, 32, AF.Sigmoid))):
                    p = pp_r.next()
                    proj(p, p[0:m, 0:n], l1c, l1s, col, m)
                    self.act(lo, lo[0:m, i, 0:n], p, p[0:m, 0:n], fn)
                self.chk('rp_l1')
                tm = tms.next()
                for c in range(4):
                    cs_ = slice(c * 128, (c + 1) * 128)
                    pr, pk, pv = pp_r.next(), pp_r.next(), pp_r.next()
                    proj(pr, pr[:, 0:n], wc, ws, c * 128, 128)
                    proj(pk, pk[:, 0:n], wc, ws, 512 + c * 128, 128)
                    proj(pv, pv[:, 0:n], wc, ws, 1024 + c * 128, 128)
                    self.chk('rp_p')
                    rsb, ksb, vsb, vbf = rsb_r.next(), ksb_r.next(), vsb_r.next(), vbf_r.next()
                    self.cp('act', rsb, rsb[:, 0:n], pr, pr[:, 0:n])
                    self.cp('act', ksb, ksb[:, 0:n], pk, pk[:, 0:n])
                    self.cp('act', vsb, vsb[:, 0:n], pv, pv[:, 0:n])
                    self.cp('dve', vbf, vbf[:, 0:n], pv, pv[:, 0:n])
                    self.chk('rp_cp')
                    kk = kk_r.next()
                    self.ts('dve', kk, kk[:, 0:n], ksb, ksb[:, 0:n], pp[:, 4, c:c + 1], extra=[pp])
                    self.chk('rp_ts')
                    sq = sq_r.next()
                    self.act(sq, sq[:, 0:n], kk, kk[:, 0:n], AF.Square)
                    pss = pp_r.next()
                    self.mm(pss, pss[:, 0:n], self.onesblk, self.onesblk[:], sq, sq[:, 0:n])
                    t1 = tmp_r.next()
                    self.act(t1, t1[:, 0:n], pss, pss[:, 0:n], AF.Ln, bias=self.epsc[:, 0:1], extra=[self.epsc])
                    t2 = tmp_r.next()
                    self.act(t2, t2[:, 0:n], t1, t1[:, 0:n], AF.Exp, scale=-0.5)
                    self.tt('dve', kk, kk[:, 0:n], kk, kk[:, 0:n], t2, t2[:, 0:n], ALU.mult)
                    self.chk('rp_kk')
                    fs = fst.next()
                    ksum = None
                    gc = gcs.next()
                    for d in range(2):
                        b0 = 64 * d
                        pw, pa = pp_r.next(), pp_r.next()
                        self.mm(pw, pw[:, 0:n], w2w, w2w[b0:b0 + 64, 0, cs_], lo, lo[b0:b0 + 64, 0, 0:n])
                        self.mm(pa, pa[:, 0:n], w2a, w2a[b0:b0 + 64, 0, cs_], lo, lo[b0:b0 + 64, 1, 0:n])
                        sg, a_ = sg_r.next(), a_r.next()
                        self.act(sg, sg[:, 0:n], pw, pw[:, 0:n], AF.Sigmoid, bias=pp[:, 0 + d, c:c + 1], extra=[pp])
                        self.act(a_, a_[:, 0:n], pa, pa[:, 0:n], AF.Sigmoid, bias=pp[:, 2 + d, c:c + 1], extra=[pp])
                        G, Ge, Te = G_r.next(), Ge_r.next(), Te_r.next()
                        s.op('dve', lambda e: e.tensor_tensor_scan(out=G[:, 0:n], data0=cmask[:, 0:n], data1=sg[:, 0:n],
                                                                    initial=0.0, op0=ALU.mult, op1=ALU.add), [cmask, sg], [G])
                        self.tt('dve', Ge, Ge[:, 0:n], G, G[:, 0:n], sg, sg[:, 0:n], ALU.subtract)
                        Gv = G[:, 0:n].rearrange("p (c t) -> p c t", t=64)
                        self.tt('dve', Te, Te[:, 0:n].rearrange("p (c t) -> p c t", t=64), G,
                                Gv[:, :, 63:64].to_broadcast([128, nch, 64]), G, Gv, ALU.subtract)
                        self.act(gc, gc[:, d, 0:nch], G, Gv[:, :, 63], AF.Exp, scale=-CDEC)
                        if d == 0:
                            inc, exc, toend = G, Ge, Te
                        else:
                            Hi = Hi_r.next()
                            self.tt('dve', Hi, Hi[:, 0:n], Te, Te[:, 0:n], sg, sg[:, 0:n], ALU.add)
                            inc, exc, toend = Hi, Te, Ge
                        E1, E2, E3, E4 = E_r.next(), E_r.next(), E_r.next(), E_r.next()
                        self.act(E1, E1[:, 0:n], inc, inc[:, 0:n], AF.Exp, scale=-CDEC)
                        self.act(E2, E2[:, 0:n], inc, inc[:, 0:n], AF.Exp, scale=CDEC)
                        self.act(E3, E3[:, 0:n], exc, exc[:, 0:n], AF.Exp, scale=-CDEC)
                        self.act(E4, E4[:, 0:n], toend, toend[:, 0:n], AF.Exp, scale=-CDEC)
                        self.chk('rp_exp')
                        kd, bd = kd_r.next(), bd_r.next()
                        self.ts('dve', kd, kd[:, 0:n], a_, a_[:, 0:n], pp[:, 5, c:c + 1], omka[:, c:c + 1], op0=ALU.mult, op1=ALU.add,
                                extra=[pp, omka])
                        self.tt('dve', kd, kd[:, 0:n], kd, kd[:, 0:n], ksb, ksb[:, 0:n], ALU.mult)
                        self.tt('dve', bd, bd[:, 0:n], kk, kk[:, 0:n], a_, a_[:, 0:n], ALU.mult)
                        self.tt('dve', fs, fs[:, d * 4 + 0, 0:n], rsb, rsb[:, 0:n], E1, E1[:, 0:n], ALU.mult)
                        self.tt('dve', fs, fs[:, d * 4 + 1, 0:n], kd, kd[:, 0:n], E2, E2[:, 0:n], ALU.mult)
                        self.tt('dve', fs, fs[:, d * 4 + 2, 0:n], bd, bd[:, 0:n], E2, E2[:, 0:n], ALU.mult)
                        self.stt(fs, fs[:, d * 4 + 3, 0:n], kk, kk[:, 0:n], -1.0, E3, E3[:, 0:n], ALU.mult, ALU.mult)
                        ke, be = kb_r.next(), kb_r.next()
                        self.tt('dve', ke, ke[:, 0:n], kd, kd[:, 0:n], E4, E4[:, 0:n], ALU.mult)
                        self.tt('dve', be, be[:, 0:n], bd, bd[:, 0:n], E4, E4[:, 0:n], ALU.mult)
                        self.chk('rp_fs')
                        for ti, src in ((2 * d, ke), (2 * d + 1, be)):
                            pt = pt_r.next()
                            for tb in range(nb):
                                s.op('pe', lambda e: e.transpose(pt[:, tb, :], src[:, tb * 128:(tb + 1) * 128], self.ident[:]),
                                     [src, self.ident], [pt])
                            self.cp('act', tm, tm[:, ti, 0:nb, cs_], pt, pt[:, 0:nb, :])
                        if d == 0:
                            ksum = tmp_r.next()
                            self.cp('dve', ksum, ksum[:, 0:n], kd, kd[:, 0:n])
                        else:
                            self.tt('dve', ksum, ksum[:, 0:n], ksum, ksum[:, 0:n], kd, kd[:, 0:n], ALU.add)
                    pt = pt_r.next()
                    for tb in range(nb):
                        s.op('pe', lambda e: e.transpose(pt[:, tb, :], vbf[:, tb * 128:(tb + 1) * 128], self.ident[:]),
                             [vbf, self.ident], [pt])
                    self.cp('act', tm, tm[:, 4, 0:nb, cs_], pt, pt[:, 0:nb, :])
                    sq2 = sq_r.next()
                    self.stt(sq2, sq2[:, 0:n], ksum, ksum[:, 0:n], pp[:, 6, c:c + 1], rsb, rsb[:, 0:n], ALU.mult, ALU.mult, extra=[pp])
                    pb = pp_r.next()
                    self.mm(pb, pb[:, 0:n], self.onesblk, self.onesblk[:], sq2, sq2[:, 0:n])
                    os_ = ost.next()
                    self.tt('dve', os_, os_[:, 0, 0:n], vsb, vsb[:, 0:n], pb, pb[:, 0:n], ALU.mult)
                    pg = pp_r.next()
                    self.mm(pg, pg[:, 0:n], g2a, g2a[:, 0, cs_], lo, lo[:, 2, 0:n], start=True, stop=False)
                    self.mm(pg, pg[:, 0:n], g2b, g2b[:, 0, cs_], lo, lo[0:32, 3, 0:n], start=False, stop=True)
                    self.cp('act', os_, os_[:, 1, 0:n], pg, pg[:, 0:n])
                    self.chk('rp_c0')
                    for d in range(2):
                        self.st(self.rw_fm, self.rw_fm[s_, d, :, c * 128:(c + 1) * 128, g0:g0 + n].rearrange("k p t -> p k t"),
                                fs, fs[:, d * 4:(d + 1) * 4, 0:n], key=d)
                    self.st(self.rw_bg, self.rw_bg[s_, :, c * 128:(c + 1) * 128, g0:g0 + n].rearrange("k p t -> p k t"), os_, os_[:, :, 0:n])
                    self.st(self.rw_gc, self.rw_gc[s_, :, c * 128:(c + 1) * 128, g0 // 64:g0 // 64 + nch].rearrange("d p c -> p d c"),
                            gc, gc[:, :, 0:nch])
                for ti in range(5):
                    self.st(self.rw_tm, self.rw_tm[s_, ti, g0:g0 + n, :].rearrange("(tb p) f -> p tb f", p=128), tm, tm[:, ti, 0:nb, :], key=ti)
        ph.close()

    def dplr_scan(self, spec):
        cfg = self.cfg
        s = self.s
        H, Kd, Vd = spec['H'], spec['Kd'], spec['Vd']
        NCC, NCL = cfg.TC // 64, cfg.TL // 64
        NC = NCC + NCL
        ph = Phase(s)
        kinds = spec['kinds']
        nk = spec['nfm']
        ki = spec['ki']
        ntm = spec['ntm']
        ti_ = spec['ti']
        chains = spec['chains']
        nchain = len(chains)
        masks = ph.sb('masks', [64, 2, 128], F32)
        self.ld(masks, masks[:], self.inp['c_scanmask'], self.inp['c_scanmask'][:, :, :])
        identf = ph.sb('identf', [64, 64], F32)
        self.ld(identf, identf[:], self.inp['c_ident64'], self.inp['c_ident64'][:, :])
        fm_r = [ph.ring(f'fm{i}', [Kd, nk, H, 64], BF16, 2) for i in range(nchain)]
        tm_r = [ph.ring(f'tm{i}', [64, ntm, H * max(Kd, Vd)], BF16, 2) for i in range(nchain)]
        dm_r = [ph.ring(f'dm{i}', [64, H, 128], BF16, 2) for i in range(nchain)] if spec.get('dmat') else None
        gc_t = [ph.sb(f'gc{i}', [Kd, H, NC], F32) for i in range(nchain)]
        S_t = [ph.sb(f'S{i}', [Kd, H, Vd], F32) for i in range(nchain)]
        Sb_t = [ph.sb(f'Sb{i}', [Kd, H, Vd], BF16) for i in range(nchain)]
        M_r = [ph.ring(f'Msb{i}', [64, H, 2, 128], BF16, 2) for i in range(nchain)]
        sq_r = [ph.ring(f'sqm{i}', [64, H, 64], F32, 5) for i in range(nchain)]
        X_r = [ph.ring(f'X{i}', [64, H, 64], F32, 3) for i in range(nchain)]
        off_r = [ph.ring(f'off{i}', [64, H, 64], F32, 1) for i in range(nchain)]
        bm = ph.sb('bm', [64, 2, 64], F32)
        self.ld(bm, bm[:], self.inp['c_blkmask'], self.inp['c_blkmask'][:, :, :])
        Xf_r = [ph.ring(f'Xf{i}', [64, H, 64], BF16, 2) for i in range(nchain)]
        wt_r = ph.ring('wt', [64, H, Vd], BF16, nchain)
        u_r = ph.ring('u', [64, H, Vd], BF16, nchain)
        y_r = ph.ring('y', [Vd, H, 64], F32, 2)
        pM = ph.ring('pM', [64, 2, 2, 128], F32, 2, psum=True)
        pG = ph.ring('pG', [128, 512], F32, 6, psum=True)
        hv = lambda p: p[0:64, 0:H * 64].rearrange("p (h t) -> p h t", h=H)
        wv = lambda p: p[0:64, 0:H * Vd].rearrange("p (h t) -> p h t", h=H)
        yv = lambda p: p[0:Vd, 0:H * 64].rearrange("p (h t) -> p h t", h=H)
        sv = lambda p: p[0:Kd, 0:H * Vd].rearrange("p (h t) -> p h t", h=H)
        for i, (s_, d) in enumerate(chains):
            spec['load_gc'](gc_t[i], s_, d)
            s.op('dve', lambda e: e.memset(S_t[i][:], 0.0), [], [S_t[i]])
            s.op('dve', lambda e: e.memset(Sb_t[i][:], 0.0), [], [Sb_t[i]])
        order = {0: list(range(NC)), 1: list(range(NCC - 1, -1, -1)) + list(range(NC - 1, NCC - 1, -1))}
        for step in range(NC):
            cur = []
            for i, (s_, d) in enumerate(chains):
                ci = order[d][step]
                fm = fm_r[i].next()
                tm = tm_r[i].next()
                spec['load_fm'](fm, s_, d, ci)
                spec['load_tm'](tm, s_, d, ci)
                dm = None
                if dm_r is not None:
                    dm = dm_r[i].next()
                    spec['load_dm'](dm, s_, d, ci)
                cur.append((i, s_, d, ci, fm, tm, dm))
            self.fl()
            st = {}
            def stage_a(i, s_, d, ci, fm, tm, dm):
                Msb = M_r[i].next()
                AR = lambda h: fm[:, ki['A']:ki['A'] + 2, h, :]
                for h0 in range(0, H, 2):
                    p = pM.next()
                    for hh in range(2):
                        h = h0 + hh
                        self.mm(p, p[:, hh, 0, :].rearrange("p (a t) -> p a t", a=2), fm, fm[:, ki['B'], h, :], fm, AR(h))
                        self.mm(p, p[:, hh, 1, :].rearrange("p (a t) -> p a t", a=2), fm, fm[:, ki['K'], h, :], fm, AR(h))
                    if dm is None:
                        self.tt('dve', Msb, Msb[:, h0:h0 + 2, :, :], p, p[:, :, :, :], masks,
                                masks[:, d, :].unsqueeze(1).unsqueeze(1).to_broadcast([64, 2, 2, 128]), ALU.mult)
                    else:
                        self.tt('dve', Msb, Msb[:, h0:h0 + 2, :, :], p, p[:, :, :, :], dm,
                                dm[:, h0:h0 + 2, :].unsqueeze(2).to_broadcast([64, 2, 2, 128]), ALU.mult)
                yield
                bmb = bm[:, 0, :].unsqueeze(1).to_broadcast([64, H, 64])
                bmcb = bm[:, 1, :].unsqueeze(1).to_broadcast([64, H, 64])
                idb = identf[:].unsqueeze(1).to_broadcast([64, H, 64])
                P0 = sq_r[i].next()
                self.cp('act', P0, P0[:], Msb, Msb[:, :, 0, 0:64])
                pt = pG.next()
                for h in range(H):
                    self.mmr(pt, hv(pt)[:, h, :], P0, P0[:, h, :], identf, identf[:, :])
                PT = sq_r[i].next()
                self.tt('dve', PT, PT[:], pt, hv(pt), bm, bmb, ALU.mult)
                PToff = off_r[i].next()
                self.tt('dve', PToff, PToff[:], pt, hv(pt), bm, bmcb, ALU.mult)
                P = sq_r[i].next()
                self.tt('dve', P, P[:], P0, P0[:], bm, bmb, ALU.mult)
                X = X_r[i].next()
                self.tt('dve', X, X[:], P, P[:], identf, idb, ALU.add)
                yield
                for lev in range(1, 5):
                    last = lev == 4
                    if not last:
                        p2 = pG.next()
                        p2v = hv(p2)
                        for h in range(H):
                            self.mmr(p2, p2v[:, h, :], PT, PT[:, h, :], P, P[:, h, :])
                    p2t = pG.next()
                    p2tv = hv(p2t)
                    for h in range(H):
                        self.mmr(p2t, p2tv[:, h, :], P, P[:, h, :], PT, PT[:, h, :])
                    nPT = sq_r[i].next()
                    self.cp('act', nPT, nPT[:], p2t, p2tv)
                    if not last:
                        nP = sq_r[i].next()
                        self.cp('act', nP, nP[:], p2, p2v)
                        P = nP
                    PT = nPT
                    px = pG.next()
                    pxv = hv(px)
                    for h in range(H):
                        self.mmr(px, pxv[:, h, :], PT, PT[:, h, :], X, X[:, h, :])
                    nX = X_r[i].next()
                    self.tt('dve', nX, nX[:], X, X[:], px, pxv, ALU.add)
                    X = nX
                    yield
                pxt = pG.next()
                for h in range(H):
                    self.mmr(pxt, hv(pxt)[:, h, :], X, X[:, h, :], identf, identf[:, :])
                XT = X_r[i].next()
                self.cp('act', XT, XT[:], pxt, hv(pxt))
                p1 = pG.next()
                for h in range(H):
                    self.mmr(p1, hv(p1)[:, h, :], PToff, PToff[:, h, :], X, X[:, h, :])
                T1 = X_r[i].next()
                self.cp('act', T1, T1[:], p1, hv(p1))
                yield
                p2_ = pG.next()
                for h in range(H):
                    self.mmr(p2_, hv(p2_)[:, h, :], XT, XT[:, h, :], T1, T1[:, h, :])
                Xf = Xf_r[i].next()
                self.tt('dve', Xf, Xf[:], X, X[:], p2_, hv(p2_), ALU.add)
                X = Xf
                st[i] = (Msb, X)
                if step == 0 and i == 0 and 'dbg_scan' in cfg.debug:
                    self.st(self.dbgT, self.dbgT[:, 0, 0:H * 256], Msb, Msb[:].rearrange("p h a t -> p (h a t)"))
                    self.st(self.dbgT, self.dbgT[:, 1, 0:H * 64], X, X[:].rearrange("p h t -> p (h t)"))
                    self.st(self.dbgT, self.dbgT[:, 4, 0:H * 64], PT0, PT0[:].rearrange("p h t -> p (h t)"))
            gens = [stage_a(*c) for c in cur]
            alive = list(gens)
            while alive:
                nxt = []
                for g_ in alive:
                    try:
                        next(g_)
                        nxt.append(g_)
                    except StopIteration:
                        pass
                alive = nxt
            wts = {}
            for (i, s_, d, ci, fm, tm, dm) in cur:
                Msb, X = st[i]
                p = pG.next()
                for h in range(H):
                    self.mm(p, wv(p)[:, h, :], fm, fm[:, ki['As'], h, :], Sb_t[i], Sb_t[i][:, h, :], start=True, stop=False)
                    self.mm(p, wv(p)[:, h, :], Msb, Msb[:, h, 1, 0:64], tm, tm[:, ti_['V'], h * Vd:(h + 1) * Vd], start=False, stop=True)
                wt = wt_r.next()
                self.cp('act', wt, wt[:], p, wv(p))
                wts[i] = wt
                if step == 0 and i == 0 and 'dbg_scan' in cfg.debug:
                    self.st(self.dbgT, self.dbgT[:, 2, 0:H * Vd], wt, wt[:].rearrange("p h t -> p (h t)"))
            us = {}
            for (i, s_, d, ci, fm, tm, dm) in cur:
                Msb, X = st[i]
                p = pG.next()
                for h in range(H):
                    self.mm(p, wv(p)[:, h, :], X, X[:, h, :], wts[i], wts[i][:, h, :])
                u = u_r.next()
                self.cp('dve', u, u[:], p, wv(p))
                us[i] = u
                if step == 0 and i == 0 and 'dbg_scan' in cfg.debug:
                    self.st(self.dbgT, self.dbgT[:, 3, 0:H * Vd], u, u[:].rearrange("p h t -> p (h t)"))
            for (i, s_, d, ci, fm, tm, dm) in cur:
                Msb, X = st[i]
                u = us[i]
                p = pG.next()
                for h in range(H):
                    V_h = tm[:, ti_['V'], h * Vd:(h + 1) * Vd]
                    self.mm(p, yv(p)[:, h, :], Sb_t[i], Sb_t[i][:, h, :], fm, fm[:, ki['Rs'], h, :], start=True, stop=False)
                    self.mm(p, yv(p)[:, h, :], u, u[:, h, :], Msb, Msb[:, h, 0, 64:128], start=False, stop=False)
                    self.mm(p, yv(p)[:, h, :], tm, V_h, Msb, Msb[:, h, 1, 64:128], start=False, stop=True)
                y = y_r.next()
                self.cp('act', y, y[:], p, yv(p))
                spec['store_y'](y, s_, d, ci)
                p = pG.next()
                for h in range(H):
                    V_h = tm[:, ti_['V'], h * Vd:(h + 1) * Vd]
                    self.mm(p, sv(p)[:, h, :], tm, tm[:, ti_['Bend'], h * Kd:(h + 1) * Kd], u, u[:, h, :], start=True, stop=False)
                    self.mm(p, sv(p)[:, h, :], tm, tm[:, ti_['Kend'], h * Kd:(h + 1) * Kd], tm, V_h, start=False, stop=True)
                S = S_t[i]
                self.tt('dve', S, S[:], S, S[:], gc_t[i], gc_t[i][:, :, ci:ci + 1].to_broadcast([Kd, H, Vd]), ALU.mult)
                self.tt('dve', S, S[:], S, S[:], p, sv(p), ALU.add)
                self.cp('act', Sb_t[i], Sb_t[i][:], S, S[:])
        ph.close()

    def phase_rwkv_scan(self, l):
        cfg = self.cfg
        S = cfg.S

        def load_fm(fm, s_, d, ci):
            for k in range(4):
                self.ld(fm, fm[:, k, :, :], self.rw_fm, self.rw_fm[s_, d, k, :, ci * 64:(ci + 1) * 64].rearrange("(h k) t -> k h t", k=64), key=k)

        def load_tm(tm, s_, d, ci):
            for j, ti in enumerate((2 * d, 2 * d + 1, 4)):
                self.ld(tm, tm[:, j, :], self.rw_tm, self.rw_tm[s_, ti, ci * 64:(ci + 1) * 64, :], key=j)

        def load_gc(gc, s_, d):
            self.ld(gc, gc[:], self.rw_gc, self.rw_gc[s_, d, :, :].rearrange("(h k) c -> k h c", k=64))

        def store_y(y, s_, d, ci):
            self.st(self.rw_y, self.rw_y[s_, d, :, ci * 64:(ci + 1) * 64].rearrange("(h v) t -> v h t", v=64), y, y[:])

        def load_fm2(fm, s_, d, ci):
            for j, k in enumerate((1, 2, 3, 0)):
                self.ld(fm, fm[:, j, :, :], self.rw_fm, self.rw_fm[s_, d, k, :, ci * 64:(ci + 1) * 64].rearrange("(h k) t -> k h t", k=64), key=j)
        spec = dict(H=8, Kd=64, Vd=64, kinds=None, nfm=4, ki=dict(K=0, B=1, A=2, R=3, As=2, Rs=3), ntm=3,
                    ti=dict(Kend=0, Bend=1, V=2), chains=[(s_, d) for s_ in range(S) for d in range(2)],
                    load_fm=load_fm2, load_tm=load_tm, load_gc=load_gc, store_y=store_y)
        self.dplr_scan(spec)

    def phase_rwkv_post(self, l):
        cfg = self.cfg
        S = cfg.S
        ph = Phase(self.s)
        pp = ph.sb('pp', [128, 9, 4], F32)
        self.ld(pp, pp[:], self.inp['rwkv_pT'], self.inp['rwkv_pT'][:, l, :, :])
        lneps = ph.sb('lneps', [128, 1], F32)
        self.s.op('dve', lambda e: e.memset(lneps[:], 64e-5), [], [lneps])
        yr = ph.ring('yy', [128, 2, 4, 512], F32, 2)
        bgr = ph.ring('bg', [128, 2, 4, 512], BF16, 2)
        osr = ph.ring('os', [128, 4, 512], BF16, 2)
        o_r = ph.ring('o', [128, 512], F32, 2)
        ob_r = ph.ring('ob', [128, 512], BF16, 2)
        t_r = ph.ring('t', [128, 512], F32, 4)
        pm = ph.ring('pm', [128, 512], F32, 2, psum=True)
        pv = ph.ring('pv', [128, 512], F32, 2, psum=True)
        for s_ in range(S):
            for (seg, t0, n, g0) in cfg.tiles(512):
                if seg == 0 and l == cfg.L - 1:
                    continue
                yy = yr.next()
                for d in range(2):
                    self.ld(yy, yy[:, d, :, 0:n], self.rw_y, self.rw_y[s_, d, :, g0:g0 + n].rearrange("(c p) t -> p c t", p=128), key=d)
                bg = bgr.next()
                for k in range(2):
                    self.ld(bg, bg[:, k, :, 0:n], self.rw_bg, self.rw_bg[s_, k, :, g0:g0 + n].rearrange("(c p) t -> p c t", p=128), key=k)
                self.fl()
                os_ = osr.next()
                for c in range(4):
                    o = o_r.next()
                    self.tt('dve', o, o[:, 0:n], yy, yy[:, 0, c, 0:n], yy, yy[:, 1, c, 0:n], ALU.add)
                    ob = ob_r.next()
                    self.cp('act', ob, ob[:, 0:n], o, o[:, 0:n])
                    p = pm.next()
                    self.mm(p, p[:, 0:n], self.onesblk, self.onesblk[:], ob, ob[:, 0:n])
                    mt = t_r.next()
                    self.act(mt, mt[:, 0:n], p, p[:, 0:n], AF.Copy, scale=-1.0 / 64)
                    self.tt('dve', o, o[:, 0:n], o, o[:, 0:n], mt, mt[:, 0:n], ALU.add)
                    sq = ob_r.next()
                    self.act(sq, sq[:, 0:n], o, o[:, 0:n], AF.Square)
                    p2 = pv.next()
                    self.mm(p2, p2[:, 0:n], self.onesblk, self.onesblk[:], sq, sq[:, 0:n])
                    t1 = t_r.next()
                    self.act(t1, t1[:, 0:n], p2, p2[:, 0:n], AF.Ln, bias=lneps[:, 0:1], scale=1.0 / 64, extra=[lneps])
                    t2 = t_r.next()
                    self.act(t2, t2[:, 0:n], t1, t1[:, 0:n], AF.Exp, scale=-0.5)
                    self.stt(o, o[:, 0:n], o, o[:, 0:n], pp[:, 7, c:c + 1], t2, t2[:, 0:n], ALU.mult, ALU.mult, extra=[pp])
                    self.stt(o, o[:, 0:n], o, o[:, 0:n], pp[:, 8, c:c + 1], bg, bg[:, 0, c, 0:n], ALU.add, ALU.add, extra=[pp])
                    self.tt('dve', os_, os_[:, c, 0:n], o, o[:, 0:n], bg, bg[:, 1, c, 0:n], ALU.mult)
                self.st(self.yaT, self.yaT[s_, :, g0:g0 + n].rearrange("(c p) t -> p c t", p=128), os_, os_[:, :, 0:n])
        ph.close()

    def phase_gdn_proj(self, l):
        cfg = self.cfg
        S = cfg.S
        s = self.s
        ph = Phase(s)
        self.wstage = ph.ring('wstg', [128, 8, 256], F32, 2)
        r = lambda ap: ap.rearrange("(kc p) c -> p kc c", p=128)
        win = self.inp['w_in']
        wu = ph.sb('wu', [128, 8, 2048], BF16)
        self.load_w(ph, wu, lambda a, b: wu[:, :, a:b], win, lambda a, b: r(win[l, :, 1536 + a:1536 + b]), 2048, 8, piece=256)
        wab = ph.sb('wab', [128, 8, 16], BF16)
        for j, (nm, d) in enumerate((('gdn_w_alpha', 0), ('gdn_w_alpha', 1), ('gdn_w_beta', 0), ('gdn_w_beta', 1))):
            src = self.inp[nm]
            self.load_w(ph, wab, lambda a, b, j=j: wab[:, :, 4 * j + a:4 * j + b], src, lambda a, b, src=src, d=d: r(src[l, d, :, a:b]), 4, 8, piece=256)
        hr = ph.ring('hb', [128, 8, 512], BF16, 2)
        ust = ph.ring('ust', [128, 16, 512], BF16, 2)
        abr = ph.ring('ab', [16, 512], F32, 2)
        pu = ph.ring('pu', [128, 512], F32, 4, psum=True)
        for s_ in range(S):
            for (seg, t0, n, g0) in cfg.tiles(512):
                hb = hr.next()
                c0 = cfg.pcol(seg, t0, 2)
                self.ld(hb, hb[:, :, 0:n], self.hT, self.hT[s_, :, c0:c0 + n].rearrange("(kc p) t -> p kc t", p=128))
                self.fl()
                us = ust.next()
                for cc in range(16):
                    p = pu.next()
                    for kc in range(8):
                        self.mm(p, p[:, 0:n], wu, wu[:, kc, cc * 128:(cc + 1) * 128], hb, hb[:, kc, 0:n], start=(kc == 0), stop=(kc == 7))
                    if cc < 12:
                        self.cp('act' if cc % 2 else 'dve', us, us[:, cc, 0:n], p, p[:, 0:n])
                    else:
                        self.act(us, us[:, cc, 0:n], p, p[:, 0:n], AF.Silu)
                p = pu.next()
                for kc in range(8):
                    self.mm(p, p[0:16, 0:n], wab, wab[:, kc, :], hb, hb[:, kc, 0:n], start=(kc == 0), stop=(kc == 7))
                ab = abr.next()
                self.cp('act', ab, ab[:, 0:n], p, p[0:16, 0:n])
                self.st(self.gd_u, self.gd_u[s_, :, c0:c0 + n].rearrange("(c p) t -> p c t", p=128), us, us[:, 0:12, 0:n], key=0)
                self.st(self.gd_z, self.gd_z[s_, :, g0:g0 + n].rearrange("(c p) t -> p c t", p=128), us, us[:, 12:16, 0:n], key=1)
                self.st(self.gd_ab, self.gd_ab[s_, :, g0:g0 + n], ab, ab[:, 0:n])
        ph.close()

    def phase_gdn_prep(self, l):
        cfg = self.cfg
        S = cfg.S
        s = self.s
        ph = Phase(s)
        cw = ph.sb('cw', [128, 12, 5], F32)
        self.ld(cw, cw[:], self.inp['gdn_convT'], self.inp['gdn_convT'][:, l, :, :])
        dg = ph.sb('dg', [128, 12, 5, 128], BF16)
        for cc in range(12):
            for j in range(5):
                self.ts('dve', dg, dg[:, cc, j, :], self.ident, self.ident[:], cw[:, cc, j:j + 1], extra=[cw])
        rowp = ph.sb('rowp', [16, 4], F32)
        self.ld(rowp, rowp[:, 0:3], self.inp['gdn_rowp'], self.inp['gdn_rowp'][:, l, :])
        self.act(rowp, rowp[:, 3:4], rowp, rowp[:, 1:2], AF.Exp)
        self.ts('dve', rowp, rowp[:, 3:4], rowp, rowp[:, 3:4], -1.0)
        selb = ph.sb('selb', [16, 16, 128], BF16)
        self.ld(selb, selb[:], self.inp['c_selb'], self.inp['c_selb'][:, :, :])
        self32 = ph.sb('self32', [16, 16, 64], F32)
        self.ld(self32, self32[:], self.inp['c_self'], self.inp['c_self'][:, :, :])
        id16 = ph.sb('id16', [16, 16], F32)
        self.ld(id16, id16[:], self.inp['c_ident64'], self.inp['c_ident64'][0:16, 0:16])
        nmask = ph.sb('nmask', [64, 2, 2, 64], F32)
        self.ld(nmask, nmask[:], self.inp['c_negmask'], self.inp['c_negmask'][:, :, :, :])
        cm16 = ph.sb('cm16', [16, 512], F32)
        self.ld(cm16, cm16[:], self.inp['c_chunkmask'], self.inp['c_chunkmask'][0:16, :])
        one16 = ph.sb('one16', [16, 1], F32)
        s.op('dve', lambda e: e.memset(one16[:], 1.0), [], [one16])
        ur = ph.ring('ub', [128, 12, 516], BF16, 2)
        abr = ph.ring('ab', [16, 512], F32, 2)
        R16 = lambda nm, k=1: ph.ring(nm, [16, 512], F32, k)
        e_r, g_r, G_r, Te_r, Hi_r, Gi_r, ga_r, ee_r, be_r = (R16('e'), R16('g'), R16('G'), R16('Te'), R16('Hi'), R16('Gi'),
                                                            R16('ga'), R16('ee'), R16('be'))
        rb_r = ph.ring('rb', [16, 2, 512], BF16, 1)
        gcr = ph.ring('gcr', [16, 8], F32, 2)
        grep_r = ph.ring('grep', [64, 8, 512], F32, 1)
        gcol_r = ph.ring('gcol', [64, 8, 16], F32, 1)
        tcol_r = ph.ring('tcol', [128, 2, 4, 16], F32, 1)
        dd_r = ph.ring('dd', [64, 4, 2, 64], F32, 2)
        dm_r = ph.ring('dmo', [64, 4, 2, 64], BF16, 3)
        qkv_r = ph.ring('qkv', [128, 512], F32, 3)
        kn_r = ph.ring('kn', [128, 512], BF16, 2)
        qn_r = ph.ring('qn', [128, 512], BF16, 2)
        vb_r = ph.ring('vb', [128, 512], BF16, 2)
        sq_r = ph.ring('sq', [128, 512], BF16, 2)
        t_r = ph.ring('t', [128, 512], F32, 3)
        rep_r = ph.ring('rep', [128, 2, 512], F32, 2)
        fst = ph.ring('fst', [128, 8, 512], BF16, 2)
        tms = ph.ring('tms', [128, 4, 4, 512], BF16, 1)
        pc = ph.ring('pc', [128, 512], F32, 5, psum=True)
        ptp = ph.ring('ptp', [128, 4, 128], BF16, 2, psum=True)
        for s_ in range(S):
            for (seg, t0, n, g0) in cfg.tiles(512):
                nb, nch = n // 128, n // 64
                ub = ur.next()
                c0 = cfg.pcol(seg, t0, 2)
                self.ld(ub, ub[:, :, 0:n + 4], self.gd_u, self.gd_u[s_, :, c0 - 2:c0 + n + 2].rearrange("(c p) t -> p c t", p=128))
                ab = abr.next()
                self.ld(ab, ab[:, 0:n], self.gd_ab, self.gd_ab[s_, :, g0:g0 + n])
                self.fl()
                e, g, G, Te, Hi, Gi, ga, ee, be = (x.next() for x in (e_r, g_r, G_r, Te_r, Hi_r, Gi_r, ga_r, ee_r, be_r))
                self.act(e, e[:, 0:n], ab, ab[:, 0:n], AF.Exp, bias=rowp[:, 0:1], extra=[rowp])
                self.act(g, g[:, 0:n], e, e[:, 0:n], AF.Ln, bias=one16[:, 0:1], extra=[one16])
                self.ts('dve', g, g[:, 0:n], g, g[:, 0:n], rowp[:, 3:4], extra=[rowp])
                self.act(be, be[:, 0:n], ab, ab[:, 0:n], AF.Sigmoid)
                s.op('dve', lambda e_: e_.tensor_tensor_scan(out=G[:, 0:n], data0=cm16[:, 0:n], data1=g[:, 0:n], initial=0.0,
                                                              op0=ALU.mult, op1=ALU.add), [cm16, g], [G])
                c3 = lambda t: t[:, 0:n].rearrange("p (c t) -> p c t", t=64)
                tot_b = c3(G)[:, :, 63:64].to_broadcast([16, nch, 64])
                self.tt('dve', Te, c3(Te), G, tot_b, G, c3(G), ALU.subtract)
                self.tt('dve', Hi, Hi[:, 0:n], Te, Te[:, 0:n], g, g[:, 0:n], ALU.add)
                self.tt('dve', Hi, Hi[:, 0:n], Hi, Hi[:, 0:n], G, G[:, 0:n], ALU.subtract)
                self.stt(Gi, Gi[:, 0:n], Hi, Hi[:, 0:n], rowp[:, 2:3], G, G[:, 0:n], ALU.mult, ALU.add, extra=[rowp])
                self.tt('dve', Te, c3(Te), G, tot_b, Gi, c3(Gi), ALU.subtract)
                self.act(ga, ga[:, 0:n], Gi, Gi[:, 0:n], AF.Exp)
                self.act(ee, ee[:, 0:n], Te, Te[:, 0:n], AF.Exp)
                gc = gcr.next()
                self.act(gc, gc[:, 0:nch], G, c3(G)[:, :, 63], AF.Exp)
                self.st(self.gd_gc, self.gd_gc[s_, :, g0 // 64:g0 // 64 + nch], gc, gc[:, 0:nch])
                rb = rb_r.next()
                self.cp('dve', rb, rb[:, 0, 0:n], ga, ga[:, 0:n])
                self.cp('dve', rb, rb[:, 1, 0:n], be, be[:, 0:n])
                grep = grep_r.next()
                for rr_ in range(8):
                    p = pc.next()
                    self.mm(p, p[0:64, 0:n], self32, self32[:, rr_, :], Gi, Gi[:, 0:n])
                    self.cp('act' if rr_ % 2 else 'dve', grep, grep[:, rr_, 0:n], p, p[0:64, 0:n])
                gcol = gcol_r.next()
                p = pc.next()
                for ch in range(nch):
                    self.mm(p, p[0:64, ch * 16:(ch + 1) * 16], Gi, Gi[:, ch * 64:(ch + 1) * 64], id16, id16[:, :])
                self.cp('act', gcol, gcol[:, 0:nch, :], p, p[0:64, 0:nch * 16].rearrange("p (c r) -> p c r", r=16))
                tcol = tcol_r.next()
                p = pc.next()
                for k_, src in enumerate((ee, be)):
                    for tb in range(nb):
                        self.mm(p, p[:, (k_ * 4 + tb) * 16:(k_ * 4 + tb + 1) * 16], src, src[:, tb * 128:(tb + 1) * 128], id16, id16[:, :])
                for k_ in range(2):
                    self.cp('act', tcol, tcol[:, k_, 0:nb, :], p, p[:, k_ * 64:k_ * 64 + nb * 16].rearrange("p (b r) -> p b r", r=16))
                for d in range(2):
                    for ch in range(nch):
                        dd = dd_r.next()
                        gsl = grep[:, d * 4:(d + 1) * 4, ch * 64:(ch + 1) * 64]
                        gcb = gcol[:, ch, d * 4:(d + 1) * 4].unsqueeze(2).to_broadcast([64, 4, 64])
                        self.tt('dve', dd, dd[:, :, 0, :], grep, gsl, gcol, gcb, ALU.subtract)
                        self.tt('dve', dd, dd[:, :, 1, :], dd, dd[:, :, 0, :], nmask,
                                nmask[:, d, 1, :].unsqueeze(1).to_broadcast([64, 4, 64]), ALU.add)
                        self.tt('dve', dd, dd[:, :, 0, :], dd, dd[:, :, 0, :], nmask,
                                nmask[:, d, 0, :].unsqueeze(1).to_broadcast([64, 4, 64]), ALU.add)
                        dm = dm_r.next()
                        self.act(dm, dm[:], dd, dd[:], AF.Exp)
                        self.st(self.gd_dm, self.gd_dm[s_, d, g0 // 64 + ch, :, :], dm, dm[:].rearrange("p h a t -> p (h a t)"))
                tm = tms.next()
                for h in range(4):
                    fs = fst.next()
                    outs3 = []
                    for grp in range(3):
                        cc = grp * 4 + h
                        p = pc.next()
                        for j in range(5):
                            self.mm(p, p[:, 0:n], dg, dg[:, cc, j, :], ub, ub[:, cc, j:j + n], start=(j == 0), stop=(j == 4))
                        o = qkv_r.next()
                        self.act(o, o[:, 0:n], p, p[:, 0:n], AF.Silu)
                        outs3.append(o)
                    q_, k_, v_ = outs3
                    kn, qn, vb = kn_r.next(), qn_r.next(), vb_r.next()
                    for src, dst, scl in ((q_, qn, 128 ** -0.5), (k_, kn, 1.0)):
                        sq = sq_r.next()
                        self.act(sq, sq[:, 0:n], src, src[:, 0:n], AF.Square)
                        p = pc.next()
                        self.mm(p, p[:, 0:n], self.ones128, self.ones128[:], sq, sq[:, 0:n])
                        t1 = t_r.next()
                        self.act(t1, t1[:, 0:n], p, p[:, 0:n], AF.Ln, bias=self.epsc[:, 0:1], extra=[self.epsc])
                        t2 = t_r.next()
                        self.act(t2, t2[:, 0:n], t1, t1[:, 0:n], AF.Exp, scale=-0.5)
                        self.stt(dst, dst[:, 0:n], src, src[:, 0:n], scl, t2, t2[:, 0:n], ALU.mult, ALU.mult)
                    self.cp('dve', vb, vb[:, 0:n], v_, v_[:, 0:n])
                    self.cp('act', fs, fs[:, 0, 0:n], kn, kn[:, 0:n])
                    self.cp('act', fs, fs[:, 1, 0:n], qn, qn[:, 0:n])
                    for d in range(2):
                        rep = rep_r.next()
                        for k2, (slot, row) in enumerate(((1, 8 + 4 * d + h), (0, 4 * d + h))):
                            p = pc.next()
                            self.mm(p, p[:, 0:n], selb, selb[:, row, :], rb, rb[:, slot, 0:n])
                            self.cp('act', rep, rep[:, k2, 0:n], p, p[:, 0:n])
                        b0 = 2 + 3 * d
                        self.stt(fs, fs[:, b0, 0:n], kn, kn[:, 0:n], -1.0, rep, rep[:, 0, 0:n], ALU.mult, ALU.mult)
                        self.tt('dve', fs, fs[:, b0 + 1, 0:n], fs, fs[:, b0, 0:n], rep, rep[:, 1, 0:n], ALU.mult)
                        self.tt('dve', fs, fs[:, b0 + 2, 0:n], qn, qn[:, 0:n], rep, rep[:, 1, 0:n], ALU.mult)
                    self.st(self.gd_fm, self.gd_fm[s_, :, h * 128:(h + 1) * 128, g0:g0 + n].rearrange("k p t -> p k t"), fs, fs[:, :, 0:n])
                    for src, base, slot in ((kn, 0, 0), (vb, 1, 1)):
                        pt = ptp.next()
                        for tb in range(nb):
                            s.op('pe', lambda e_: e_.transpose(pt[:, tb, :], src[:, tb * 128:(tb + 1) * 128], self.ident[:]),
                                 [src, self.ident], [pt])
                        for d in range(2):
                            for tb in range(nb):
                                row = (4 * d + h) if slot == 0 else (8 + 4 * d + h)
                                self.act(tm, tm[:, 2 * d + base, tb, h * 128:(h + 1) * 128], pt, pt[:, tb, :], AF.Copy,
                                         scale=tcol[:, slot, tb, row:row + 1], extra=[tcol])
                for ti in range(4):
                    self.st(self.gd_tm, self.gd_tm[s_, ti, g0:g0 + n, :].rearrange("(tb p) f -> p tb f", p=128), tm, tm[:, ti, 0:nb, :], key=ti)
        ph.close()

    def phase_gdn_scan(self, l):
        cfg = self.cfg
        S = cfg.S
        NC = cfg.TT // 64

        def load_fm(fm, s_, d, ci):
            for j, k in enumerate((0, 2 + 3 * d, 1, 3 + 3 * d, 4 + 3 * d)):
                self.ld(fm, fm[:, j, :, :], self.gd_fm, self.gd_fm[s_, k, :, ci * 64:(ci + 1) * 64].rearrange("(h k) t -> k h t", k=128), key=j)

        def load_tm(tm, s_, d, ci):
            for j in range(2):
                self.ld(tm, tm[:, j, :], self.gd_tm, self.gd_tm[s_, 2 * d + j, ci * 64:(ci + 1) * 64, :], key=j)

        def load_dm(dm, s_, d, ci):
            self.ld(dm, dm[:].rearrange("p h t -> p (h t)"), self.gd_dm, self.gd_dm[s_, d, ci, :, :])

        def load_gc(gc, s_, d):
            self.ld(gc, gc[:], self.gd_gc, self.gd_gc[s_:s_ + 1, d * 4:(d + 1) * 4, :].to_broadcast([128, 4, NC]))

        def store_y(y, s_, d, ci):
            self.st(self.gd_y, self.gd_y[s_, d, :, ci * 64:(ci + 1) * 64].rearrange("(h v) t -> v h t", v=128), y, y[:])
        spec = dict(H=4, Kd=128, Vd=128, kinds=None, nfm=5, ki=dict(K=0, B=0, A=1, R=2, As=3, Rs=4), ntm=2,
                    ti=dict(Kend=0, Bend=0, V=1), chains=[(s_, d) for s_ in range(S) for d in range(2)],
                    load_fm=load_fm, load_tm=load_tm, load_gc=load_gc, load_dm=load_dm, store_y=store_y, dmat=True)
        self.dplr_scan(spec)

    def phase_gdn_post(self, l):
        cfg = self.cfg
        S = cfg.S
        ph = Phase(self.s)
        gn = ph.sb('gn', [128, 1], F32)
        self.ld(gn, gn[:], self.inp['gdn_normT'], self.inp['gdn_normT'][l, :, :])
        yr = ph.ring('yy', [128, 2, 4, 512], F32, 2)
        zr = ph.ring('z', [128, 4, 512], BF16, 2)
        osr = ph.ring('os', [128, 4, 512], BF16, 2)
        o_r = ph.ring('o', [128, 512], F32, 2)
        sq_r = ph.ring('sq', [128, 512], BF16, 2)
        t_r = ph.ring('t', [128, 512], F32, 4)
        pm = ph.ring('pm', [128, 512], F32, 2, psum=True)
        for s_ in range(S):
            for (seg, t0, n, g0) in cfg.tiles(512):
                if seg == 0 and l == cfg.L - 1:
                    continue
                yy = yr.next()
                for d in range(2):
                    self.ld(yy, yy[:, d, :, 0:n], self.gd_y, self.gd_y[s_, d, :, g0:g0 + n].rearrange("(c p) t -> p c t", p=128), key=d)
                z = zr.next()
                self.ld(z, z[:, :, 0:n], self.gd_z, self.gd_z[s_, :, g0:g0 + n].rearrange("(c p) t -> p c t", p=128))
                self.fl()
                os_ = osr.next()
                for c in range(4):
                    o = o_r.next()
                    self.tt('dve', o, o[:, 0:n], yy, yy[:, 0, c, 0:n], yy, yy[:, 1, c, 0:n], ALU.add)
                    sq = sq_r.next()
                    self.act(sq, sq[:, 0:n], o, o[:, 0:n], AF.Square)
                    p = pm.next()
                    self.mm(p, p[:, 0:n], self.ones128, self.ones128[:], sq, sq[:, 0:n])
                    t1 = t_r.next()
                    self.act(t1, t1[:, 0:n], p, p[:, 0:n], AF.Ln, bias=self.epsc[:, 0:1], scale=1.0 / 128, extra=[self.epsc])
                    t2 = t_r.next()
                    self.act(t2, t2[:, 0:n], t1, t1[:, 0:n], AF.Exp, scale=-0.5)
                    self.stt(o, o[:, 0:n], o, o[:, 0:n], gn[:, 0:1], t2, t2[:, 0:n], ALU.mult, ALU.mult, extra=[gn])
                    self.tt('dve', os_, os_[:, c, 0:n], o, o[:, 0:n], z, z[:, c, 0:n], ALU.mult)
                self.st(self.ybT, self.ybT[s_, :, g0:g0 + n].rearrange("(c p) t -> p c t", p=128), os_, os_[:, :, 0:n])
        ph.close()

    def phase_merge(self, l):
        cfg = self.cfg
        S = cfg.S
        s = self.s
        ph = Phase(s)
        NT = 256
        self.wstage = ph.ring('wstg', [128, 8, 512], F32, 2)
        r = lambda ap: ap.rearrange("(kc p) c -> p kc c", p=128)
        wg = ph.sb('wg', [128, 8, 3072], BF16)
        wgs = self.inp['w_gate']
        self.load_w(ph, wg, lambda a, b: wg[:, :, a:b], wgs, lambda a, b: r(wgs[l, :, a:b]), 3072, 8)
        wu = []
        for j, nm in enumerate(('w_up_a', 'w_up_b', 'w_up_c')):
            w = ph.sb(nm, [128, 4, 1024], BF16)
            src = self.inp[nm]
            self.load_w(ph, w, lambda a, b, w=w: w[:, :, a:b], src, lambda a, b, src=src: r(src[l, :, a:b]), 1024, 4)
            wu.append(w)
        wo = ph.sb('wo', [128, 8, 1024], BF16)
        wos = self.inp['w_out']
        self.load_w(ph, wo, lambda a, b: wo[:, :, a:b], wos, lambda a, b: r(wos[l, :, a:b]), 1024, 8)
        bg = ph.sb('bg', [128, 24], F32)
        self.ld(bg, bg[:], self.inp['b_gateT'], self.inp['b_gateT'][:, l, :])
        hr = ph.ring('hb', [128, 8, NT], BF16, 2)
        yr = ph.ring('y', [128, 3, 4, NT], BF16, 2)
        xr = ph.ring('x', [128, 8, NT], F32, 2)
        mbr = ph.ring('mb', [128, 8, NT], BF16, 1)
        gtr = ph.ring('gt', [128, NT], F32, 3)
        tmr = ph.ring('tm', [128, NT], F32, 4)
        pg = ph.ring('pg', [128, NT], F32, 3, psum=True)
        pu = ph.ring('pu', [128, NT], F32, 3, psum=True)
        pw = ph.ring('pw', [128, NT], F32, 2, psum=True)
        xs = self.xsrc(l)
        ysrc = (self.yaT, self.ybT, self.ycT)
        for s_ in range(S):
            for (seg, t0, n, g0) in cfg.tiles(NT):
                if seg == 0 and l == cfg.L - 1:
                    continue
                which = S if seg == 0 else s_
                hb = hr.next()
                c0 = cfg.pcol(seg, t0, 2)
                self.ld(hb, hb[:, :, 0:n], self.hT, self.hT[s_, :, c0:c0 + n].rearrange("(kc p) t -> p kc t", p=128))
                y = yr.next()
                for j in range(3):
                    self.ld(y, y[:, j, :, 0:n], ysrc[j], ysrc[j][s_, :, g0:g0 + n].rearrange("(kc p) t -> p kc t", p=128), key=j)
                xt = xr.next()
                self.ld(xt, xt[:, :, 0:n], xs, xs[s_, :, g0:g0 + n].rearrange("(kc p) t -> p kc t", p=128))
                self.fl()
                mb = mbr.next()
                for m in range(8):
                    macc = tmr.next()
                    for j in range(3):
                        p = pg.next()
                        for kc in range(8):
                            self.mm(p, p[:, 0:n], wg, wg[:, kc, j * 1024 + m * 128:j * 1024 + (m + 1) * 128], hb, hb[:, kc, 0:n],
                                    start=(kc == 0), stop=(kc == 7))
                        gt = gtr.next()
                        self.act(gt, gt[:, 0:n], p, p[:, 0:n], AF.Sigmoid, bias=bg[:, j * 8 + m:j * 8 + m + 1], extra=[bg])
                        p2 = pu.next()
                        for kc in range(4):
                            self.mm(p2, p2[:, 0:n], wu[j], wu[j][:, kc, m * 128:(m + 1) * 128], y, y[:, j, kc, 0:n],
                                    start=(kc == 0), stop=(kc == 3))
                        if j == 0:
                            self.tt('dve', macc, macc[:, 0:n], gt, gt[:, 0:n], p2, p2[:, 0:n], ALU.mult)
                        else:
                            t = tmr.next()
                            self.tt('dve', t, t[:, 0:n], gt, gt[:, 0:n], p2, p2[:, 0:n], ALU.mult)
                            if j == 1:
                                self.tt('dve', macc, macc[:, 0:n], macc, macc[:, 0:n], t, t[:, 0:n], ALU.add)
                            else:
                                self.tt('dve', mb, mb[:, m, 0:n], macc, macc[:, 0:n], t, t[:, 0:n], ALU.add)
                for m in range(8):
                    p = pw.next()
                    for kc in range(8):
                        self.mm(p, p[:, 0:n], wo, wo[:, kc, m * 128:(m + 1) * 128], mb, mb[:, kc, 0:n], start=(kc == 0), stop=(kc == 7))
                    t = tmr.next()
                    self.act(t, t[:, 0:n], p, p[:, 0:n], AF.Copy, scale=self.MOD[:, l, 16 + m, which:which + 1], extra=[self.MOD])
                    self.tt('dve', xt, xt[:, m, 0:n], xt, xt[:, m, 0:n], t, t[:, 0:n], ALU.add)
                self.st(self.xmid, self.xmid[s_, :, g0:g0 + n].rearrange("(kc p) t -> p kc t", p=128), xt, xt[:, :, 0:n])
        ph.close()

    def phase_ffn(self, l):
        cfg = self.cfg
        S = cfg.S
        s = self.s
        ph = Phase(s)
        NT = 256
        NF = 22
        self.wstage = ph.ring('wstg', [128, 8, 256], F32, 2)
        r = lambda ap: ap.rearrange("(kc p) c -> p kc c", p=128)
        w1 = ph.sb('w1', [128, 8, 2816], BF16)
        w3 = ph.sb('w3', [128, 8, 2816], BF16)
        w2 = ph.sb('w2', [128, NF, 1024], BF16)
        for w, nm in ((w1, 'ffn_w1'), (w3, 'ffn_w3')):
            src = self.inp[nm]
            self.load_w(ph, w, lambda a, b, w=w: w[:, :, a:b], src, lambda a, b, src=src: r(src[l, :, a:b]), 2816, 8, piece=256)
        src2 = self.inp['ffn_w2']
        for f0 in range(0, NF, 8):
            f1 = min(NF, f0 + 8)
            self.load_w(ph, w2, lambda a, b, f0=f0, f1=f1: w2[:, f0:f1, a:b], src2,
                        lambda a, b, f0=f0, f1=f1: r(src2[l, f0 * 128:f1 * 128, a:b]), 1024, f1 - f0, piece=256)
        xr = ph.ring('x', [128, 8, NT], F32, 2)
        hr = ph.ring('hb', [128, 8, NT], BF16, 1)
        sqr = ph.ring('sq', [128, 8, NT], BF16, 1)
        hid = ph.ring('hid', [128, NF, NT], BF16, 1)
        tmr = ph.ring('tm', [128, NT], F32, 4)
        psr = ph.ring('pss', [128, NT], F32, 1, psum=True)
        p1r = ph.ring('p1', [128, NT], F32, 2, psum=True)
        p3r = ph.ring('p3', [128, NT], F32, 2, psum=True)
        p2r = ph.ring('p2', [128, NT], F32, 2, psum=True)
        for s_ in range(S):
            for (seg, t0, n, g0) in cfg.tiles(NT):
                if seg == 0 and l == cfg.L - 1:
                    continue
                which = S if seg == 0 else s_
                xt = xr.next()
                self.ld(xt, xt[:, :, 0:n], self.xmid, self.xmid[s_, :, g0:g0 + n].rearrange("(kc p) t -> p kc t", p=128))
                self.fl()
                hb = hr.next()
                self.norm_mod(ph, xt, n, self.A2, lambda kc: self.A2[:, kc, which:which + 1],
                              self.MOD, self.modcol(l, 3, which), hb, sqr, psr, tmr)
                hd = hid.next()
                for f in range(NF):
                    p1 = p1r.next()
                    for kc in range(8):
                        self.mm(p1, p1[:, 0:n], w1, w1[:, kc, f * 128:(f + 1) * 128], hb, hb[:, kc, 0:n], start=(kc == 0), stop=(kc == 7))
                    p3 = p3r.next()
                    for kc in range(8):
                        self.mm(p3, p3[:, 0:n], w3, w3[:, kc, f * 128:(f + 1) * 128], hb, hb[:, kc, 0:n], start=(kc == 0), stop=(kc == 7))
                    a = tmr.next()
                    self.act(a, a[:, 0:n], p1, p1[:, 0:n], AF.Silu)
                    self.tt('dve', hd, hd[:, f, 0:n], a, a[:, 0:n], p3, p3[:, 0:n], ALU.mult)
                for m in range(8):
                    p = p2r.next()
                    for f in range(NF):
                        self.mm(p, p[:, 0:n], w2, w2[:, f, m * 128:(m + 1) * 128], hd, hd[:, f, 0:n], start=(f == 0), stop=(f == NF - 1))
                    t = tmr.next()
                    self.act(t, t[:, 0:n], p, p[:, 0:n], AF.Copy, scale=self.MOD[:, l, 40 + m, which:which + 1], extra=[self.MOD])
                    self.tt('dve', xt, xt[:, m, 0:n], xt, xt[:, m, 0:n], t, t[:, 0:n], ALU.add)
                self.st(self.xcur, self.xcur[s_, :, g0:g0 + n].rearrange("(kc p) t -> p kc t", p=128), xt, xt[:, :, 0:n])
        ph.close()

    def phase_final(self):
        cfg = self.cfg
        S, TC = cfg.S, cfg.TC
        ph = Phase(self.s)
        gf = ph.sb('gf', [128, 8], F32)
        self.ld(gf, gf[:], self.inp['final_normT'], self.inp['final_normT'][:, :])
        xr = ph.ring('x', [128, 8, 512], F32, 2)
        hr = ph.ring('ho', [128, 8, 512], F32, 2)
        sqr = ph.ring('sq', [128, 8, 512], BF16, 1)
        tmr = ph.ring('tm', [128, 512], F32, 4)
        psr = ph.ring('pss', [128, 512], F32, 2, psum=True)
        for s_ in range(S):
            for (seg, t0, n, g0) in cfg.tiles(512):
                if seg == 0:
                    continue
                xt = xr.next()
                self.ld(xt, xt[:, :, 0:n], self.xcur, self.xcur[s_, :, g0:g0 + n].rearrange("(kc p) t -> p kc t", p=128))
                self.fl()
                ho = hr.next()
                self.norm_mod(ph, xt, n, gf, lambda kc: gf[:, kc:kc + 1], None, None, ho, sqr, psr, tmr)
                self.st(self.outT, self.outT[s_, :, t0:t0 + n].rearrange("(kc p) t -> p kc t", p=128), ho, ho[:, :, 0:n])
        ph.close()


def pmajor(v, nk):
    v = np.asarray(v)
    lead = v.shape[:-1]
    a = v.reshape(*lead, nk, 128)
    return np.ascontiguousarray(np.moveaxis(a, -1, 0))


def rope_tables(TC, TL, grid_w=64, theta=10000.0, dh=64):
    half = dh // 2
    t = np.arange(TL)
    row = (t // grid_w).astype(np.float32)
    col = (t % grid_w).astype(np.float32)
    inv = (theta ** (-np.arange(0, half, 2, dtype=np.float32) / half)).astype(np.float32)
    ang = np.concatenate([row[:, None] * inv, col[:, None] * inv], axis=-1).astype(np.float32)
    cos, sin = np.cos(ang), np.sin(ang)
    tab = np.zeros((128, 2, TC + TL), np.float32)
    tab[:, 0, :TC] = 1.0
    for p in range(128):
        d = p % 64
        i = d // 2
        tab[p, 0, TC:] = cos[:, i]
        tab[p, 1, TC:] = (-sin[:, i]) if d % 2 == 0 else sin[:, i]
    return tab


def host_prep(inputs, cfg, core, b0):
    S, TC, TL, L = cfg.S, cfg.TC, cfg.TL, cfg.L
    f32 = np.float32
    m = {}
    x = np.asarray(inputs['x'])[b0:b0 + S]
    ctx = np.asarray(inputs['ctx'])[b0:b0 + S]
    m['xin'] = np.ascontiguousarray(np.concatenate([ctx, x], axis=1).transpose(0, 2, 1)).astype(f32)
    cc = np.concatenate([np.asarray(inputs['c'])[b0:b0 + S], np.asarray(inputs['c_ctx'])[None, :]], axis=0)
    m['cT'] = np.ascontiguousarray(cc.T.reshape(8, 128, S + 1).transpose(1, 0, 2)).astype(f32)
    m['ada_bT'] = np.ascontiguousarray(np.asarray(inputs['ada_b'])[:L].reshape(L, 48, 128).transpose(2, 0, 1)).astype(f32)
    m['norm1T'] = np.ascontiguousarray(np.asarray(inputs['norm1'])[:L].reshape(L, 8, 128).transpose(2, 0, 1)).astype(f32)
    m['norm2T'] = np.ascontiguousarray(np.asarray(inputs['norm2'])[:L].reshape(L, 8, 128).transpose(2, 0, 1)).astype(f32)
    m['final_normT'] = np.ascontiguousarray(np.asarray(inputs['final_norm']).reshape(8, 128).T).astype(f32)
    m['b_gateT'] = np.ascontiguousarray(np.asarray(inputs['b_gate'])[:L].reshape(L, 24, 128).transpose(2, 0, 1)).astype(f32)
    for nm in WEIGHT_NAMES:
        m[nm] = np.ascontiguousarray(np.asarray(inputs[nm])[:L]).astype(f32)
    QO = 3 * 512 + 4 * 512
    perm = np.arange(640) ^ 1
    m['w_in_perm'] = np.ascontiguousarray(m['w_in'][:, :, QO:QO + 640][:, :, perm])
    gq = np.asarray(inputs['attn_q_norm'])[:L]
    gk = np.asarray(inputs['attn_k_norm'])[:L]
    p64 = np.arange(64) ^ 1
    g = np.stack([np.tile(gq, (1, 2)), np.tile(gq[:, p64], (1, 2)), np.tile(gk, (1, 2)), np.tile(gk[:, p64], (1, 2))], axis=-1)
    m['attn_gT'] = np.ascontiguousarray(g.transpose(1, 0, 2)).astype(f32)
    m['attn_gB'] = np.ascontiguousarray(np.stack([gq, gk], axis=1)).astype(f32)
    P9 = np.stack([np.asarray(inputs['rwkv_w0'])[:L, 0], np.asarray(inputs['rwkv_w0'])[:L, 1],
                   np.asarray(inputs['rwkv_a0'])[:L, 0], np.asarray(inputs['rwkv_a0'])[:L, 1],
                   np.asarray(inputs['rwkv_k_k'])[:L], np.asarray(inputs['rwkv_k_a'])[:L],
                   np.asarray(inputs['rwkv_r_k'])[:L].reshape(L, 512), np.asarray(inputs['rwkv_lnx_w'])[:L],
                   np.asarray(inputs['rwkv_lnx_b'])[:L]], axis=1)
    m['rwkv_pT'] = np.ascontiguousarray(P9.reshape(L, 9, 4, 128).transpose(3, 0, 1, 2)).astype(f32)
    m['rwkv_mu_xT'] = np.ascontiguousarray(np.asarray(inputs['rwkv_mu_x'])[:L].reshape(L, 3, 8, 128).transpose(3, 0, 1, 2)).astype(f32)
    m['rwkv_mu_rkv'] = np.ascontiguousarray(np.asarray(inputs['rwkv_mu_rkv'])[:L].reshape(L, 1536)).astype(f32)
    cm = np.ones((128, 512), f32)
    cm[:, ::64] = 0.0
    m['c_chunkmask'] = cm
    jj, tt_ = np.meshgrid(np.arange(64), np.arange(64), indexing='ij')
    sm = np.zeros((64, 2, 128), f32)
    sm[:, 0, 0:64] = (jj < tt_)
    sm[:, 0, 64:128] = (jj <= tt_)
    sm[:, 1, 0:64] = (jj > tt_)
    sm[:, 1, 64:128] = (jj >= tt_)
    m['c_scanmask'] = sm
    m['c_ident64'] = np.eye(64, dtype=f32)
    bmk = np.zeros((64, 2, 64), f32)
    bmk[:, 0, :] = ((jj // 32) == (tt_ // 32))
    bmk[:, 1, :] = 1.0 - bmk[:, 0, :]
    m['c_blkmask'] = bmk
    m['gdn_convT'] = np.ascontiguousarray(np.asarray(inputs['gdn_conv'])[:L].reshape(L, 12, 128, 5).transpose(2, 0, 1, 3)).astype(f32)
    rp = np.zeros((16, L, 3), f32)
    rp[0:8, :, 0] = np.asarray(inputs['gdn_dt_bias'])[:L].reshape(L, 8).T
    rp[0:8, :, 1] = np.asarray(inputs['gdn_A_log'])[:L].reshape(L, 8).T
    rp[4:8, :, 2] = 1.0
    m['gdn_rowp'] = rp
    m['gdn_normT'] = np.ascontiguousarray(np.asarray(inputs['gdn_norm'])[:L][:, :, None]).astype(f32)
    sb_ = np.zeros((16, 16, 128), f32)
    for r_ in range(16):
        sb_[r_, r_, :] = 1.0
    m['c_selb'] = sb_.astype(ml_dtypes.bfloat16)
    m['c_self'] = np.ascontiguousarray(sb_[:, :, :64])
    nm_ = np.zeros((64, 2, 2, 64), f32)
    NEG = -30000.0
    nm_[:, 0, 0] = np.where(jj < tt_, 0.0, NEG)
    nm_[:, 0, 1] = np.where(jj <= tt_, 0.0, NEG)
    nm_[:, 1, 0] = np.where(jj > tt_, 0.0, NEG)
    nm_[:, 1, 1] = np.where(jj >= tt_, 0.0, NEG)
    m['c_negmask'] = nm_
    m['c_ident'] = np.eye(128, dtype=f32).astype(ml_dtypes.bfloat16)
    ob = np.zeros((128, 128), f32)
    ob[:64, :64] = 1
    ob[64:, 64:] = 1
    m['c_onesblk'] = ob.astype(ml_dtypes.bfloat16)
    m['c_rope'] = rope_tables(TC, TL)
    return m


def input_shapes(m):
    out = {}
    for k, v in m.items():
        out[k] = (v.shape, BF16 if v.dtype == ml_dtypes.bfloat16 else F32)
    return out


_CACHE = {}


def run(inputs, cfg, n_cores):
    maps = [host_prep(inputs, cfg, c, c * cfg.S) for c in range(n_cores)]
    key = (cfg.S, cfg.TC, cfg.TL, cfg.L, tuple(sorted(cfg.debug)))
    kern = Kern(cfg, input_shapes(maps[0]))
    nc = kern.build()
    res = run_bass_kernel_spmd(nc, maps, core_ids=list(range(n_cores)))
    return kern, res


def kernel(**inputs):
    cfg = Cfg(S=2, TC=256, TL=4096, L=4)
    kern, res = run(inputs, cfg, 8)
    outs = [np.asarray(r['outT']).transpose(0, 2, 1) for r in res.results]
    return np.ascontiguousarray(np.concatenate(outs, axis=0)).astype(np.float32)
```
